# Optimizing a Trainium2 kernel written in Bass

```python
import jax, jax.numpy as jnp
from jax import lax
import numpy as np

D_MODEL = 1024
BATCH = 2
SEQ = 8192
DEPTH = 1
DEC_BATCH = 128
DEC_SEQ = 4
PAST_LEN = 16384
PAGE_SIZE = 128

NH_M = 4
DHK_M = 128
DHV_M = 256
DQK_M = NH_M * DHK_M
DV_M = NH_M * DHV_M
CHUNK_M = 128
NH_A = 16
NKV_A = 4
HD_A = 64
WINDOW = 128
DQ_A = NH_A * HD_A
DKV_A = NKV_A * HD_A
D_FF = 2816
CONV_W = 3
N_MOD = 6
ALPHA = (2 * DEPTH) ** 0.25
BETA = (8 * DEPTH) ** -0.25
LN_EPS = 1e-5
IN_SPLITS = (DQK_M, DQK_M, DV_M, NH_M, NH_M, DV_M, DQ_A, DKV_A, DKV_A, D_MODEL, D_MODEL)
D_IN = DQK_M * 2 + DV_M * 2 + NH_M * 2 + DQ_A + DKV_A * 2 + D_MODEL * 2

kernel_name = 'hybrid_mlstm_swa_convffn_step'


def _ln(x, g=None, b=None):
    xf = x.astype(jnp.float32)
    mu = jnp.mean(xf, axis=-1, keepdims=True)
    var = jnp.mean(jnp.square(xf - mu), axis=-1, keepdims=True)
    y = (xf - mu) * lax.rsqrt(var + LN_EPS)
    if g is not None:
        y = y * g.astype(jnp.float32) + b.astype(jnp.float32)
    return y.astype(x.dtype)


def _split(z, sizes):
    out, off = [], 0
    for s in sizes:
        out.append(z[..., off:off + s])
        off += s
    return out


def _alibi_slopes():
    return jnp.exp2(-8.0 * jnp.arange(1, NH_A + 1, dtype=jnp.float32) / NH_A)


def _to_chunks(a, nc, t):
    b = a.shape[0]
    a = a.reshape((b, nc, t) + a.shape[2:])
    return jnp.transpose(a, (1, 0, 3, 2) + tuple(range(4, a.ndim)))


def _mlstm(q, k, v, ig, lf, C0, n0, m0):
    B, L = q.shape[:2]
    T = CHUNK_M if L % CHUNK_M == 0 else L
    NC = L // T
    f32 = jnp.float32
    qc = _to_chunks(q.astype(f32) * DHK_M ** -0.5, NC, T)
    kc = _to_chunks(k.astype(f32), NC, T)
    vc = _to_chunks(v.astype(f32), NC, T)
    ic = _to_chunks(ig, NC, T)
    fc = _to_chunks(lf, NC, T)
    causal = jnp.tril(jnp.ones((T, T), dtype=bool))

    def step(carry, inp):
        C, n, m = carry
        qx, kx, vx, ix, fx = inp
        b = jnp.cumsum(fx, axis=-1)
        dmat = b[..., :, None] - b[..., None, :] + ix[..., None, :]
        dmat = jnp.where(causal, dmat, -jnp.inf)
        inter = b + m[..., None]
        mt = jnp.maximum(inter, jnp.max(dmat, axis=-1))
        smat = jnp.einsum('bhtd,bhsd->bhts', qx, kx) * jnp.exp(dmat - mt[..., None])
        a_in = jnp.exp(inter - mt)
        num = jnp.einsum('bhts,bhsv->bhtv', smat, vx) + a_in[..., None] * jnp.einsum('bhvd,bhtd->bhtv', C, qx)
        den = jnp.sum(smat, axis=-1) + a_in * jnp.einsum('bhd,bhtd->bht', n, qx)
        h = num / jnp.maximum(jnp.abs(den), jnp.exp(-mt))[..., None]
        bT = b[..., -1]
        wk = bT[..., None] - b + ix
        m_new = jnp.maximum(bT + m, jnp.max(wk, axis=-1))
        decay = jnp.exp(bT + m - m_new)
        ws = jnp.exp(wk - m_new[..., None])
        C_new = decay[..., None, None] * C + jnp.einsum('bhs,bhsv,bhsd->bhvd', ws, vx, kx)
        n_new = decay[..., None] * n + jnp.einsum('bhs,bhsd->bhd', ws, kx)
        return (C_new, n_new, m_new), h

    (C, n, m), h = lax.scan(step, (C0.astype(f32), n0.astype(f32), m0.astype(f32)), (qc, kc, vc, ic, fc))
    h = jnp.transpose(h, (1, 0, 3, 2, 4)).reshape(B, L, NH_M, DHV_M)
    return h, C, n, m


def _swa(q, k, v, kbuf, vbuf, pos0, sinks):
    B, L = q.shape[:2]
    W = kbuf.shape[1]
    Qb = WINDOW if L % WINDOW == 0 else L
    NB = L // Qb
    G = NH_A // NKV_A
    f32 = jnp.float32
    kx = jnp.concatenate([kbuf.astype(k.dtype), k], axis=1)
    vx = jnp.concatenate([vbuf.astype(v.dtype), v], axis=1)
    idx = jnp.arange(NB)[:, None] * Qb + jnp.arange(W + Qb)[None, :]
    kb = kx[:, idx].astype(f32)
    vb = vx[:, idx].astype(f32)
    qb = q.reshape(B, NB, Qb, NKV_A, G, HD_A).astype(f32) * HD_A ** -0.5
    qpos = pos0 + jnp.arange(L).reshape(NB, Qb)
    kpos = pos0 - W + idx
    delta = qpos[:, :, None] - kpos[:, None, :]
    valid = (delta >= 0) & (delta < WINDOW) & (kpos[:, None, :] >= 0)
    slopes = _alibi_slopes().reshape(NKV_A, G)[:, :, None, None]
    s = jnp.einsum('bnqkgd,bnskd->bnkgqs', qb, kb)
    s = s - slopes * delta[:, None, None].astype(f32)
    s = jnp.where(valid[:, None, None], s, -jnp.inf)
    sink = sinks.astype(f32).reshape(NKV_A, G)[:, :, None, None]
    mx = jnp.maximum(jnp.max(s, axis=-1, keepdims=True), sink)
    p = jnp.exp(s - mx)
    den = jnp.sum(p, axis=-1, keepdims=True) + jnp.exp(sink - mx)
    o = jnp.einsum('bnkgqs,bnskd->bnqkgd', p / den, vb).reshape(B, L, NH_A * HD_A)
    return o, kx[:, -W:], vx[:, -W:]


def _conv_ffn(h, cbuf, w_up, b_up, conv_w, conv_b, w_down, b_down):
    L = h.shape[1]
    u = h @ w_up + b_up
    ux = jnp.concatenate([cbuf.astype(u.dtype), u], axis=1)
    y = conv_b + ux[:, 0:L] * conv_w[0]
    for j in range(1, CONV_W):
        y = y + ux[:, j:j + L] * conv_w[j]
    a, g = y[..., :D_FF], y[..., D_FF:]
    out = (jax.nn.gelu(a) * g) @ w_down + b_down
    return out, ux[:, L:]


def _layer(x, c, pos0, C0, n0, m0, kbuf, vbuf, cbuf,
           w_ada, b_ada, w_in, b_in, mlstm_norm_w, attn_sinks,
           w_branch_m, w_branch_a, w_out, ln1_g, ln1_b,
           w_up, b_up, conv_w, conv_b, w_down, b_down, ln2_g, ln2_b):
    B, L, _ = x.shape
    f32 = jnp.float32
    mod = (jax.nn.silu(c) @ w_ada + b_ada)[:, None, :]
    sh1, sc1, g1, sh2, sc2, g2 = jnp.split(mod, N_MOD, axis=-1)
    h = _ln(x) * (1.0 + sc1) + sh1
    z = h @ w_in + b_in
    qm, km, vm, ig, fg, og, qa, ka, va, gm, ga = _split(z, IN_SPLITS)
    hm, C, n, m = _mlstm(qm.reshape(B, L, NH_M, DHK_M), km.reshape(B, L, NH_M, DHK_M),
                         vm.reshape(B, L, NH_M, DHV_M), ig.astype(f32),
                         jax.nn.log_sigmoid(fg.astype(f32)), C0, n0, m0)
    hm = _ln(hm) * mlstm_norm_w.astype(f32).reshape(NH_M, DHV_M)
    hm = (hm.reshape(B, L, DV_M) * jax.nn.sigmoid(og.astype(f32))).astype(x.dtype)
    ha, k_new, v_new = _swa(qa.reshape(B, L, NH_A, HD_A), ka.reshape(B, L, NKV_A, HD_A),
                            va.reshape(B, L, NKV_A, HD_A), kbuf, vbuf, pos0, attn_sinks)
    merged = (jax.nn.sigmoid(gm) * (hm @ w_branch_m)
              + jax.nn.sigmoid(ga) * (ha.astype(x.dtype) @ w_branch_a))
    x = _ln(ALPHA * x + g1 * (merged @ w_out), ln1_g, ln1_b)
    h2 = _ln(x) * (1.0 + sc2) + sh2
    f, cbuf_new = _conv_ffn(h2, cbuf, w_up, b_up, conv_w, conv_b, w_down, b_down)
    x = _ln(ALPHA * x + g2 * f, ln2_g, ln2_b)
    dt = x.dtype
    return x, (C.astype(dt), n.astype(dt), m.astype(dt), k_new, v_new, cbuf_new)


def setup_inputs(seed: int = 0) -> dict:
    key = jax.random.key(seed)
    ks = iter(jax.random.split(key, 40))

    def nrm(shape, scale=1.0):
        return jax.random.normal(next(ks), shape, jnp.float32) * scale

    D = D_MODEL
    F2 = 2 * D_FF
    WB = min(WINDOW, PAST_LEN)
    f_off = 2 * DQK_M + DV_M + NH_M
    b_in = nrm((DEPTH, D_IN), 0.02).at[:, f_off:f_off + NH_M].add(3.0)
    return {
        'x_prompt': nrm((BATCH, SEQ, D)),
        'x_sample': nrm((DEC_BATCH, DEC_SEQ, D)),
        'c_prompt': nrm((BATCH, D)),
        'c_sample': nrm((DEC_BATCH, D)),
        'state_mlstm_C': nrm((DEPTH, DEC_BATCH, NH_M, DHV_M, DHK_M), 0.1),
        'state_mlstm_n': jnp.abs(nrm((DEPTH, DEC_BATCH, NH_M, DHK_M))),
        'state_mlstm_m': nrm((DEPTH, DEC_BATCH, NH_M)),
        'cache_k_win': nrm((DEPTH, DEC_BATCH, WB, NKV_A, HD_A)),
        'cache_v_win': nrm((DEPTH, DEC_BATCH, WB, NKV_A, HD_A)),
        'state_ffn_conv': nrm((DEPTH, DEC_BATCH, CONV_W - 1, F2), 0.5),
        'w_ada': nrm((DEPTH, D, N_MOD * D), D ** -0.5),
        'b_ada': nrm((DEPTH, N_MOD * D), 0.02),
        'w_in': nrm((DEPTH, D, D_IN), D ** -0.5),
        'b_in': b_in,
        'mlstm_norm_w': 1.0 + nrm((DEPTH, DV_M), 0.02),
        'attn_sinks': nrm((DEPTH, NH_A)),
        'w_branch_m': nrm((DEPTH, DV_M, D), BETA * DV_M ** -0.5),
        'w_branch_a': nrm((DEPTH, DQ_A, D), BETA * DQ_A ** -0.5),
        'w_out': nrm((DEPTH, D, D), BETA * D ** -0.5),
        'ln1_g': 1.0 + nrm((DEPTH, D), 0.02),
        'ln1_b': nrm((DEPTH, D), 0.02),
        'w_up': nrm((DEPTH, D, F2), D ** -0.5),
        'b_up': nrm((DEPTH, F2), 0.02),
        'conv_w': nrm((DEPTH, CONV_W, F2), CONV_W ** -0.5),
        'conv_b': nrm((DEPTH, F2), 0.02),
        'w_down': nrm((DEPTH, D_FF, D), BETA * D_FF ** -0.5),
        'b_down': nrm((DEPTH, D), 0.02),
        'ln2_g': 1.0 + nrm((DEPTH, D), 0.02),
        'ln2_b': nrm((DEPTH, D), 0.02),
    }


def reference(x_prompt, x_sample, c_prompt, c_sample, state_mlstm_C, state_mlstm_n, state_mlstm_m,
              cache_k_win, cache_v_win, state_ffn_conv, w_ada, b_ada, w_in, b_in, mlstm_norm_w,
              attn_sinks, w_branch_m, w_branch_a, w_out, ln1_g, ln1_b, w_up, b_up, conv_w, conv_b,
              w_down, b_down, ln2_g, ln2_b):
    params = (w_ada, b_ada, w_in, b_in, mlstm_norm_w, attn_sinks, w_branch_m, w_branch_a, w_out,
              ln1_g, ln1_b, w_up, b_up, conv_w, conv_b, w_down, b_down, ln2_g, ln2_b)
    Bp = x_prompt.shape[0]
    dt = x_prompt.dtype
    yp, ys = x_prompt, x_sample
    new_p, new_s = [], []
    for l in range(DEPTH):
        wl = [w[l] for w in params]
        yp, sp = _layer(yp, c_prompt, 0,
                        jnp.zeros((Bp, NH_M, DHV_M, DHK_M), dt), jnp.zeros((Bp, NH_M, DHK_M), dt),
                        jnp.zeros((Bp, NH_M), dt), jnp.zeros((Bp, WINDOW, NKV_A, HD_A), dt),
                        jnp.zeros((Bp, WINDOW, NKV_A, HD_A), dt), jnp.zeros((Bp, CONV_W - 1, 2 * D_FF), dt),
                        *wl)
        ys, ss = _layer(ys, c_sample, PAST_LEN, state_mlstm_C[l], state_mlstm_n[l], state_mlstm_m[l],
                        cache_k_win[l], cache_v_win[l], state_ffn_conv[l], *wl)
        new_p.append(sp)
        new_s.append(ss)
    p_C, p_n, p_m, p_k, p_v, p_conv = [jnp.stack(a) for a in zip(*new_p)]
    s_C, s_n, s_m, s_k, s_v, s_conv = [jnp.stack(a) for a in zip(*new_s)]
    return (yp, ys, p_C, p_n, p_m, p_k, p_v, p_conv, s_C, s_n, s_m, s_k, s_v, s_conv)
```

```python
import contextlib
import numpy as np
import ml_dtypes
import concourse.bass as bass
import concourse.mybir as mybir
from concourse.bass_utils import run_bass_kernel_spmd

F32 = mybir.dt.float32
BF16 = mybir.dt.bfloat16
AF = mybir.ActivationFunctionType
ALU = mybir.AluOpType
AX = mybir.AxisListType

D = 1024
KC = 8
DIN = 6664
DFF = 2816
FC = 22
NEG = -30000.0
LN_EPS = 1e-5
ALPHA = 2.0 ** 0.25
NSEQ_S = 16
NTS = 64


class Buf:
    __slots__ = ("w", "r", "name", "excl")

    def __init__(self, name="", init=None):
        self.w = {}
        self.r = dict(init) if init else {}
        self.name = name
        self.excl = False


class _StopBuild(Exception):
    pass


class Sched:
    stop_at = None

    def checkpoint(self, name):
        if Sched.stop_at is not None and name == Sched.stop_at:
            self.stopped = True

    def __init__(self, nc, stack, n_dma=40):
        self.nc = nc
        self.eng = dict(pe=nc.tensor, act=nc.scalar, dve=nc.vector, pool=nc.gpsimd, sp=nc.sync)
        self.sem = {}
        self.cnt = {}
        self.waited = {k: {} for k in self.eng}
        for k in self.eng:
            self.sem[k] = stack.enter_context(nc.semaphore("e_" + k))
            self.cnt[k] = 0
        self.dsem = [stack.enter_context(nc.semaphore("d%d" % i)) for i in range(n_dma)]
        self.dcnt = [0] * n_dma
        self.dpool = {"sp": list(range(0, n_dma - 12)), "pool": list(range(n_dma - 12, n_dma))}
        self.dnext = {"sp": 0, "pool": 0}
        self.pending_pe = False
        self.ninst = 0
        self.epoch = {}
        self.stopped = False
        self.marks = []
        self.npe = 0
        self.log = {k: [] for k in self.eng}

    def check_deadlock(self):
        sem = {}
        pc = {k: 0 for k in self.log}
        progress = True
        while progress:
            progress = False
            for e, lg in self.log.items():
                while pc[e] < len(lg):
                    kind, key, val = lg[pc[e]]
                    if kind == "w":
                        if sem.get(key, 0) < val:
                            break
                    else:
                        sem[key] = sem.get(key, 0) + val
                    pc[e] += 1
                    progress = True
        stuck = {e: (pc[e], self.log[e][pc[e]]) for e in self.log if pc[e] < len(self.log[e])}
        return stuck

    def phase_end(self):
        self.marks.append(self.npe)
        self._phase_end()

    def _phase_end(self):
        ep = {k: v for k, v in self.cnt.items() if v > 0 and k != "sp"}
        for i, v in enumerate(self.dcnt):
            if v > 0:
                ep[i] = v
        self.epoch = ep

    def _wait(self, e, key, val):
        if val <= 0:
            return
        w = self.waited[e]
        if w.get(key, 0) >= val:
            return
        w[key] = val
        semh = self.sem[key] if isinstance(key, str) else self.dsem[key]
        self.eng[e].wait_ge(semh, val)
        self.log[e].append(("w", key, val))

    def _deps(self, e, reads, writes):
        need = {}
        for b in reads:
            for k, v in b.w.items():
                if need.get(k, 0) < v:
                    need[k] = v
            if b.excl:
                for k, v in b.r.items():
                    if k != e and need.get(k, 0) < v:
                        need[k] = v
        for b in writes:
            for k, v in b.w.items():
                if need.get(k, 0) < v:
                    need[k] = v
            for k, v in b.r.items():
                if need.get(k, 0) < v:
                    need[k] = v
        for k, v in need.items():
            if k == e and e == "pe":
                continue
            self._wait(e, k, v)

    def _mark(self, tok, reads, writes):
        k, v = tok
        for b in reads:
            if b.r.get(k, 0) < v:
                b.r[k] = v
        for b in writes:
            b.w = {k: v}
            b.r = {}

    def op(self, e, fn, reads=(), writes=(), inc=True):
        if self.stopped:
            return (e, 0)
        self._deps(e, reads, writes)
        ins = fn(self.eng[e])
        self.ninst += 1
        if e == "pe":
            self.npe += 1
        if inc:
            self.cnt[e] += 1
            ins.then_inc(self.sem[e], 1)
            self.log[e].append(("i", e, 1))
            tok = (e, self.cnt[e])
        else:
            tok = (e, self.cnt[e] + 1)
        self._mark(tok, reads, writes)
        return tok

    def dma(self, q, out, in_, reads=(), writes=(), acc=False, **kw):
        if self.stopped:
            return (q, 0)
        pl = self.dpool[q]
        i = pl[self.dnext[q]]
        self.dnext[q] = (self.dnext[q] + 1) % len(pl)
        self._wait(q, i, self.dcnt[i])
        if acc:
            self._deps(q, reads, ())
        else:
            self._deps(q, reads, writes)
        self.dcnt[i] += 16
        self.eng[q].dma_start(out=out, in_=in_, **kw).then_inc(self.dsem[i], 16)
        self.log[q].append(("i", i, 16))
        self.ninst += 1
        tok = (i, self.dcnt[i])
        if acc:
            self._mark(tok, reads, ())
            for b in writes:
                b.w[i] = self.dcnt[i]
        else:
            self._mark(tok, reads, writes)
        return tok

    def finish(self, bufs):
        for b in bufs:
            for k, v in list(b.w.items()) + list(b.r.items()):
                self._wait("sp", k, v)
        for i in range(len(self.dsem)):
            self._wait("sp", i, self.dcnt[i])


class T:
    def __init__(self, h, name="", init=None):
        self.h = h
        self.b = Buf(name, init)

    def __getitem__(self, idx):
        return self.h[idx]


class Cfg:
    def __init__(self, seq=8192, sample=True):
        self.SEQ = seq
        self.SEG = seq // 4
        self.NOWN = self.SEG // 128
        self.NSCAN = max(0, (3 * self.SEG - 256) // 128)
        self.NB_ALL = self.NSCAN + 2 + self.NOWN
        self.sample = sample


def build(cfg, debug=False, stop_after=None):
    nc = bass.Bass("TRN2", target_bir_lowering=False)
    st = contextlib.ExitStack()
    S = Sched(nc, st)
    NOWN, NSCAN, NB_ALL = cfg.NOWN, cfg.NSCAN, cfg.NB_ALL

    def din(name, shape, dt=F32):
        return T(nc.dram_tensor(name, list(shape), dt, kind="ExternalInput").ap(), name)

    def dout(name, shape, dt=F32):
        return T(nc.dram_tensor(name, list(shape), dt, kind="ExternalOutput").ap(), name)

    def dscr(name, shape, dt=F32):
        return T(nc.dram_tensor(name, list(shape), dt, kind="Internal").ap(), name)

    uniq = {"n": 0}

    def sb(name, shape, dt=F32, stack=None):
        uniq["n"] += 1
        return T((stack or st).enter_context(nc.sbuf_tensor("%s_%d" % (name, uniq["n"]), list(shape), dt)), name, S.epoch)

    def ps(name, shape, dt=F32):
        t_ = T(st.enter_context(nc.psum_tensor(name, list(shape), dt)), name)
        t_.b.excl = True
        return t_

    xs = din("xs", [NB_ALL * 128, D])
    tokc = din("tokc", [128, NB_ALL, 2])
    cTtok = din("cTtok", [128, KC, 65])
    cTrep = din("cTrep", [128, KC, 192])
    first_bias = din("first_bias", [128, 128])
    convvalid = din("convvalid", [128, 1])
    w_ada = din("w_ada", [D, 6 * D])
    b_ada = din("b_ada", [1, 6 * D])
    b_adaT = din("b_adaT", [128, 48])
    w_in = din("w_in", [D, DIN])
    b_in = din("b_in", [1, DIN])
    b_inT = din("b_inT", [128, 32])
    b_qk64 = din("b_qk64", [64, 20])
    norm_wT = din("norm_wT", [128, KC])
    sinks_rep = din("sinks_rep", [128, 16])
    w_bm = din("w_bm", [D, D])
    w_ba = din("w_ba", [D, D])
    w_out = din("w_out", [D, D])
    ln1_g = din("ln1_g", [1, D])
    ln1_b = din("ln1_b", [1, D])
    w_upr = din("w_upr", [FC, 128, KC * 256])
    b_upT = din("b_upT", [128, 2 * FC])
    conv_wT = din("conv_wT", [128, 2 * FC, 3])
    conv_bT = din("conv_bT", [128, 2 * FC])
    w_down = din("w_down", [DFF, D])
    b_down = din("b_down", [1, D])
    ln2_g = din("ln2_g", [1, D])
    ln2_b = din("ln2_b", [1, D])
    c_ident = din("c_ident", [128, 128])
    c_tri = din("c_tri", [128, 128])
    c_swab = din("c_swab", [128, 16, 256])

    xsamp = din("xsamp", [NTS, D])
    convst = din("convst", [128, 2 * FC, 32])
    y_out = dout("y_out", [NOWN * 128, D])
    ys_out = dout("ys_out", [NTS, D])
    sconv_out = dout("sconv_out", [128, 2 * FC, 32])
    pC_out = dout("pC_out", [128, 4, 257])
    pm_out = dout("pm_out", [4, 1])
    pkv_out = dout("pkv_out", [128, 512])
    pconv_out = dout("pconv_out", [128, 2 * FC, 2])
    x1_scr = dscr("x1_scr", [(NOWN + 1) * 128 + NTS, D])
    dbg = {}

    psf = [ps("psf%d" % i, [128, 512], F32) for i in range(6)]
    psb = [ps("psb%d" % i, [128, 1024], BF16) for i in range(2)]
    rot = {"f": 0, "b": 0, "n": 4}

    def PF():
        rot["f"] = (rot["f"] + 1) % rot["n"]
        return psf[rot["f"]]

    def PL(i):
        return psf[4 + i]

    def PB():
        rot["b"] = (rot["b"] + 1) % len(psb)
        return psb[rot["b"]]

    ident = sb("ident", [128, 128])
    identb = sb("identb", [128, 128], BF16)
    tri = sb("tri", [128, 128])
    trib = sb("trib", [128, 128], BF16)
    ones = sb("ones", [128, 128])
    onesb = sb("onesb", [128, 128], BF16)
    tokc_sb = sb("tokc_sb", [128, NB_ALL, 2])
    modF = sb("modF", [128, 32, 1])
    modR_scr = dscr("modR_scr", [128, 4, D])
    SAMP = {}
    modFs_scr = dscr("modFs_scr", [128, 32, NTS])
    gS_scr = dscr("gS_scr", [NTS, 2, D])
    gP = sb("gP", [128, 2, D])
    mst = sb("mst", [4, 1])
    CnT = sb("CnT", [128, 4, 257])
    CnTb = sb("CnTb", [128, 4, 257], BF16)
    ones_col = sb("ones_col", [128, 1], BF16)

    S.dma("sp", ident[:], c_ident[:, :], writes=[ident.b])
    S.dma("sp", tri[:], c_tri[:, :], writes=[tri.b])
    S.dma("sp", tokc_sb[:], tokc[:, :, :], writes=[tokc_sb.b])
    S.op("dve", lambda e: e.tensor_copy(identb[:], ident[:]), [ident.b], [identb.b])
    S.op("dve", lambda e: e.tensor_copy(trib[:], tri[:]), [tri.b], [trib.b])
    S.op("pool", lambda e: e.memset(ones[:], 1.0), [], [ones.b])
    S.op("pool", lambda e: e.memset(onesb[:], 1.0), [], [onesb.b])
    S.op("pool", lambda e: e.memset(ones_col[:], 1.0), [], [ones_col.b])
    S.op("pool", lambda e: e.memset(mst[:], 0.0), [], [mst.b])
    S.op("pool", lambda e: e.memset(CnT[:], 0.0), [], [CnT.b])
    S.op("pool", lambda e: e.memset(CnTb[:], 0.0), [], [CnTb.b])

    def load_w(dst, src, c0, c1, rows=D, stack=None):
        v = src.h.rearrange("(k p) c -> p k c", p=128)
        nk = rows // 128
        for k in range(nk):
            S.dma("pool", dst[:, k, 0:c1 - c0], v[:, k, c0:c1], writes=[dst.b], acc=(k > 0))

    def bias_hilo(name, src, c0, c1, stack):
        n = c1 - c0
        stg = sb(name + "_stg", [64, n], F32, stack)
        tb = sb(name + "_tb", [64, n], BF16, stack)
        out = sb(name, [64, n], BF16, stack)
        S.op("pool", lambda e: e.memset(out[:], 0.0), [], [out.b])
        S.dma("sp", stg[0:1, :], src[0:1, c0:c1], writes=[stg.b])
        S.dma("sp", stg[32:33, :], src[0:1, c0:c1], writes=[stg.b])
        S.op("dve", lambda e: e.tensor_copy(out[0:1, :], stg[0:1, :]), [stg.b], [out.b])
        S.op("dve", lambda e: e.tensor_copy(tb[32:33, :], stg[32:33, :]), [stg.b], [tb.b])
        S.op("dve", lambda e: e.tensor_tensor(stg[32:33, :], stg[32:33, :], tb[32:33, :], ALU.subtract),
             [tb.b], [stg.b])
        S.op("dve", lambda e: e.tensor_copy(out[32:33, :], stg[32:33, :]), [stg.b], [out.b])
        return out

    def mm_tok(pst, n0, n, hT, c0, nt, W, wc0, bias, bc0):
        for k in range(KC):
            S.op("pe", lambda e, k=k: e.matmul(pst[:nt, n0:n0 + n], hT[:, k, c0:c0 + nt], W[:, k, wc0:wc0 + n],
                                               start=(k == 0), stop=(k == KC - 1 and bias is None)),
                 [hT.b, W.b], [pst.b], inc=(k == KC - 1 and bias is None))
        if bias is not None:
            S.op("pe", lambda e: e.matmul(pst[:nt, n0:n0 + n], onesb[0:64, 0:nt], bias[:, bc0:bc0 + n],
                                          start=False, stop=True), [bias.b, onesb.b], [pst.b])

    def ln_stats(xt, nt, tmp):
        S.op("dve", lambda e: e.bn_stats(tmp[:nt, 0:6], xt[:nt, 0:512]), [xt.b], [tmp.b])
        S.op("dve", lambda e: e.bn_stats(tmp[:nt, 6:12], xt[:nt, 512:1024]), [xt.b], [tmp.b])
        S.op("dve", lambda e: e.bn_aggr(tmp[:nt, 12:14], tmp[:nt, 0:12]), [tmp.b], [tmp.b])
        S.op("act", lambda e: e.activation(tmp[:nt, 14:15], tmp[:nt, 13:14], AF.Ln, bias=epsc[:nt, 0:1], scale=1.0),
             [tmp.b, epsc.b], [tmp.b])
        S.op("act", lambda e: e.activation(tmp[:nt, 15:16], tmp[:nt, 14:15], AF.Exp, scale=-0.5), [tmp.b], [tmp.b])
        S.op("dve", lambda e: e.tensor_scalar(tmp[:nt, 16:17], tmp[:nt, 12:13], tmp[:nt, 15:16], -1.0, ALU.mult, ALU.mult),
             [tmp.b], [tmp.b])
        return tmp[:nt, 15:16], tmp[:nt, 16:17]

    epsc = sb("epsc", [128, 1])
    S.op("pool", lambda e: e.memset(epsc[:], LN_EPS), [], [epsc.b])

    xbl = [sb("xb%d" % i, [128, D], BF16) for i in range(2)]
    xbrot = {"i": 0}

    def ln_mod_rows(xn, nt, mrow):
        xbrot["i"] ^= 1
        xb = xbl[xbrot["i"]]
        S.op("dve", lambda e: e.tensor_tensor(xn[:nt, :], xn[:nt, :], mrow[:nt, 1, :], ALU.mult), [xn.b, mrow.b], [xn.b])
        S.op("dve", lambda e: e.tensor_tensor(xb[:nt, :], xn[:nt, :], mrow[:nt, 0, :], ALU.add), [xn.b, mrow.b], [xb.b])
        return xb

    def xb_to_hT(xb, nt, hT, c0):
        ptb = PB()
        for k in range(KC):
            S.op("pe", lambda e, k=k: e.transpose(ptb[:, k * 128:k * 128 + nt], xb[:nt, k * 128:(k + 1) * 128], identb[:nt, :nt]),
                 [xb.b, identb.b], [ptb.b], inc=(k == KC - 1))
        S.op("act", lambda e: e.activation(hT[:, :, c0:c0 + nt], ptb[:, :].rearrange("p (k t) -> p k t", k=KC)[:, :, 0:nt], AF.Identity),
             [ptb.b], [hT.b])

    def ln_rows_part1(xt, nt, xn, tmp, mrow):
        rstd, nmr = ln_stats(xt, nt, tmp)
        S.op("act", lambda e: e.activation(xn[:nt, :], xt[:nt, :], AF.Identity, bias=nmr, scale=rstd),
             [xt.b, tmp.b], [xn.b])
        return ln_mod_rows(xn, nt, mrow)

    def ln_to_hT(xt, nt, hT, c0, msel, samp, xn, tmp, mrow=None):
        rstd, nmr = ln_stats(xt, nt, tmp)
        S.op("act", lambda e: e.activation(xn[:nt, :], xt[:nt, :], AF.Identity, bias=nmr, scale=rstd),
             [xt.b, tmp.b], [xn.b])
        if mrow is not None:
            xb = ln_mod_rows(xn, nt, mrow)
            xb_to_hT(xb, nt, hT, c0)
            return
        for half in range(2):
            pst = PF()
            for kk in range(4):
                k = half * 4 + kk
                S.op("pe", lambda e, k=k, kk=kk: e.transpose(pst[:, kk * 128:kk * 128 + nt], xn[:nt, k * 128:(k + 1) * 128],
                                                             ident[:nt, :nt]), [xn.b, ident.b], [pst.b], inc=(kk == 3))
            for kk in range(4):
                k = half * 4 + kk
                if not samp:
                    S.op("act", lambda e, k=k, kk=kk: e.activation(
                        hT[:, k, c0:c0 + nt], pst[:, kk * 128:kk * 128 + nt], AF.Identity,
                        bias=modF[:, (msel) * 8 + k, 0:1], scale=modF[:, (msel + 1) * 8 + k, 0:1]),
                        [pst.b, modF.b], [hT.b])
                else:
                    S.op("dve", lambda e, k=k, kk=kk: e.tensor_tensor(
                        xn[:, 0:nt], pst[:, kk * 128:kk * 128 + nt], SAMP["modFs"][:, (msel + 1) * 8 + k, :], ALU.mult),
                        [pst.b, SAMP["modFs"].b], [xn.b])
                    S.op("dve", lambda e, k=k: e.tensor_tensor(
                        hT[:, k, c0:c0 + nt], xn[:, 0:nt], SAMP["modFs"][:, msel * 8 + k, :], ALU.add),
                        [xn.b, SAMP["modFs"].b], [hT.b])

    with contextlib.ExitStack() as ph:
        wada = sb("wada", [128, KC, 2048], BF16, ph)
        scT = sb("scT", [128, KC, 65], F32, ph)
        scTb = sb("scTb", [128, KC, 65], BF16, ph)
        scR = sb("scR", [128, KC, 192], F32, ph)
        scRb = sb("scRb", [128, KC, 192], BF16, ph)
        badaT = sb("badaT", [128, 48], F32, ph)
        modFt = sb("modFt", [128, 32, 65], F32, ph)
        gSt = sb("gSt", [NTS, 2, D], F32, ph)
        S.dma("sp", scT[:], cTtok[:, :, :], writes=[scT.b])
        S.dma("sp", scR[:], cTrep[:, :, :], writes=[scR.b])
        S.dma("sp", badaT[:], b_adaT[:, :], writes=[badaT.b])
        S.op("act", lambda e: e.activation(scTb[:], scT[:], AF.Silu), [scT.b], [scTb.b])
        S.op("act", lambda e: e.activation(scRb[:], scR[:], AF.Silu), [scR.b], [scRb.b])
        mrt = sb("mrt", [128, D], F32, ph)
        for gi, (cbase, addone) in enumerate([(0, False), (1024, True), (3072, False), (4096, True)]):
            if gi % 2 == 0:
                load_w(wada, w_ada, cbase if gi == 0 else 3072, (cbase if gi == 0 else 3072) + 2048)
            for oc in range(8):
                pst = PF()
                wc = (gi % 2) * 1024 + oc * 128
                for k in range(KC):
                    S.op("pe", lambda e, k=k, wc=wc: e.matmul(pst[:, 0:65], wada[:, k, wc:wc + 128], scTb[:, k, :],
                                                              start=(k == 0), stop=(k == KC - 1)),
                         [wada.b, scTb.b], [pst.b], inc=(k == KC - 1))
                bcol = (cbase // 128) + oc
                if addone:
                    S.op("dve", lambda e, bcol=bcol, gi=gi, oc=oc: e.tensor_scalar(
                        modFt[:, gi * 8 + oc, :], pst[:, 0:65], badaT[:, bcol:bcol + 1], 1.0, ALU.add, ALU.add),
                        [pst.b, badaT.b], [modFt.b])
                else:
                    S.op("dve", lambda e, bcol=bcol, gi=gi, oc=oc: e.tensor_scalar(
                        modFt[:, gi * 8 + oc, :], pst[:, 0:65], badaT[:, bcol:bcol + 1], None, ALU.add),
                        [pst.b, badaT.b], [modFt.b])
            bh = bias_hilo("bada_r%d" % gi, b_ada, cbase, cbase + 1024, ph)
            for half in range(2):
                pst = PF()
                mm_tok(pst, 0, 512, scRb, 0, 128, wada, (gi % 2) * 1024 + half * 512, bh, half * 512)
                S.op("dve", lambda e, half=half, pst=pst, addone=addone: e.tensor_scalar(
                    mrt[:, half * 512:(half + 1) * 512], pst[:, :], 1.0 if addone else 0.0, None, ALU.add), [pst.b], [mrt.b])
            S.dma("sp", modR_scr[:, gi, :], mrt[:], reads=[mrt.b], writes=[modR_scr.b], acc=(gi > 0))
        for gi, cbase in enumerate([2048, 5120]):
            load_w(wada, w_ada, cbase, cbase + 1024)
            bh = bias_hilo("bada_g%d" % gi, b_ada, cbase, cbase + 1024, ph)
            for (dst, r0, nt) in [(gP, 0, 128), (gSt, 128, NTS)]:
                for half in range(2):
                    pst = PF()
                    mm_tok(pst, 0, 512, scRb, r0, nt, wada, half * 512, bh, half * 512)
                    S.op("act", lambda e, dst=dst, nt=nt, half=half, gi=gi: e.activation(
                        dst[:nt, gi, half * 512:(half + 1) * 512], pst[:nt, :], AF.Identity),
                        [pst.b], [dst.b])
        S.op("dve", lambda e: e.tensor_copy(modF[:], modFt[:, :, 0:1]), [modFt.b], [modF.b])
        S.dma("sp", modFs_scr[:, :, :], modFt[:, :, 1:65], reads=[modFt.b], writes=[modFs_scr.b])
        S.dma("sp", gS_scr[:, :, :], gSt[:], reads=[gSt.b], writes=[gS_scr.b])

    S.phase_end()

    def mlstm_gates(zg, nt, blk_all, gt, tri_t=None, ones_t=None):
        tri_t = tri_t or tri
        ones_t = ones_t or ones
        S.op("act", lambda e: e.activation(gt[:nt, 32:36], zg[:nt, 4:8], AF.Exp, scale=-1.0), [zg.b], [gt.b])
        S.op("act", lambda e: e.activation(gt[:nt, 32:36], gt[:nt, 32:36], AF.Ln, bias=onec[:nt, 0:1], scale=1.0),
             [gt.b, onec.b], [gt.b])
        S.op("dve", lambda e: e.tensor_scalar(gt[:nt, 0:4], gt[:nt, 32:36], tokc_sb[:nt, blk_all, 0:1], -1.0,
                                              ALU.mult, ALU.mult), [gt.b, tokc_sb.b], [gt.b])
        S.op("dve", lambda e: e.tensor_scalar(gt[:nt, 4:8], zg[:nt, 0:4], tokc_sb[:nt, blk_all, 1:2], None, ALU.add),
             [zg.b, tokc_sb.b], [gt.b])
        pst = PF()
        S.op("pe", lambda e: e.matmul(pst[:nt, 0:4], tri_t[:nt, :nt], gt[:nt, 0:4], start=True, stop=True),
             [tri_t.b, gt.b], [pst.b], inc=False)
        S.op("pe", lambda e: e.matmul(pst[:nt, 4:8], ones_t[:nt, :nt], gt[:nt, 0:4], start=True, stop=True),
             [ones_t.b, gt.b], [pst.b])
        S.op("dve", lambda e: e.tensor_copy(gt[:nt, 8:16], pst[:nt, 0:8]), [pst.b], [gt.b])
        S.op("act", lambda e: e.activation(gt[:nt, 16:20], gt[:nt, 8:12], AF.Exp), [gt.b], [gt.b])
        S.op("dve", lambda e: e.tensor_tensor(gt[:nt, 36:40], gt[:nt, 4:8], gt[:nt, 8:12], ALU.subtract), [gt.b], [gt.b])
        S.op("act", lambda e: e.activation(gt[:nt, 20:24], gt[:nt, 36:40], AF.Exp), [gt.b], [gt.b])
        S.op("act", lambda e: e.activation(gt[:nt, 24:28], gt[:nt, 12:16], AF.Exp), [gt.b], [gt.b])
        S.op("dve", lambda e: e.tensor_tensor(gt[:nt, 28:32], gt[:nt, 36:40], gt[:nt, 12:16], ALU.add), [gt.b], [gt.b])

    onec = sb("onec", [128, 1])
    S.op("pool", lambda e: e.memset(onec[:], 1.0), [], [onec.b])
    mtmp = sb("mtmp", [4, 8])

    def m_update(gt, nt):
        p1 = PF()
        S.op("pe", lambda e: e.transpose(p1[0:4, 0:nt], gt[:nt, 28:32], ident[:nt, :nt]), [gt.b, ident.b], [p1.b], inc=False)
        S.op("pe", lambda e: e.transpose(p1[0:4, 128:128 + nt], gt[:nt, 12:16], ident[:nt, :nt]), [gt.b, ident.b], [p1.b])
        S.op("dve", lambda e: e.reduce_max(mtmp[:, 0:1], p1[0:4, 0:nt], AX.X), [p1.b], [mtmp.b])
        S.op("dve", lambda e: e.tensor_tensor(mtmp[:, 1:2], p1[0:4, 128:129], mst[:, 0:1], ALU.add), [p1.b, mst.b], [mtmp.b])
        S.op("dve", lambda e: e.tensor_tensor(mst[:, 0:1], mtmp[:, 0:1], mtmp[:, 1:2], ALU.max), [mtmp.b], [mst.b])

    def state_update(ks, v1, gt, nt):
        for hp in range(2):
            pst = PF()
            for hh in range(2):
                h = hp * 2 + hh
                S.op("pe", lambda e, h=h, hh=hh: e.matmul(pst[:, hh * 256:(hh + 1) * 256], ks[:nt, h, :], v1[:nt, h, 0:256],
                                                          start=True, stop=True), [ks.b, v1.b], [pst.b], inc=(hh == 1))
            for hh in range(2):
                h = hp * 2 + hh
                S.op("dve", lambda e, h=h: e.tensor_scalar(CnT[:, h, 0:256], CnT[:, h, 0:256], gt[:, 24 + h:25 + h], None, ALU.mult),
                     [gt.b], [CnT.b])
                S.op("dve", lambda e, h=h, hh=hh: e.scalar_tensor_tensor(
                    CnT[:, h, 0:256], pst[:, hh * 256:(hh + 1) * 256], gt[:, 24 + h:25 + h], CnT[:, h, 0:256], ALU.mult, ALU.add),
                    [pst.b, gt.b], [CnT.b])
        pst = PF()
        for h in range(4):
            S.op("pe", lambda e, h=h: e.matmul(pst[:, h:h + 1], ks[:nt, h, :], ones_col[:nt, 0:1], start=True, stop=True),
                 [ks.b, ones_col.b], [pst.b], inc=(h == 3))
        for h in range(4):
            S.op("dve", lambda e, h=h: e.tensor_scalar(CnT[:, h, 256:257], CnT[:, h, 256:257], gt[:, 24 + h:25 + h], None, ALU.mult),
                 [gt.b], [CnT.b])
            S.op("dve", lambda e, h=h: e.scalar_tensor_tensor(
                CnT[:, h, 256:257], pst[:, h:h + 1], gt[:, 24 + h:25 + h], CnT[:, h, 256:257], ALU.mult, ALU.add),
                [pst.b, gt.b], [CnT.b])
        S.op("act", lambda e: e.activation(CnTb[:], CnT[:], AF.Identity), [CnT.b], [CnTb.b])

    gt = sb("gt", [128, 40])
    lnt = sb("lnt", [128, 20])
    ks = sb("ks", [128, 4, 128], BF16)
    v1 = sb("v1", [128, 4, 257], BF16)
    S.op("pool", lambda e: e.memset(v1[:], 1.0), [], [v1.b])
    xtl = [sb("xt%d" % i, [128, D]) for i in range(2)]
    xn = sb("xn", [128, D])
    xrot = {"i": 0}
    ks_g, v1_g, gt_g = ks, v1, gt

    def XT():
        xrot["i"] ^= 1
        return xtl[xrot["i"]]

    def kv_from_psum(pk, pv0, pv1, nt, ks=None, v1=None, gt=None):
        ks = ks or ks_g
        v1 = v1 or v1_g
        gt = gt or gt_g
        for h in range(4):
            S.op("dve", lambda e, h=h: e.tensor_scalar(ks[:nt, h, :], pk[:nt, h * 128:(h + 1) * 128], gt[:nt, 20 + h:21 + h], None, ALU.mult),
                 [pk.b, gt.b], [ks.b])
        S.op("act", lambda e: e.activation(v1[:nt, 0:2, 0:256], pv0[:nt, :].rearrange("p (h d) -> p h d", h=2), AF.Identity), [pv0.b], [v1.b])
        S.op("act", lambda e: e.activation(v1[:nt, 2:4, 0:256], pv1[:nt, :].rearrange("p (h d) -> p h d", h=2), AF.Identity), [pv1.b], [v1.b])

    def scan_block(hT, c0, ba, W, wk0, bias):
        pg = PF()
        mm_tok(pg, 0, 8, hT, c0, 128, W, wk0 + 1536, bias, wk0 + 1536)
        mlstm_gates(pg, 128, ba, gt)
        pk, pv0, pv1 = PF(), PF(), PF()
        mm_tok(pk, 0, 512, hT, c0, 128, W, wk0, bias, wk0)
        mm_tok(pv0, 0, 512, hT, c0, 128, W, wk0 + 512, bias, wk0 + 512)
        mm_tok(pv1, 0, 512, hT, c0, 128, W, wk0 + 1024, bias, wk0 + 1024)
        kv_from_psum(pk, pv0, pv1, 128)
        state_update(ks, v1, gt, 128)
        m_update(gt, 128)


    def head_ln_s(banks, hst, hmn, nt):
        for h in range(4):
            S.op("dve", lambda e, h=h: e.tensor_copy(hst[:nt, 56 + h:57 + h], banks[h][:nt, 256:257]), [banks[h].b], [hst.b])
        head_ln([banks[0], banks[2]], None, hst, hmn, nt, banks=banks)

    def head_ln(pnum, pden, hst, hmn, nt, banks=None):
        if banks is None:
            S.op("dve", lambda e: e.tensor_copy(hst[:nt, 56:60], pden[:nt, 0:4]), [pden.b], [hst.b])
        S.op("dve", lambda e: e.tensor_scalar(hst[:nt, 52:56], hst[:nt, 56:60], -1.0, None, ALU.mult), [hst.b], [hst.b])
        S.op("dve", lambda e: e.tensor_tensor(hst[:nt, 0:4], hst[:nt, 56:60], hst[:nt, 52:56], ALU.max), [hst.b], [hst.b])
        S.op("dve", lambda e: e.tensor_scalar(hst[:nt, 0:4], hst[:nt, 0:4], 1.0, None, ALU.max), [hst.b], [hst.b])
        S.op("dve", lambda e: e.scalar_tensor_tensor(hst[:nt, 4:8], hst[:nt, 0:4], LN_EPS, hst[:nt, 0:4], ALU.mult, ALU.mult), [hst.b], [hst.b])
        for h in range(4):
            pn = pnum[h // 2] if banks is None else banks[h]
            cs = (h % 2) * 256 if banks is None else 0
            S.op("dve", lambda e, h=h, pn=pn, cs=cs: e.bn_stats(hst[:nt, 8 + 6 * h:14 + 6 * h], pn[:nt, cs:cs + 256]), [pn.b], [hst.b])
            S.op("dve", lambda e, h=h: e.bn_aggr(hst[:nt, 32 + 2 * h:34 + 2 * h], hst[:nt, 8 + 6 * h:14 + 6 * h]), [hst.b], [hst.b])
        mvv = hst[:nt, 32:40].rearrange("p (h t) -> p h t", t=2)
        S.op("dve", lambda e: e.tensor_tensor(hst[:nt, 40:44], mvv[:, :, 1], hst[:nt, 4:8], ALU.add), [hst.b], [hst.b])
        S.op("act", lambda e: e.activation(hst[:nt, 40:44], hst[:nt, 40:44], AF.Ln), [hst.b], [hst.b])
        S.op("act", lambda e: e.activation(hst[:nt, 44:48], hst[:nt, 40:44], AF.Exp, scale=-0.5), [hst.b], [hst.b])
        S.op("dve", lambda e: e.scalar_tensor_tensor(hst[:nt, 48:52], mvv[:, :, 0], -1.0, hst[:nt, 44:48], ALU.mult, ALU.mult), [hst.b], [hst.b])
        for h in range(4):
            pn = pnum[h // 2] if banks is None else banks[h]
            cs = (h % 2) * 256 if banks is None else 0
            S.op("act", lambda e, h=h, pn=pn, cs=cs: e.activation(hmn[:nt, h * 256:(h + 1) * 256], pn[:nt, cs:cs + 256], AF.Identity,
                                                                  bias=hst[:nt, 48 + h:49 + h], scale=hst[:nt, 44 + h:45 + h]),
                 [pn.b, hst.b], [hmn.b])

    def hm_transpose_out(hmn, nt, hmT, c0, nwT, sgog):
        for half in range(2):
            pst = PF()
            for kk in range(4):
                k = half * 4 + kk
                S.op("pe", lambda e, k=k, kk=kk, pst=pst: e.transpose(pst[:, kk * 128:kk * 128 + nt], hmn[:nt, k * 128:(k + 1) * 128], ident[:nt, :nt]),
                     [hmn.b, ident.b], [pst.b], inc=(kk == 3))
            for kk in range(4):
                k = half * 4 + kk
                S.op("dve", lambda e, k=k, kk=kk, pst=pst: e.scalar_tensor_tensor(
                    hmT[:, k, c0:c0 + nt], pst[:, kk * 128:kk * 128 + nt], nwT[:, k:k + 1], sgog[:, k, c0:c0 + nt], ALU.mult, ALU.mult),
                    [pst.b, nwT.b, sgog.b], [hmT.b])

    if cfg.sample:
        sC0 = din("sC0", [NSEQ_S, 128, 4 * 257])
        sm0rep = din("sm0rep", [128, 64])
        sm0T = din("sm0T", [4, NSEQ_S])
        c_tri_s = din("c_tri_s", [NTS, NTS])
        c_ones_s = din("c_ones_s", [NTS, NTS])
        c_seqmask = din("c_seqmask", [128, NSEQ_S, NTS])
        c_seqmaskT = din("c_seqmaskT", [NTS, NSEQ_S])
        c_pick = din("c_pick", [NTS, 128])
        ckT = din("ckT", [64, NSEQ_S, 4, 128])
        cvn = din("cvn", [128, NSEQ_S, 256])
        ck_nat = din("ck_nat", [NSEQ_S, 128, 256])
        cv_nat = din("cv_nat", [NSEQ_S, 128, 256])
        c_sbias = din("c_sbias", [NTS, 16, 192])
        sC_out = dout("sC_out", [NSEQ_S, 128, 4 * 257])
        sm_out = dout("sm_out", [4, NSEQ_S])
        sk_out = dout("sk_out", [NSEQ_S, 128, 256])
        sv_out = dout("sv_out", [NSEQ_S, 128, 256])

    def samp_mlstm(hT, c0, Wm, bm, hmT, sgog, nwT, ph):
        nt = NTS
        tri_s = sb("tri_s", [NTS, NTS], F32, ph)
        ones_s = sb("ones_s", [NTS, NTS], F32, ph)
        smaskb = sb("smaskb", [128, NSEQ_S, NTS], BF16, ph)
        smaskT = sb("smaskT", [NTS, NSEQ_S], F32, ph)
        pick = sb("pick", [NTS, 128], F32, ph)
        em0 = sb("em0", [128, 64], F32, ph)
        m0T = sb("m0T", [4, NSEQ_S], F32, ph)
        S.dma("sp", tri_s[:], c_tri_s[:, :], writes=[tri_s.b])
        S.dma("sp", ones_s[:], c_ones_s[:, :], writes=[ones_s.b])
        S.dma("pool", smaskb[:], c_seqmask[:, :, :], writes=[smaskb.b])
        S.dma("sp", smaskT[:], c_seqmaskT[:, :], writes=[smaskT.b])
        S.dma("sp", pick[:], c_pick[:, :], writes=[pick.b])
        S.dma("sp", em0[:], sm0rep[:, :], writes=[em0.b])
        S.dma("sp", m0T[:], sm0T[:, :], writes=[m0T.b])
        S.op("act", lambda e: e.activation(em0[:], em0[:], AF.Exp), [em0.b], [em0.b])
        qs = sb("sqs", [128, 4, 128], BF16, ph)
        qkT = sb("sqkT", [128, 8, NTS], BF16, ph)
        qm = sb("sqm", [128, 4, NSEQ_S, NTS], BF16, ph)
        SmT = sb("sSmT", [NTS, 4, NTS], BF16, ph)
        hmn = sb("shmn", [128, D], F32, ph)
        hst = sb("shst", [128, 64], F32, ph)
        Cin = [sb("Cin%d" % i, [128, 4, 257], F32, ph) for i in range(2)]
        Cbf = [sb("Cbf%d" % i, [128, 4, 257], BF16, ph) for i in range(2)]
        pg = PF()
        mm_tok(pg, 0, 8, hT, c0, nt, Wm, 2048, bm, 2048)
        mlstm_gates(pg, nt, NB_ALL - 1, gt, tri_s, ones_s)
        pq = PF()
        mm_tok(pq, 0, 512, hT, c0, nt, Wm, 0, bm, 0)
        for h in range(4):
            S.op("dve", lambda e, h=h, pq=pq: e.tensor_scalar(qs[:nt, h, :], pq[:nt, h * 128:(h + 1) * 128], gt[:nt, 16 + h:17 + h],
                                                              128.0 ** -0.5, ALU.mult, ALU.mult), [pq.b, gt.b], [qs.b])
        pk, pv0, pv1 = PF(), PF(), PF()
        mm_tok(pk, 0, 512, hT, c0, nt, Wm, 512, bm, 512)
        mm_tok(pv0, 0, 512, hT, c0, nt, Wm, 1024, bm, 1024)
        mm_tok(pv1, 0, 512, hT, c0, nt, Wm, 1536, bm, 1536)
        kv_from_psum(pk, pv0, pv1, nt)
        ptb = PB()
        for h in range(4):
            S.op("pe", lambda e, h=h: e.transpose(ptb[:, h * 64:(h + 1) * 64], qs[:nt, h, :], identb[:nt, :nt]),
                 [qs.b, identb.b], [ptb.b], inc=False)
        for h in range(4):
            S.op("pe", lambda e, h=h: e.transpose(ptb[:, 256 + h * 64:256 + (h + 1) * 64], ks[:nt, h, :], identb[:nt, :nt]),
                 [ks.b, identb.b], [ptb.b], inc=(h == 3))
        S.op("act", lambda e: e.activation(qkT[:].rearrange("p a b -> p (a b)"), ptb[:, 0:512], AF.Identity), [ptb.b], [qkT.b])
        for h in range(4):
            S.op("dve", lambda e, h=h: e.tensor_tensor(qm[:, h, :, :], qkT[:, h, :].unsqueeze(1).broadcast_to([128, NSEQ_S, NTS]), smaskb[:], ALU.mult),
                 [qkT.b, smaskb.b], [qm.b])
        pS = PF()
        for h in range(4):
            S.op("pe", lambda e, h=h: e.matmul(pS[:nt, h * 64:(h + 1) * 64], qkT[:, 4 + h, :], qkT[:, h, :], start=True, stop=True),
                 [qkT.b], [pS.b], inc=(h == 3))
        S.op("dve", lambda e: e.tensor_tensor(SmT[:], pS[:nt, 0:256].rearrange("p (h t) -> p h t", h=4),
                                              tri_s[:, :].unsqueeze(1).broadcast_to([NTS, 4, NTS]), ALU.mult), [pS.b, tri_s.b], [SmT.b])
        banks = [PL(0), PL(1), psf[2], psf[3]]
        for h in range(4):
            S.op("pe", lambda e, h=h: e.matmul(banks[h][:nt, 0:257], SmT[:, h, :], v1[:nt, h, 0:257], start=True, stop=False),
                 [SmT.b, v1.b], [banks[h].b], inc=True)
        for b in range(NSEQ_S):
            ci_ = Cin[b % 2]
            cb_ = Cbf[b % 2]
            S.dma("sp", ci_[:].rearrange("p h v -> p (h v)"), sC0[b, :, :], writes=[ci_.b])
            for h in range(4):
                S.op("dve", lambda e, b=b, h=h, ci_=ci_, cb_=cb_: e.tensor_scalar(cb_[:, h, :], ci_[:, h, :], em0[:, b * 4 + h:b * 4 + h + 1], None, ALU.mult),
                     [ci_.b, em0.b], [cb_.b])
            for h in range(4):
                S.op("pe", lambda e, h=h, b=b, cb_=cb_: e.matmul(banks[h][:nt, 0:257], qm[:, h, b, :], cb_[:, h, 0:257],
                                                               start=False, stop=(b == NSEQ_S - 1)), [qm.b, cb_.b], [banks[h].b], inc=True)
        head_ln_s(banks, hst, hmn, nt)
        hm_transpose_out(hmn, nt, hmT, c0, nwT, sgog)
        p1 = PF()
        S.op("pe", lambda e: e.transpose(p1[0:4, 0:nt], gt[:nt, 28:32], ident[:nt, :nt]), [gt.b, ident.b], [p1.b], inc=False)
        S.op("pe", lambda e: e.transpose(p1[0:4, 128:128 + nt], gt[:nt, 12:16], ident[:nt, :nt]), [gt.b, ident.b], [p1.b])
        mn = sb("smn", [4, 3, NSEQ_S], F32, ph)
        S.op("dve", lambda e: e.reduce_max(mn[:, 0, :], p1[0:4, 0:nt].rearrange("p (b t) -> p b t", t=4), AX.X), [p1.b], [mn.b])
        S.op("dve", lambda e: e.tensor_tensor(mn[:, 1, :], p1[0:4, 128:128 + nt].rearrange("p (b t) -> p b t", t=4)[:, :, 0], m0T[:, :], ALU.add),
             [p1.b, m0T.b], [mn.b])
        S.op("dve", lambda e: e.tensor_tensor(mn[:, 2, :], mn[:, 0, :], mn[:, 1, :], ALU.max), [mn.b], [mn.b])
        S.dma("sp", sm_out[:, :], mn[:, 2, :], reads=[mn.b], writes=[sm_out.b])
        dg = sb("sdg", [4, 4, NSEQ_S], F32, ph)
        S.op("dve", lambda e: e.tensor_tensor(dg[:], ident[0:4, 0:4].unsqueeze(2).broadcast_to([4, 4, NSEQ_S]),
                                              mn[:, 2, :].unsqueeze(1).broadcast_to([4, 4, NSEQ_S]), ALU.mult), [ident.b, mn.b], [dg.b])
        dg2 = sb("sdg2", [NTS, NSEQ_S, 4], F32, ph)
        S.op("dve", lambda e: e.tensor_tensor(dg2[:], gt[:nt, 12:16].unsqueeze(1).broadcast_to([NTS, NSEQ_S, 4]),
                                              smaskT[:, :].unsqueeze(2).broadcast_to([NTS, NSEQ_S, 4]), ALU.mult), [gt.b, smaskT.b], [dg2.b])
        pr = PF()
        S.op("pe", lambda e: e.matmul(pr[:, 0:64], ones[0:4, 0:128], dg[:].rearrange("p a b -> p (a b)"), start=True, stop=True),
             [ones.b, dg.b], [pr.b], inc=False)
        S.op("pe", lambda e: e.matmul(pr[:, 64:128], pick[:, :], dg2[:].rearrange("p a b -> p (a b)"), start=True, stop=True),
             [pick.b, dg2.b], [pr.b])
        fac = sb("sfac", [128, NSEQ_S, 4], F32, ph)
        S.op("act", lambda e: e.activation(fac[:], pr[:, 64:128].rearrange("p (b h) -> p b h", h=4), AF.Identity), [pr.b], [fac.b])
        S.op("dve", lambda e: e.tensor_tensor(fac[:], fac[:], pr[:, 0:64].rearrange("p (h b) -> p b h", h=4), ALU.subtract), [pr.b, fac.b], [fac.b])
        S.op("act", lambda e: e.activation(fac[:], fac[:], AF.Exp), [fac.b], [fac.b])
        ksm = [sb("ksm%d" % i, [NTS, 4, 128], BF16, ph) for i in range(2)]
        Cout = [sb("Cout%d" % i, [128, 4, 257], F32, ph) for i in range(2)]
        for b in range(NSEQ_S):
            km = ksm[b % 2]
            co = Cout[b % 2]
            ci_ = Cin[b % 2]
            S.dma("sp", ci_[:].rearrange("p h v -> p (h v)"), sC0[b, :, :], writes=[ci_.b])
            S.op("dve", lambda e, b=b, km=km: e.tensor_scalar(km[:], ks[:nt, :, :], smaskT[:, b:b + 1], None, ALU.mult), [ks.b, smaskT.b], [km.b])
            for hp in range(2):
                pst = PF()
                for hh in range(2):
                    h = hp * 2 + hh
                    S.op("pe", lambda e, h=h, hh=hh, km=km, pst=pst: e.matmul(pst[:, hh * 256:(hh + 1) * 256], km[:, h, :], v1[:nt, h, 0:256],
                                                                             start=True, stop=True), [km.b, v1.b], [pst.b], inc=(hh == 1))
                for hh in range(2):
                    h = hp * 2 + hh
                    S.op("dve", lambda e, h=h, b=b, ci_=ci_, co=co: e.tensor_scalar(co[:, h, 0:256], ci_[:, h, 0:256], em0[:, b * 4 + h:b * 4 + h + 1],
                                                                                   fac[:, b, h:h + 1], ALU.mult, ALU.mult), [ci_.b, em0.b, fac.b], [co.b])
                    S.op("dve", lambda e, h=h, hh=hh, b=b, co=co, pst=pst: e.scalar_tensor_tensor(
                        co[:, h, 0:256], pst[:, hh * 256:(hh + 1) * 256], fac[:, b, h:h + 1], co[:, h, 0:256], ALU.mult, ALU.add),
                        [pst.b, fac.b], [co.b])
            pst = PF()
            for h in range(4):
                S.op("pe", lambda e, h=h, km=km, pst=pst: e.matmul(pst[:, h:h + 1], km[:, h, :], ones_col[:nt, 0:1], start=True, stop=True),
                     [km.b, ones_col.b], [pst.b], inc=(h == 3))
            for h in range(4):
                S.op("dve", lambda e, h=h, b=b, ci_=ci_, co=co: e.tensor_scalar(co[:, h, 256:257], ci_[:, h, 256:257], em0[:, b * 4 + h:b * 4 + h + 1],
                                                                               fac[:, b, h:h + 1], ALU.mult, ALU.mult), [ci_.b, em0.b, fac.b], [co.b])
                S.op("dve", lambda e, h=h, b=b, co=co, pst=pst: e.scalar_tensor_tensor(
                    co[:, h, 256:257], pst[:, h:h + 1], fac[:, b, h:h + 1], co[:, h, 256:257], ALU.mult, ALU.add), [pst.b, fac.b], [co.b])
            S.dma("sp", sC_out[b, :, :], co[:].rearrange("p h v -> p (h v)"), reads=[co.b], writes=[sC_out.b])

    def samp_swa(hT, c0, Wqa, Wkd, Wkv2, bkv2, bqk, snk, haT, ph):
        nt = NTS
        qaS = sb("qaS", [64, 16, NTS], BF16, ph)
        kdS = sb("kdS", [64, 4, NTS], BF16, ph)
        kvS = sb("kvS", [NTS, 512], BF16, ph)
        kvSf = sb("kvSf", [NTS, 512], F32, ph)
        ckb = sb("ckb", [64, NSEQ_S, 4, 128], BF16, ph)
        cvb = sb("cvb", [128, NSEQ_S, 256], BF16, ph)
        sbias = sb("sbias", [NTS, 16, 192], F32, ph)
        smask = sb("smask2", [128, NSEQ_S, NTS], F32, ph)
        smaskb = sb("smaskb2", [128, NSEQ_S, NTS], BF16, ph)
        S.dma("pool", ckb[:], ckT[:, :, :, :], writes=[ckb.b])
        S.dma("pool", cvb[:], cvn[:, :, :], writes=[cvb.b])
        S.dma("sp", sbias[:], c_sbias[:, :, :], writes=[sbias.b])
        S.dma("sp", smask[:], c_seqmask[:, :, :], writes=[smask.b])
        S.op("dve", lambda e: e.tensor_copy(smaskb[:], smask[:]), [smask.b], [smaskb.b])
        S.dma("sp", sk_out[:, 0:124, :], ck_nat[:, 4:128, :], writes=[sk_out.b])
        S.dma("sp", sv_out[:, 0:124, :], cv_nat[:, 4:128, :], writes=[sv_out.b])
        for h in range(16):
            pst = PF()
            for k in range(KC):
                S.op("pe", lambda e, k=k, h=h, pst=pst: e.matmul(pst[0:64, 0:nt], Wqa[:, k, h * 64:(h + 1) * 64], hT[:, k, c0:c0 + nt],
                                                                start=(k == 0), stop=(k == KC - 1)), [Wqa.b, hT.b], [pst.b], inc=(k == KC - 1))
            S.op("act", lambda e, h=h, pst=pst: e.activation(qaS[:, h, :], pst[0:64, 0:nt], AF.Identity, bias=bqk[:, h:h + 1], scale=1.0),
                 [pst.b, bqk.b], [qaS.b])
        for jj in range(4):
            pst = PF()
            for k in range(KC):
                S.op("pe", lambda e, k=k, jj=jj, pst=pst: e.matmul(pst[0:64, 0:nt], Wkd[:, k, jj * 64:(jj + 1) * 64], hT[:, k, c0:c0 + nt],
                                                                  start=(k == 0), stop=(k == KC - 1)), [Wkd.b, hT.b], [pst.b], inc=(k == KC - 1))
            S.op("act", lambda e, jj=jj, pst=pst: e.activation(kdS[:, jj, :], pst[0:64, 0:nt], AF.Identity, bias=bqk[:, 16 + jj:17 + jj], scale=1.0),
                 [pst.b, bqk.b], [kdS.b])
        pst = PF()
        mm_tok(pst, 0, 512, hT, c0, nt, Wkv2, 0, bkv2, 0)
        S.op("act", lambda e, pst=pst: e.activation(kvS[:, :], pst[:nt, :], AF.Identity), [pst.b], [kvS.b])
        S.op("dve", lambda e, pst=pst: e.tensor_copy(kvSf[:, :], pst[:nt, :]), [pst.b], [kvSf.b])
        for b in range(NSEQ_S):
            S.dma("sp", sk_out[b, 124:128, :], kvSf[4 * b:4 * b + 4, 0:256], reads=[kvSf.b], writes=[sk_out.b], acc=True)
            S.dma("sp", sv_out[b, 124:128, :], kvSf[4 * b:4 * b + 4, 256:512], reads=[kvSf.b], writes=[sv_out.b], acc=True)
        qmS = sb("qmS", [64, NSEQ_S, NTS], BF16, ph)
        Sb_ = sb("sSb", [NTS, 192], F32, ph)
        Pb_ = sb("sPb", [NTS, 192], BF16, ph)
        PTc = sb("sPTc", [128, NTS], BF16, ph)
        PTn = sb("sPTn", [NTS, NTS], BF16, ph)
        PTm = sb("sPTm", [128, NSEQ_S, NTS], BF16, ph)
        ast = sb("sast", [NTS, 96], F32, ph)
        hab = sb("shab", [NTS, D], BF16, ph)
        po = [PL(0), PL(1)]
        for h in range(16):
            hp, hh = h // 2, h % 2
            jj = h // 4
            b0 = hh * 64
            S.op("dve", lambda e, h=h: e.tensor_tensor(qmS[:], qaS[:, h, :].unsqueeze(1).broadcast_to([64, NSEQ_S, NTS]), smaskb[0:64, :, :], ALU.mult),
                 [qaS.b, smaskb.b], [qmS.b])
            pS = PF()
            for b in range(NSEQ_S):
                S.op("pe", lambda e, b=b, jj=jj, pS=pS: e.matmul(pS[:nt, 0:128], qmS[:, b, :], ckb[:, b, jj, :],
                                                                start=(b == 0), stop=(b == NSEQ_S - 1)), [qmS.b, ckb.b], [pS.b], inc=False)
            S.op("pe", lambda e, h=h, jj=jj, pS=pS: e.matmul(pS[:nt, 128:192], qaS[:, h, :], kdS[:, jj, :], start=True, stop=True),
                 [qaS.b, kdS.b], [pS.b])
            S.op("dve", lambda e, h=h, pS=pS: e.scalar_tensor_tensor(Sb_[:, :], pS[:nt, 0:192], 0.125, sbias[:, h, :], ALU.mult, ALU.add),
                 [pS.b, sbias.b], [Sb_.b])
            S.op("dve", lambda e, h=h: e.reduce_max(ast[:, h:h + 1], Sb_[:, :], AX.X), [Sb_.b], [ast.b])
            S.op("dve", lambda e, h=h: e.tensor_tensor(ast[:, 16 + h:17 + h], ast[:, h:h + 1], snk[:nt, h:h + 1], ALU.max), [ast.b, snk.b], [ast.b])
            S.op("dve", lambda e, h=h: e.tensor_scalar(ast[:, 32 + h:33 + h], ast[:, 16 + h:17 + h], -1.0, None, ALU.mult), [ast.b], [ast.b])
            S.op("dve", lambda e, h=h: e.tensor_tensor(ast[:, 64 + h:65 + h], snk[:nt, h:h + 1], ast[:, 16 + h:17 + h], ALU.subtract), [ast.b, snk.b], [ast.b])
            S.op("act", lambda e, h=h: e.activation(ast[:, 64 + h:65 + h], ast[:, 64 + h:65 + h], AF.Exp), [ast.b], [ast.b])
            S.op("act", lambda e, h=h: e.activation(Pb_[:, :], Sb_[:, :], AF.Exp, bias=ast[:, 32 + h:33 + h], scale=1.0, accum_out=ast[:, 48 + h:49 + h]),
                 [Sb_.b, ast.b], [Pb_.b, ast.b])
            ptb = PB()
            S.op("pe", lambda e, ptb=ptb: e.transpose(ptb[:, 0:nt], Pb_[:, 0:128], identb[:nt, :nt]), [Pb_.b, identb.b], [ptb.b], inc=False)
            S.op("pe", lambda e, ptb=ptb: e.transpose(ptb[0:nt, 64:64 + nt], Pb_[:, 128:192], identb[:nt, :nt]), [Pb_.b, identb.b], [ptb.b])
            S.op("act", lambda e, ptb=ptb: e.activation(PTc[:, :], ptb[:, 0:nt], AF.Identity), [ptb.b], [PTc.b])
            S.op("act", lambda e, ptb=ptb: e.activation(PTn[:, :], ptb[0:nt, 64:64 + nt], AF.Identity), [ptb.b], [PTn.b])
            S.op("dve", lambda e: e.tensor_tensor(PTm[:], PTc[:, :].unsqueeze(1).broadcast_to([128, NSEQ_S, NTS]), smaskb[:], ALU.mult),
                 [PTc.b, smaskb.b], [PTm.b])
            pod = po[h // 8]
            oc0 = (h % 8) * 64
            for b in range(NSEQ_S):
                S.op("pe", lambda e, b=b, jj=jj, pod=pod, oc0=oc0: e.matmul(pod[:nt, oc0:oc0 + 64], PTm[:, b, :], cvb[:, b, jj * 64:(jj + 1) * 64],
                                                                           start=(b == 0), stop=False), [PTm.b, cvb.b], [pod.b], inc=False)
            S.op("pe", lambda e, jj=jj, pod=pod, oc0=oc0: e.matmul(pod[:nt, oc0:oc0 + 64], PTn[:, :], kvS[:, 256 + jj * 64:256 + (jj + 1) * 64],
                                                                  start=False, stop=True), [PTn.b, kvS.b], [pod.b])
        S.op("dve", lambda e: e.tensor_tensor(ast[:, 80:96], ast[:, 48:64], ast[:, 64:80], ALU.add), [ast.b], [ast.b])
        S.op("dve", lambda e: e.reciprocal(ast[:, 80:96], ast[:, 80:96]), [ast.b], [ast.b])
        for g in range(2):
            S.op("dve", lambda e, g=g: e.tensor_tensor(
                hab[:, g * 512:(g + 1) * 512].rearrange("p (h d) -> p h d", h=8), po[g][:nt, :].rearrange("p (h d) -> p h d", h=8),
                ast[:, 80 + 8 * g:88 + 8 * g].unsqueeze(2).broadcast_to([NTS, 8, 64]), ALU.mult), [po[g].b, ast.b], [hab.b])
        ptb = PB()
        for k in range(KC):
            S.op("pe", lambda e, k=k, ptb=ptb: e.transpose(ptb[:, k * 64:(k + 1) * 64], hab[:, k * 128:(k + 1) * 128], identb[:nt, :nt]),
                 [hab.b, identb.b], [ptb.b], inc=(k == KC - 1))
        S.op("act", lambda e, ptb=ptb: e.activation(haT[:, :, c0:c0 + nt], ptb[:, 0:512].rearrange("p (k t) -> p k t", k=KC), AF.Identity),
             [ptb.b], [haT.b])

    outs = []
    try:
        if NSCAN > 0:
            rot["n"] = 6
            with contextlib.ExitStack() as ph:
                Wkv = sb("Wkv", [128, KC, 1544], BF16, ph)
                load_w(Wkv, w_in, 512, 2056)
                bkv = bias_hilo("bkv", b_in, 512, 2056, ph)
                hTs = [sb("hTs%d" % i, [128, KC, 128], BF16, ph) for i in range(2)]
                mrow1 = sb("mrow_s", [128, 2, D], F32, ph)
                S.dma("sp", mrow1[:], modR_scr[:, 0:2, :], reads=[modR_scr.b], writes=[mrow1.b])
                xn2 = [xn, sb("xn_b", [128, D], F32, ph)]
                lnt2 = [lnt, sb("lnt_b", [128, 20], F32, ph)]
                gt2 = [gt, sb("gt_b", [128, 40], F32, ph)]
                ks2 = [ks, sb("ks_b", [128, 4, 128], BF16, ph)]
                v12 = [v1, sb("v1_b", [128, 4, 257], BF16, ph)]
                S.op("pool", lambda e: e.memset(v12[1][:], 1.0), [], [v12[1].b])

                xbs = {}

                def stA1(i):
                    xt = XT()
                    S.dma("sp", xt[:], xs[i * 128:(i + 1) * 128, :], writes=[xt.b])
                    xbs[i] = ln_rows_part1(xt, 128, xn2[i % 2], lnt2[i % 2], mrow1)

                def stA2(i):
                    xb_to_hT(xbs.pop(i), 128, hTs[i % 2], 0)
                    pg = PF()
                    mm_tok(pg, 0, 8, hTs[i % 2], 0, 128, Wkv, 1536, bkv, 1536)
                    mlstm_gates(pg, 128, i, gt2[i % 2])

                def stB(i):
                    pk, pv0, pv1 = PF(), PF(), PF()
                    mm_tok(pk, 0, 512, hTs[i % 2], 0, 128, Wkv, 0, bkv, 0)
                    mm_tok(pv0, 0, 512, hTs[i % 2], 0, 128, Wkv, 512, bkv, 512)
                    mm_tok(pv1, 0, 512, hTs[i % 2], 0, 128, Wkv, 1024, bkv, 1024)
                    kv_from_psum(pk, pv0, pv1, 128, ks2[i % 2], v12[i % 2], gt2[i % 2])

                def stC(i):
                    state_update(ks2[i % 2], v12[i % 2], gt2[i % 2], 128)
                    m_update(gt2[i % 2], 128)

                stA1(0)
                stA2(0)
                if NSCAN > 1:
                    stA1(1)
                stB(0)
                for ba in range(NSCAN):
                    if ba + 2 < NSCAN:
                        stA1(ba + 2)
                    if ba + 1 < NSCAN:
                        stA2(ba + 1)
                    stC(ba)
                    if ba + 1 < NSCAN:
                        stB(ba + 1)
            rot["n"] = 4
            rot["f"] = 0
            S.phase_end()

        own_blocks = [("h2", NSCAN, 128), ("h1", NSCAN + 1, 128)] + [("own", NSCAN + 2 + i, 128) for i in range(NOWN)]
        npass = 2 if NOWN >= 8 else 1
        per = (len(own_blocks) + npass - 1) // npass
        passes = [own_blocks[i * per:(i + 1) * per] for i in range(npass)]
        if cfg.sample:
            passes.append([("samp", -1, NTS)])
        NTPMAX = max(sum(b[2] for b in p) for p in passes)
        NBPMAX = max(len(p) for p in passes)

        kd_carry = sb("kd_carry", [64, 4, 128], BF16)
        kv_carry = sb("kv_carry", [128, 512], BF16)
        cv_carry = sb("cv_carry", [128, 2 * FC, 2])
        S.op("pool", lambda e: e.memset(kd_carry[:], 0.0), [], [kd_carry.b])
        S.op("pool", lambda e: e.memset(kv_carry[:], 0.0), [], [kv_carry.b])
        S.op("pool", lambda e: e.memset(cv_carry[:], 0.0), [], [cv_carry.b])
        cvv = sb("cvv", [128, 1])
        S.dma("sp", cvv[:], convvalid[:, :], writes=[cvv.b])
        x1row = {"n": 0}

        for pi, blocks in enumerate(passes):
            rot["n"] = 4
            rot["f"] = 0
            offs = []
            o = 0
            for b in blocks:
                offs.append(o)
                o += b[2]
            NTP = o
            NBP = len(blocks)
            prompt_passes = [i for i, p_ in enumerate(passes) if any(b_[0] != "samp" for b_ in p_)]
            blocks_last_prompt_pass = (pi == prompt_passes[-1])
            with contextlib.ExitStack() as pp:
                hT = sb("hT", [128, KC, NTP], BF16, pp)
                if any(b_[0] == "samp" for b_ in blocks):
                    SAMP["modFs"] = sb("modFs", [128, 32, NTS], F32, pp)
                    SAMP["gS"] = sb("gS", [NTS, 2, D], F32, pp)
                    S.dma("sp", SAMP["modFs"][:], modFs_scr[:, :, :], reads=[modFs_scr.b], writes=[SAMP["modFs"].b])
                    S.dma("sp", SAMP["gS"][:], gS_scr[:, :, :], reads=[gS_scr.b], writes=[SAMP["gS"].b])
                pm_ = pp.enter_context(contextlib.ExitStack())
                hmT = sb("hmT", [128, KC, NTP], BF16, pm_)
                haT = sb("haT", [128, KC, NTP], BF16, pm_)
                S.op("pool", lambda e: e.memset(hmT[:], 0.0), [], [hmT.b])
                S.op("pool", lambda e: e.memset(haT[:], 0.0), [], [haT.b])
                phB = pm_.enter_context(contextlib.ExitStack())
                Wm = sb("Wm", [128, KC, 2056], BF16, phB)
                Wog = sb("Wog", [128, KC, 1024], BF16, phB)
                load_w(Wm, w_in, 0, 2056)
                load_w(Wog, w_in, 2056, 3080)
                bm = bias_hilo("bm", b_in, 0, 2056, phB)
                with contextlib.ExitStack() as ph:
                    mrowA = sb("mrowA", [128, 2, D], F32, ph)
                    S.dma("sp", mrowA[:], modR_scr[:, 0:2, :], reads=[modR_scr.b], writes=[mrowA.b])
                    xnA = [xn, sb("xnA", [128, D], F32, ph)]
                    lntA = [lnt, sb("lntA", [128, 20], F32, ph)]
                    pend = None
                    for bi, (kind, ba, nt) in enumerate(blocks):
                        xt = XT()
                        if kind == "samp":
                            S.dma("sp", xt[:nt, :], xsamp[:, :], writes=[xt.b])
                            ln_to_hT(xt, nt, hT, offs[bi], 0, True, xn, lnt, None)
                            continue
                        S.dma("sp", xt[:], xs[ba * 128:(ba + 1) * 128, :], writes=[xt.b])
                        xb_ = ln_rows_part1(xt, nt, xnA[bi % 2], lntA[bi % 2], mrowA)
                        if pend is not None:
                            xb_to_hT(pend[0], 128, hT, pend[1])
                        pend = (xb_, offs[bi])
                    if pend is not None:
                        xb_to_hT(pend[0], 128, hT, pend[1])
                S.phase_end()

                if True:
                    ph = phB
                    sgog = sb("sgog", [128, KC, NTP], BF16, ph)
                    binT = sb("binT", [128, 32], F32, ph)
                    nwT = sb("nwT", [128, KC], F32, ph)
                    S.dma("sp", binT[:], b_inT[:, :], writes=[binT.b])
                    S.dma("sp", nwT[:], norm_wT[:, :], writes=[nwT.b])
                    qs = sb("qs", [128, 4, 128], BF16, ph)
                    qkT = sb("qkT", [128, 8, 128], BF16, ph)
                    SmT = sb("SmT", [128, 4, 128], BF16, ph)
                    hmn = sb("hmn", [128, D], F32, ph)
                    hst = sb("hst", [128, 64], F32, ph)
                    t0 = 0
                    while t0 < NTP:
                        n = min(512, NTP - t0)
                        for oc in range(KC):
                            pst = PF()
                            for k in range(KC):
                                S.op("pe", lambda e, k=k, oc=oc, t0=t0, n=n: e.matmul(
                                    pst[:, 0:n], Wog[:, k, oc * 128:(oc + 1) * 128], hT[:, k, t0:t0 + n],
                                    start=(k == 0), stop=(k == KC - 1)), [Wog.b, hT.b], [pst.b], inc=(k == KC - 1))
                            S.op("act", lambda e, oc=oc, t0=t0, n=n: e.activation(
                                sgog[:, oc, t0:t0 + n], pst[:, 0:n], AF.Sigmoid, bias=binT[:, oc:oc + 1], scale=1.0),
                                [pst.b, binT.b], [sgog.b])
                        t0 += n
                    for bi, (kind, ba, nt) in enumerate(blocks):
                        c0 = offs[bi]
                        if kind == "h2":
                            scan_block(hT, c0, ba, Wm, 512, bm)
                            continue
                        if kind == "samp":
                            samp_mlstm(hT, c0, Wm, bm, hmT, sgog, nwT, ph)
                            continue
                        pg = PF()
                        mm_tok(pg, 0, 8, hT, c0, 128, Wm, 2048, bm, 2048)
                        mlstm_gates(pg, 128, ba, gt)
                        pq = PF()
                        mm_tok(pq, 0, 512, hT, c0, 128, Wm, 0, bm, 0)
                        for h in range(4):
                            S.op("dve", lambda e, h=h, pq=pq: e.tensor_scalar(qs[:, h, :], pq[:, h * 128:(h + 1) * 128], gt[:, 16 + h:17 + h],
                                                                              128.0 ** -0.5, ALU.mult, ALU.mult), [pq.b, gt.b], [qs.b])
                        pk, pv0, pv1 = PF(), PF(), PF()
                        mm_tok(pk, 0, 512, hT, c0, 128, Wm, 512, bm, 512)
                        mm_tok(pv0, 0, 512, hT, c0, 128, Wm, 1024, bm, 1024)
                        mm_tok(pv1, 0, 512, hT, c0, 128, Wm, 1536, bm, 1536)
                        kv_from_psum(pk, pv0, pv1, 128)
                        ptb = PB()
                        for h in range(4):
                            S.op("pe", lambda e, h=h: e.transpose(ptb[:, h * 128:(h + 1) * 128], qs[:, h, :], identb[:, :]),
                                 [qs.b, identb.b], [ptb.b], inc=False)
                        for h in range(4):
                            S.op("pe", lambda e, h=h: e.transpose(ptb[:, 512 + h * 128:512 + (h + 1) * 128], ks[:, h, :], identb[:, :]),
                                 [ks.b, identb.b], [ptb.b], inc=(h == 3))
                        S.op("act", lambda e: e.activation(qkT[:].rearrange("p a b -> p (a b)"), ptb[:, :], AF.Identity), [ptb.b], [qkT.b])
                        pS = PF()
                        for h in range(4):
                            S.op("pe", lambda e, h=h: e.matmul(pS[:, h * 128:(h + 1) * 128], qkT[:, 4 + h, :], qkT[:, h, :], start=True, stop=True),
                                 [qkT.b], [pS.b], inc=(h == 3))
                        S.op("dve", lambda e: e.tensor_tensor(SmT[:], pS[:, :].rearrange("p (h t) -> p h t", h=4),
                                                              tri[:, :].unsqueeze(1).broadcast_to([128, 4, 128]), ALU.mult),
                             [pS.b, tri.b], [SmT.b])
                        pnum = [PL(0), PL(1)]
                        for h in range(4):
                            pn = pnum[h // 2]
                            cs = (h % 2) * 256
                            S.op("pe", lambda e, h=h, pn=pn, cs=cs: e.matmul(pn[:, cs:cs + 256], SmT[:, h, :], v1[:, h, 0:256], start=True, stop=False),
                                 [SmT.b, v1.b], [pn.b], inc=False)
                            S.op("pe", lambda e, h=h, pn=pn, cs=cs: e.matmul(pn[:, cs:cs + 256], qkT[:, h, :], CnTb[:, h, 0:256], start=False, stop=True),
                                 [qkT.b, CnTb.b], [pn.b], inc=True)
                        pden = PF()
                        for h in range(4):
                            S.op("pe", lambda e, h=h: e.matmul(pden[:, h:h + 1], SmT[:, h, :], ones_col[:, 0:1], start=True, stop=False),
                                 [SmT.b, ones_col.b], [pden.b], inc=False)
                            S.op("pe", lambda e, h=h: e.matmul(pden[:, h:h + 1], qkT[:, h, :], CnTb[:, h, 256:257], start=False, stop=True),
                                 [qkT.b, CnTb.b], [pden.b], inc=True)
                        head_ln(pnum, pden, hst, hmn, 128)
                        state_update(ks, v1, gt, 128)
                        m_update(gt, 128)
                        hm_transpose_out(hmn, 128, hmT, c0, nwT, sgog)
                phB.close()
                S.phase_end()
                if debug and pi == 0:
                    dbg["hmT"] = dout("dbg_hmT", [128, KC, NTP], BF16)
                    S.dma("sp", dbg["hmT"][:, :, :], hmT[:], reads=[hmT.b], writes=[dbg["hmT"].b])
                    dbg["hT"] = dout("dbg_hT", [128, KC, NTP], BF16)
                    S.dma("sp", dbg["hT"][:, :, :], hT[:], reads=[hT.b], writes=[dbg["hT"].b])
                    dbg["CnT"] = dout("dbg_CnT", [128, 4, 257])
                    S.dma("sp", dbg["CnT"][:, :, :], CnT[:], reads=[CnT.b], writes=[dbg["CnT"].b])
                if stop_after == "B":
                    pm_.close()
                    break
                with contextlib.ExitStack() as ph:
                    Wqa = sb("Wqa", [128, KC, 1024], BF16, ph)
                    Wkd = sb("Wkd", [128, KC, 256], BF16, ph)
                    Wkv2 = sb("Wkv2", [128, KC, 512], BF16, ph)
                    load_w(Wqa, w_in, 3080, 4104)
                    load_w(Wkd, w_in, 4104, 4360)
                    load_w(Wkv2, w_in, 4104, 4616)
                    bkv2 = bias_hilo("bkv2", b_in, 4104, 4616, ph)
                    binT = sb("binTc", [128, 32], F32, ph)
                    bqk = sb("bqk", [64, 20], F32, ph)
                    swab = sb("swab", [128, 16, 256], F32, ph)
                    fbias = sb("fbias", [128, 128], F32, ph)
                    snk = sb("snk", [128, 16], F32, ph)
                    S.dma("sp", binT[:], b_inT[:, :], writes=[binT.b])
                    S.dma("sp", bqk[:], b_qk64[:, :], writes=[bqk.b])
                    S.dma("sp", swab[:], c_swab[:, :, :], writes=[swab.b])
                    S.dma("sp", fbias[:], first_bias[:, :], writes=[fbias.b])
                    S.dma("sp", snk[:], sinks_rep[:, :], writes=[snk.b])
                    qaT = sb("qaT", [64, 16, 128], BF16, ph)
                    kdT = sb("kdT", [64, 4, 128 + NTP], BF16, ph)
                    kvA = sb("kvA", [128, NBP + 1, 512], BF16, ph)
                    kvf = sb("kvf", [128, 512], F32, ph)
                    Sb = [sb("Sb%d" % i, [128, 2, 256], F32, ph) for i in range(3)]
                    Pb = [sb("Pb%d" % i, [128, 256], BF16, ph) for i in range(6)]
                    astl = [sb("ast%d" % i, [128, 12], F32, ph) for i in range(3)]
                    PT2 = [sb("PT2_%d" % i, [128, 4, 128], BF16, ph) for i in range(2)]
                    ast = sb("ast", [128, 96], F32, ph)
                    hab = sb("hab", [128, D], BF16, ph)
                    S.op("dve", lambda e: e.tensor_copy(kdT[:, :, 0:128], kd_carry[:]), [kd_carry.b], [kdT.b])
                    S.op("dve", lambda e: e.tensor_copy(kvA[:, 0, :], kv_carry[:]), [kv_carry.b], [kvA.b])
                    nprompt = sum(b[2] for b in blocks if b[0] != "samp")
                    t0 = 0
                    while t0 < nprompt:
                        n = min(512, nprompt - t0)
                        for jj in range(4):
                            pst = PF()
                            for k in range(KC):
                                S.op("pe", lambda e, k=k, jj=jj, t0=t0, n=n, pst=pst: e.matmul(
                                    pst[0:64, 0:n], Wkd[:, k, jj * 64:(jj + 1) * 64], hT[:, k, t0:t0 + n],
                                    start=(k == 0), stop=(k == KC - 1)), [Wkd.b, hT.b], [pst.b], inc=(k == KC - 1))
                            S.op("act", lambda e, jj=jj, t0=t0, n=n, pst=pst: e.activation(
                                kdT[:, jj, 128 + t0:128 + t0 + n], pst[0:64, 0:n], AF.Identity, bias=bqk[:, 16 + jj:17 + jj], scale=1.0),
                                [pst.b, bqk.b], [kdT.b])
                        t0 += n
                    S.checkpoint("C1")
                    last_prompt_bi = max([bi for bi, b in enumerate(blocks) if b[0] != "samp"] + [-1])
                    for bi, (kind, ba, nt) in enumerate(blocks):
                        if kind == "samp":
                            continue
                        pst = PF()
                        mm_tok(pst, 0, 512, hT, offs[bi], 128, Wkv2, 0, bkv2, 0)
                        S.op("act", lambda e, bi=bi, pst=pst: e.activation(kvA[:, bi + 1, :], pst[:, :], AF.Identity), [pst.b], [kvA.b])
                        if blocks_last_prompt_pass and bi == last_prompt_bi:
                            S.op("dve", lambda e, pst=pst: e.tensor_copy(kvf[:], pst[:, :]), [pst.b], [kvf.b])
                            S.dma("sp", pkv_out[:, :], kvf[:], reads=[kvf.b], writes=[pkv_out.b])
                    S.checkpoint("C2a")
                    if nprompt > 0:
                        S.op("dve", lambda e: e.tensor_copy(kd_carry[:], kdT[:, :, nprompt:nprompt + 128]), [kdT.b], [kd_carry.b])
                        S.op("dve", lambda e: e.tensor_copy(kv_carry[:], kvA[:, last_prompt_bi + 1, :]), [kvA.b], [kv_carry.b])
                    S.checkpoint("C2")
                    for bi, (kind, ba, nt) in enumerate(blocks):
                        c0 = offs[bi]
                        if kind == "h2":
                            continue
                        if kind == "samp":
                            samp_swa(hT, c0, Wqa, Wkd, Wkv2, bkv2, bqk, snk, haT, ph)
                            continue
                        first_own = (kind == "own" and ba == NSCAN + 2)
                        for h in range(16):
                            pst = PF()
                            for k in range(KC):
                                S.op("pe", lambda e, k=k, h=h, pst=pst: e.matmul(
                                    pst[0:64, 0:128], Wqa[:, k, h * 64:(h + 1) * 64], hT[:, k, c0:c0 + 128],
                                    start=(k == 0), stop=(k == KC - 1)), [Wqa.b, hT.b], [pst.b], inc=(k == KC - 1))
                            S.op("dve", lambda e, h=h, pst=pst: e.tensor_scalar(
                                qaT[:, h, :], pst[0:64, 0:128], bqk[:, h:h + 1], None, ALU.add),
                                [pst.b, bqk.b], [qaT.b])
                        po = [PL(0), PL(1)]
                        def stage1a(hp):
                            sbt = Sb[hp % 3]
                            a_ = astl[hp % 3]
                            for hh in range(2):
                                h = 2 * hp + hh
                                jj = h // 4
                                pS = PF()
                                S.op("pe", lambda e, h=h, jj=jj, pS=pS: e.matmul(
                                    pS[:, 0:256], qaT[:, h, :], kdT[:, jj, c0:c0 + 256],
                                    start=True, stop=True), [qaT.b, kdT.b], [pS.b])
                                S.op("dve", lambda e, h=h, hh=hh, sbt=sbt, pS=pS: e.scalar_tensor_tensor(
                                    sbt[:, hh, :], pS[:, 0:256], 0.125, swab[:, h, :], ALU.mult, ALU.add), [pS.b, swab.b], [sbt.b])
                            if first_own:
                                S.op("dve", lambda e, sbt=sbt: e.tensor_tensor(
                                    sbt[:, :, 0:128], sbt[:, :, 0:128], fbias[:, :].unsqueeze(1).broadcast_to([128, 2, 128]), ALU.add),
                                    [sbt.b, fbias.b], [sbt.b])
                            S.op("dve", lambda e, sbt=sbt, a_=a_: e.reduce_max(a_[:, 0:2], sbt[:], AX.X), [sbt.b], [a_.b])
                            S.op("dve", lambda e, hp=hp, a_=a_: e.tensor_tensor(a_[:, 2:4], a_[:, 0:2], snk[:, 2 * hp:2 * hp + 2], ALU.max), [a_.b, snk.b], [a_.b])
                            S.op("dve", lambda e, a_=a_: e.tensor_scalar(a_[:, 4:6], a_[:, 2:4], -1.0, None, ALU.mult), [a_.b], [a_.b])
                            S.op("dve", lambda e, hp=hp, a_=a_: e.tensor_tensor(a_[:, 6:8], snk[:, 2 * hp:2 * hp + 2], a_[:, 2:4], ALU.subtract), [a_.b, snk.b], [a_.b])

                        def stage1b(hp):
                            sbt = Sb[hp % 3]
                            a_ = astl[hp % 3]
                            S.op("act", lambda e, hp=hp, a_=a_: e.activation(ast[:, 64 + 2 * hp:66 + 2 * hp], a_[:, 6:8], AF.Exp), [a_.b], [ast.b])
                            for hh in range(2):
                                h = 2 * hp + hh
                                pb_ = Pb[(hp % 3) * 2 + hh]
                                S.op("act", lambda e, h=h, hh=hh, sbt=sbt, pb_=pb_, a_=a_: e.activation(
                                    pb_[:], sbt[:, hh, :], AF.Exp, bias=a_[:, 4 + hh:5 + hh], scale=1.0, accum_out=ast[:, 48 + h:49 + h]),
                                    [sbt.b, a_.b], [pb_.b, ast.b])

                        def stage2(hp):
                            pt_ = PT2[hp % 2]
                            ptb = PB()
                            for hh in range(2):
                                pb_ = Pb[(hp % 3) * 2 + hh]
                                for half in range(2):
                                    q_ = hh * 2 + half
                                    S.op("pe", lambda e, half=half, pb_=pb_, ptb=ptb, q_=q_: e.transpose(
                                        ptb[:, q_ * 128:(q_ + 1) * 128], pb_[:, half * 128:(half + 1) * 128], identb[:, :]),
                                        [pb_.b, identb.b], [ptb.b], inc=(q_ == 3))
                            S.op("act", lambda e, pt_=pt_, ptb=ptb: e.activation(pt_[:].rearrange("p a b -> p (a b)"), ptb[:, 0:512], AF.Identity),
                                 [ptb.b], [pt_.b])
                            for hh in range(2):
                                h = 2 * hp + hh
                                jj = h // 4
                                pod = po[h // 8]
                                oc0 = (h % 8) * 64
                                for half in range(2):
                                    S.op("pe", lambda e, half=half, hh=hh, pt_=pt_, pod=pod, oc0=oc0, jj=jj: e.matmul(
                                        pod[:, oc0:oc0 + 64], pt_[:, hh * 2 + half, :], kvA[:, bi + half, 256 + jj * 64:256 + (jj + 1) * 64],
                                        start=(half == 0), stop=(half == 1)), [pt_.b, kvA.b], [pod.b], inc=(half == 1))

                        stage1a(0)
                        stage1b(0)
                        stage1a(1)
                        for hp in range(8):
                            if hp + 2 < 8:
                                stage1a(hp + 2)
                            stage2(hp)
                            if hp + 1 < 8:
                                stage1b(hp + 1)
                        S.op("dve", lambda e: e.tensor_tensor(ast[:, 80:96], ast[:, 48:64], ast[:, 64:80], ALU.add), [ast.b], [ast.b])
                        S.op("dve", lambda e: e.reciprocal(ast[:, 80:96], ast[:, 80:96]), [ast.b], [ast.b])
                        for g in range(2):
                            S.op("dve", lambda e, g=g: e.tensor_tensor(
                                hab[:, g * 512:(g + 1) * 512].rearrange("p (h d) -> p h d", h=8), po[g][:, :].rearrange("p (h d) -> p h d", h=8),
                                ast[:, 80 + 8 * g:88 + 8 * g].unsqueeze(2).broadcast_to([128, 8, 64]), ALU.mult), [po[g].b, ast.b], [hab.b])
                        ptb = PB()
                        for k in range(KC):
                            S.op("pe", lambda e, k=k, ptb=ptb: e.transpose(ptb[:, k * 128:(k + 1) * 128], hab[:, k * 128:(k + 1) * 128], identb[:, :]),
                                 [hab.b, identb.b], [ptb.b], inc=(k == KC - 1))
                        S.op("act", lambda e, ptb=ptb: e.activation(haT[:, :, c0:c0 + 128], ptb[:, :].rearrange("p (k t) -> p k t", k=KC), AF.Identity),
                             [ptb.b], [haT.b])
                    if debug and pi == 0 and stop_after == "C":
                        for nm, tt, shp, dt_ in [("qaT", qaT, [64, 16, 128], BF16), ("kdT", kdT, [64, 4, 128 + NTP], BF16),
                                                 ("kvA", kvA, [128, NBP + 1, 512], BF16), ("ast", ast, [128, 96], F32),
                                                 ("Sb0", Sb[0], [128, 2, 256], F32), ("Sb1", Sb[1], [128, 2, 256], F32), ("hab", hab, [128, D], BF16)]:
                            dbg[nm] = dout("dbg_" + nm, shp, dt_)
                            S.dma("sp", dbg[nm][:], tt[:], reads=[tt.b], writes=[dbg[nm].b])
                S.phase_end()
                if debug and pi == 0:
                    dbg["haT"] = dout("dbg_haT", [128, KC, NTP], BF16)
                    S.dma("sp", dbg["haT"][:, :, :], haT[:], reads=[haT.b], writes=[dbg["haT"].b])
                if stop_after == "C":
                    pm_.close()
                    break

                rot["n"] = 6
                first_tok = offs[1] if blocks[0][0] == "h2" else 0
                x1rows = {}
                with contextlib.ExitStack() as ph:
                    Wgm = sb("Wgm", [128, KC, 1024], BF16, ph)
                    Wga = sb("Wga", [128, KC, 1024], BF16, ph)
                    Wbm = sb("Wbm", [128, KC, 1024], BF16, ph)
                    Wba = sb("Wba", [128, KC, 1024], BF16, ph)
                    Wout = sb("Wout", [128, KC, 1024], BF16, ph)
                    load_w(Wgm, w_in, 4616, 5640)
                    load_w(Wga, w_in, 5640, 6664)
                    load_w(Wbm, w_bm, 0, 1024)
                    load_w(Wba, w_ba, 0, 1024)
                    load_w(Wout, w_out, 0, 1024)
                    binT = sb("binTd", [128, 32], F32, ph)
                    S.dma("sp", binT[:], b_inT[:, :], writes=[binT.b])
                    l1g = sb("l1g", [128, D], F32, ph)
                    l1b = sb("l1b", [128, D], F32, ph)
                    S.dma("sp", l1g[:], ln1_g[0:1, :].broadcast_to([128, D]), writes=[l1g.b])
                    S.dma("sp", l1b[:], ln1_b[0:1, :].broadcast_to([128, D]), writes=[l1b.b])
                    mgT = sb("mgT", [128, KC, 512], BF16, ph)
                    sgm = sb("sgm", [128, 512], F32, ph)
                    sga = sb("sga", [128, 512], F32, ph)
                    t1 = sgm
                    t2 = sga
                    pre = xn
                    x1t = sb("x1t", [128, D], F32, ph)
                    t0 = first_tok
                    while t0 < NTP:
                        n = min(512, NTP - t0)
                        for oc in range(KC):
                            for (Wg, Wb, src, sg, tt, bofs) in [(Wgm, Wbm, hmT, sgm, t1, 16), (Wga, Wba, haT, sga, t2, 24)]:
                                pg_ = PF()
                                for k in range(KC):
                                    S.op("pe", lambda e, k=k, Wg=Wg, pg_=pg_: e.matmul(pg_[:, 0:n], Wg[:, k, oc * 128:(oc + 1) * 128], hT[:, k, t0:t0 + n],
                                                                                       start=(k == 0), stop=(k == KC - 1)), [Wg.b, hT.b], [pg_.b], inc=(k == KC - 1))
                                S.op("act", lambda e, sg=sg, pg_=pg_, bofs=bofs: e.activation(sg[:, 0:n], pg_[:, 0:n], AF.Sigmoid,
                                                                                               bias=binT[:, bofs + oc:bofs + oc + 1], scale=1.0),
                                     [pg_.b, binT.b], [sg.b])
                                pb2 = PF()
                                for k in range(KC):
                                    S.op("pe", lambda e, k=k, Wb=Wb, src=src, pb2=pb2: e.matmul(pb2[:, 0:n], Wb[:, k, oc * 128:(oc + 1) * 128], src[:, k, t0:t0 + n],
                                                                                                start=(k == 0), stop=(k == KC - 1)), [Wb.b, src.b], [pb2.b], inc=(k == KC - 1))
                                S.op("dve", lambda e, sg=sg, tt=tt, pb2=pb2: e.tensor_tensor(tt[:, 0:n], pb2[:, 0:n], sg[:, 0:n], ALU.mult), [pb2.b, sg.b], [tt.b])
                            S.op("pool", lambda e, oc=oc: e.tensor_tensor(mgT[:, oc, 0:n], t1[:, 0:n], t2[:, 0:n], ALU.add), [t1.b, t2.b], [mgT.b])
                        for bi, (kind, ba, nt) in enumerate(blocks):
                            c0 = offs[bi]
                            if c0 < t0 or c0 >= t0 + n:
                                continue
                            samp = kind == "samp"
                            xt = XT()
                            if samp:
                                S.dma("sp", xt[:nt, :], xsamp[:, :], writes=[xt.b])
                            else:
                                S.dma("sp", xt[:], xs[ba * 128:(ba + 1) * 128, :], writes=[xt.b])
                            g1 = SAMP["gS"] if samp else gP
                            for half in range(2):
                                py = PF()
                                mm_tok(py, 0, 512, mgT, c0 - t0, nt, Wout, half * 512, None, 0)
                                S.op("dve", lambda e, py=py, half=half, g1=g1: e.tensor_tensor(
                                    pre[:nt, half * 512:(half + 1) * 512], py[:nt, :], g1[:nt, 0, half * 512:(half + 1) * 512], ALU.mult),
                                    [py.b, g1.b], [pre.b])
                            S.op("dve", lambda e, xt=xt: e.scalar_tensor_tensor(pre[:nt, :], xt[:nt, :], ALPHA, pre[:nt, :], ALU.mult, ALU.add),
                                 [xt.b, pre.b], [pre.b])
                            rstd, nmr = ln_stats(pre, nt, lnt)
                            S.op("act", lambda e, rstd=rstd, nmr=nmr: e.activation(x1t[:nt, :], pre[:nt, :], AF.Identity, bias=nmr, scale=rstd),
                                 [pre.b, lnt.b], [x1t.b])
                            S.op("dve", lambda e: e.tensor_tensor(x1t[:nt, :], x1t[:nt, :], l1g[:nt, :], ALU.mult), [x1t.b, l1g.b], [x1t.b])
                            S.op("pool", lambda e: e.tensor_tensor(x1t[:nt, :], x1t[:nt, :], l1b[:nt, :], ALU.add), [x1t.b, l1b.b], [x1t.b])
                            r0 = x1row["n"]
                            x1row["n"] += nt
                            x1rows[bi] = r0
                            S.dma("sp", x1_scr[r0:r0 + nt, :], x1t[:nt, :], reads=[x1t.b], writes=[x1_scr.b])
                            ln_to_hT(x1t, nt, hT, c0, 2, samp, xn, lnt)
                        t0 += n
                S.phase_end()
                if debug and pi == 0:
                    dbg["h2T"] = dout("dbg_h2T", [128, KC, NTP], BF16)
                    S.dma("sp", dbg["h2T"][:, :, :], hT[:], reads=[hT.b], writes=[dbg["h2T"].b])
                pm_.close()
                S.phase_end()
                if stop_after == "D":
                    break

                nprompt = sum(b[2] for b in blocks if b[0] != "samp")
                tiles = []
                t0 = first_tok
                while t0 < nprompt:
                    n = min(512, nprompt - t0)
                    tiles.append((t0, n, False))
                    t0 += n
                if any(b[0] == "samp" for b in blocks):
                    tiles.append((nprompt, NTS, True))
                with contextlib.ExitStack() as pe_:
                    actT = sb("actT", [128, FC, NTP], BF16, pe_)
                    Wdn = sb("Wdn", [128, FC, D], BF16, pe_)
                    load_w(Wdn, w_down, 0, D, rows=DFF)
                    with contextlib.ExitStack() as ph:
                        Wupc = [sb("Wupc%d" % i, [128, KC, 256], BF16, ph) for i in range(3)]
                        bupT = sb("bupT", [128, 2 * FC], F32, ph)
                        cwT = sb("cwT", [128, 2 * FC, 3], F32, ph)
                        cbT = sb("cbT", [128, 2 * FC], F32, ph)
                        S.dma("sp", bupT[:], b_upT[:, :], writes=[bupT.b])
                        S.dma("sp", cwT[:], conv_wT[:, :, :], writes=[cwT.b])
                        S.dma("sp", cbT[:], conv_bT[:, :], writes=[cbT.b])
                        ue = [sb("ue%d" % i, [128, 520], F32, ph) for i in range(4)]
                        yc = [sb("yc%d" % i, [128, 512], F32, ph) for i in range(4)]
                        eit = {"n": 0}
                        if cfg.sample:
                            cst = sb("cst", [128, 2 * FC, 32], F32, ph)
                            S.dma("sp", cst[:], convst[:, :, :], writes=[cst.b])
                            cso = sb("cso", [128, 2 * FC, 32], F32, ph)
                        S.dma("pool", Wupc[0][:].rearrange("p k c -> p (k c)"), w_upr[0, :, :], writes=[Wupc[0].b])
                        for c in range(FC):
                            Wc = Wupc[c % 3]
                            if c + 1 < FC:
                                Wn_ = Wupc[(c + 1) % 3]
                                S.dma("pool", Wn_[:].rearrange("p k c -> p (k c)"), w_upr[c + 1, :, :], writes=[Wn_.b])
                            for (t0, n, samp) in tiles:
                                has_h1 = (not samp) and pi == 0 and t0 == first_tok
                                eit["n"] += 1
                                eb = (eit["n"] % 2) * 2
                                for part in range(2):
                                    ci = part * FC + c
                                    u_ = ue[eb + part]
                                    y_ = yc[eb + part]
                                    pu = PF()
                                    for k in range(KC):
                                        S.op("pe", lambda e, k=k, part=part, pu=pu, Wc=Wc, t0=t0, n=n: e.matmul(
                                            pu[:, 0:n], Wc[:, k, part * 128:(part + 1) * 128], hT[:, k, t0:t0 + n],
                                            start=(k == 0), stop=(k == KC - 1)), [Wc.b, hT.b], [pu.b], inc=(k == KC - 1))
                                    if not samp:
                                        S.op("pool", lambda e, u_=u_, ci=ci: e.tensor_copy(u_[:, 0:2], cv_carry[:, ci, :]), [cv_carry.b], [u_.b])
                                        S.op("act", lambda e, u_=u_, ci=ci, pu=pu, n=n: e.activation(u_[:, 2:2 + n], pu[:, 0:n], AF.Identity, bias=bupT[:, ci:ci + 1], scale=1.0),
                                             [pu.b, bupT.b], [u_.b])
                                        if has_h1:
                                            S.op("dve", lambda e, u_=u_: e.tensor_scalar(u_[:, 2:130], u_[:, 2:130], cvv[:, 0:1], None, ALU.mult), [u_.b, cvv.b], [u_.b])
                                        S.op("pool", lambda e, u_=u_, ci=ci, n=n: e.tensor_copy(cv_carry[:, ci, :], u_[:, n:n + 2]), [u_.b], [cv_carry.b])
                                        S.op("dve", lambda e, u_=u_, y_=y_, ci=ci, n=n: e.tensor_scalar(y_[:, 0:n], u_[:, 0:n], cwT[:, ci, 0:1], cbT[:, ci:ci + 1], ALU.mult, ALU.add),
                                             [u_.b, cwT.b, cbT.b], [y_.b])
                                        S.op("dve", lambda e, u_=u_, y_=y_, ci=ci, n=n: e.scalar_tensor_tensor(y_[:, 0:n], u_[:, 1:n + 1], cwT[:, ci, 1:2], y_[:, 0:n], ALU.mult, ALU.add),
                                             [u_.b, cwT.b], [y_.b])
                                        S.op("dve", lambda e, u_=u_, y_=y_, ci=ci, n=n: e.scalar_tensor_tensor(y_[:, 0:n], u_[:, 2:n + 2], cwT[:, ci, 2:3], y_[:, 0:n], ALU.mult, ALU.add),
                                             [u_.b, cwT.b], [y_.b])
                                    else:
                                        u3 = u_[:, 0:96].rearrange("p (b t) -> p b t", t=6)
                                        y3 = y_[:, 0:64].rearrange("p (b t) -> p b t", t=4)
                                        S.op("pool", lambda e, u3=u3, ci=ci: e.tensor_copy(u3[:, :, 0:2], cst[:, ci, :].rearrange("p (b t) -> p b t", t=2)), [cst.b], [u_.b])
                                        S.op("act", lambda e, u3=u3, ci=ci, pu=pu: e.activation(u3[:, :, 2:6], pu[:, 0:64].rearrange("p (b t) -> p b t", t=4), AF.Identity,
                                                                                                bias=bupT[:, ci:ci + 1], scale=1.0), [pu.b, bupT.b], [u_.b])
                                        S.op("pool", lambda e, u3=u3, ci=ci: e.tensor_copy(cso[:, ci, :].rearrange("p (b t) -> p b t", t=2), u3[:, :, 4:6]), [u_.b], [cso.b])
                                        S.op("dve", lambda e, u3=u3, y3=y3, ci=ci: e.tensor_scalar(y3, u3[:, :, 0:4], cwT[:, ci, 0:1], cbT[:, ci:ci + 1], ALU.mult, ALU.add),
                                             [u_.b, cwT.b, cbT.b], [y_.b])
                                        S.op("dve", lambda e, u3=u3, y3=y3, ci=ci: e.scalar_tensor_tensor(y3, u3[:, :, 1:5], cwT[:, ci, 1:2], y3, ALU.mult, ALU.add),
                                             [u_.b, cwT.b], [y_.b])
                                        S.op("dve", lambda e, u3=u3, y3=y3, ci=ci: e.scalar_tensor_tensor(y3, u3[:, :, 2:6], cwT[:, ci, 2:3], y3, ALU.mult, ALU.add),
                                             [u_.b, cwT.b], [y_.b])
                                ya_, yg_ = yc[eb], yc[eb + 1]
                                S.op("act", lambda e, n=n, ya_=ya_: e.activation(ya_[:, 0:n], ya_[:, 0:n], AF.Gelu_apprx_tanh), [ya_.b], [ya_.b])
                                S.op("dve", lambda e, c=c, t0=t0, n=n, ya_=ya_, yg_=yg_: e.tensor_tensor(actT[:, c, t0:t0 + n], ya_[:, 0:n], yg_[:, 0:n], ALU.mult),
                                     [ya_.b, yg_.b], [actT.b])
                        if pi == len(passes) - 1:
                            S.dma("sp", pconv_out[:, :, :], cv_carry[:], reads=[cv_carry.b], writes=[pconv_out.b])
                            if cfg.sample:
                                S.dma("sp", sconv_out[:, :, :], cso[:], reads=[cso.b], writes=[sconv_out.b])
                    S.phase_end()
                    with contextlib.ExitStack() as ph:
                        bdn = bias_hilo("bdn", b_down, 0, D, ph)
                        l2g = sb("l2g", [128, D], F32, ph)
                        l2b = sb("l2b", [128, D], F32, ph)
                        S.dma("sp", l2g[:], ln2_g[0:1, :].broadcast_to([128, D]), writes=[l2g.b])
                        S.dma("sp", l2b[:], ln2_b[0:1, :].broadcast_to([128, D]), writes=[l2b.b])
                        pre = sb("pre2", [128, D], F32, ph)
                        yt = sb("yt", [128, D], F32, ph)
                        for bi, (kind, ba, nt) in enumerate(blocks):
                            c0 = offs[bi]
                            if kind in ("h1", "h2"):
                                continue
                            samp = kind == "samp"
                            g2 = SAMP["gS"] if samp else gP
                            xt = XT()
                            r0 = x1rows[bi]
                            S.dma("sp", xt[:nt, :], x1_scr[r0:r0 + nt, :], reads=[x1_scr.b], writes=[xt.b])
                            for half in range(2):
                                pf_ = PF()
                                for k in range(FC):
                                    S.op("pe", lambda e, k=k, pf_=pf_, half=half: e.matmul(pf_[:nt, :], actT[:, k, c0:c0 + nt], Wdn[:, k, half * 512:(half + 1) * 512],
                                                                                           start=(k == 0), stop=False), [actT.b, Wdn.b], [pf_.b], inc=False)
                                S.op("pe", lambda e, pf_=pf_, half=half: e.matmul(pf_[:nt, :], onesb[0:64, 0:nt], bdn[:, half * 512:(half + 1) * 512], start=False, stop=True),
                                     [bdn.b, onesb.b], [pf_.b])
                                S.op("dve", lambda e, pf_=pf_, half=half, g2=g2: e.tensor_tensor(
                                    pre[:nt, half * 512:(half + 1) * 512], pf_[:nt, :], g2[:nt, 1, half * 512:(half + 1) * 512], ALU.mult), [pf_.b, g2.b], [pre.b])
                            S.op("dve", lambda e, xt=xt: e.scalar_tensor_tensor(pre[:nt, :], xt[:nt, :], ALPHA, pre[:nt, :], ALU.mult, ALU.add), [xt.b, pre.b], [pre.b])
                            rstd, nmr = ln_stats(pre, nt, lnt)
                            S.op("act", lambda e, rstd=rstd, nmr=nmr: e.activation(yt[:nt, :], pre[:nt, :], AF.Identity, bias=nmr, scale=rstd), [pre.b, lnt.b], [yt.b])
                            S.op("dve", lambda e: e.tensor_tensor(yt[:nt, :], yt[:nt, :], l2g[:nt, :], ALU.mult), [yt.b, l2g.b], [yt.b])
                            S.op("pool", lambda e: e.tensor_tensor(yt[:nt, :], yt[:nt, :], l2b[:nt, :], ALU.add), [yt.b, l2b.b], [yt.b])
                            if samp:
                                S.dma("sp", ys_out[:, :], yt[:nt, :], reads=[yt.b], writes=[ys_out.b])
                            else:
                                ob = ba - (NSCAN + 2)
                                S.dma("sp", y_out[ob * 128:(ob + 1) * 128, :], yt[:, :], reads=[yt.b], writes=[y_out.b])
                    S.phase_end()
            S.phase_end()

        pmd = sb("pmd", [4, 4])
        pen = sb("pen", [128, 4])
        pCs = sb("pCs", [128, 4, 257])
        S.op("dve", lambda e: e.tensor_scalar(pmd[:], ident[0:4, 0:4], mst[:, 0:1], None, ALU.mult), [ident.b, mst.b], [pmd.b])
        pst = PF()
        S.op("pe", lambda e: e.matmul(pst[:, 0:4], ones[0:4, 0:128], pmd[:, :], start=True, stop=True), [ones.b, pmd.b], [pst.b])
        S.op("act", lambda e: e.activation(pen[:], pst[:, 0:4], AF.Exp, scale=-1.0), [pst.b], [pen.b])
        for h in range(4):
            S.op("dve", lambda e, h=h: e.tensor_scalar(pCs[:, h, :], CnT[:, h, :], pen[:, h:h + 1], None, ALU.mult), [CnT.b, pen.b], [pCs.b])
        S.dma("sp", pC_out[:, :, :], pCs[:], reads=[pCs.b], writes=[pC_out.b])
        S.dma("sp", pm_out[:, :], mst[:], reads=[mst.b], writes=[pm_out.b])
        outs = [y_out, pC_out, pm_out, pkv_out, pconv_out]
        if cfg.sample:
            outs += [ys_out, sconv_out, sC_out, sm_out, sk_out, sv_out]

    except _StopBuild:
        outs = []

    return nc, S, st, locals()


def _fm(vec, nch):
    return np.ascontiguousarray(np.asarray(vec, np.float32).reshape(nch, 128).T)


def alibi_slopes():
    return np.exp2(-8.0 * np.arange(1, 17, dtype=np.float32) / 16).astype(np.float32)


def prep_shared(cfg, inp):
    sh = {}
    f = lambda a: np.ascontiguousarray(np.asarray(a, np.float32))
    b_in = f(inp["b_in"][0])
    w_in = f(inp["w_in"][0])
    sh["w_ada"] = f(inp["w_ada"][0]); sh["b_ada"] = f(inp["b_ada"]); sh["b_adaT"] = _fm(inp["b_ada"][0], 48)
    sh["w_in"] = w_in; sh["b_in"] = f(inp["b_in"])
    sh["b_inT"] = np.concatenate([_fm(b_in[2056:3080], 8), _fm(b_in[3080:4104], 8),
                                  _fm(b_in[4616:5640], 8), _fm(b_in[5640:6664], 8)], axis=1)
    sh["b_qk64"] = np.ascontiguousarray(np.concatenate([b_in[3080:4104].reshape(16, 64).T, b_in[4104:4360].reshape(4, 64).T], axis=1))
    sh["norm_wT"] = _fm(inp["mlstm_norm_w"][0], 8)
    sh["sinks_rep"] = np.ascontiguousarray(np.broadcast_to(f(inp["attn_sinks"][0])[None, :], (128, 16)))
    sh["w_bm"] = f(inp["w_branch_m"][0]); sh["w_ba"] = f(inp["w_branch_a"][0]); sh["w_out"] = f(inp["w_out"][0])
    sh["ln1_g"] = f(inp["ln1_g"]); sh["ln1_b"] = f(inp["ln1_b"]); sh["ln2_g"] = f(inp["ln2_g"]); sh["ln2_b"] = f(inp["ln2_b"])
    wu = f(inp["w_up"][0]).reshape(KC, 128, 2, FC, 128)
    sh["w_upr"] = np.ascontiguousarray(wu.transpose(3, 1, 0, 2, 4).reshape(FC, 128, KC * 256))
    sh["b_upT"] = _fm(inp["b_up"][0], 44)
    cw = f(inp["conv_w"][0])
    sh["conv_wT"] = np.ascontiguousarray(cw.reshape(3, 44, 128).transpose(2, 1, 0))
    sh["conv_bT"] = _fm(inp["conv_b"][0], 44)
    sh["w_down"] = f(inp["w_down"][0]); sh["b_down"] = f(inp["b_down"])
    sh["c_ident"] = np.eye(128, dtype=np.float32)
    sh["c_tri"] = np.triu(np.ones((128, 128), np.float32))
    t = np.arange(128)[:, None]; s_ = np.arange(256)[None, :]
    delta = (t + 128 - s_).astype(np.float32)
    ok = (delta >= 0) & (delta < 128)
    sl = alibi_slopes()
    sw = np.where(ok[:, None, :], -sl[None, :, None] * delta[:, None, :], NEG).astype(np.float32)
    sh["c_swab"] = np.ascontiguousarray(sw)
    if cfg.sample:
        tok = np.arange(NTS)
        sq = tok // 4
        ii = tok % 4
        same = (sq[:, None] == sq[None, :])
        sh["c_tri_s"] = (same & (tok[:, None] <= tok[None, :])).astype(np.float32)
        sh["c_ones_s"] = same.astype(np.float32)
        mT = (sq[:, None] == np.arange(NSEQ_S)[None, :]).astype(np.float32)
        sh["c_seqmaskT"] = np.ascontiguousarray(mT)
        sh["c_seqmask"] = np.ascontiguousarray(np.broadcast_to(mT.T[None, :, :], (128, NSEQ_S, NTS)))
        pk = np.zeros((NTS, 128), np.float32); pk[ii == 0, :] = 1.0
        sh["c_pick"] = pk
        sb_ = np.full((NTS, 16, 192), NEG, np.float32)
        s_c = np.arange(128)[None, :]
        dl = (128 + ii[:, None] - s_c).astype(np.float32)
        okc = (dl >= 0) & (dl < 128)
        sb_[:, :, 0:128] = np.where(okc[:, None, :], -sl[None, :, None] * dl[:, None, :], NEG)
        dn = (ii[:, None] - ii[None, :]).astype(np.float32)
        okn = same & (dn >= 0)
        sb_[:, :, 128:192] = np.where(okn[:, None, :], -sl[None, :, None] * dn[:, None, :], NEG)
        sh["c_sbias"] = np.ascontiguousarray(sb_)
    return sh


def prep_core(cfg, inp, c):
    seq, j = c // 4, c % 4
    SEG, NSCAN, NOWN, NB_ALL = cfg.SEG, cfg.NSCAN, cfg.NOWN, cfg.NB_ALL
    P = j * SEG
    x = np.asarray(inp["x_prompt"], np.float32)[seq]
    npre = (NSCAN + 2) * 128
    xs = np.zeros((NB_ALL * 128, D), np.float32)
    valid = np.zeros((NB_ALL * 128,), np.float32)
    if P > 0:
        xs[npre - P:npre] = x[0:P]; valid[npre - P:npre] = 1.0
    xs[npre:] = x[P:P + SEG]; valid[npre:] = 1.0
    m = {"xs": xs}
    tokc = np.zeros((128, NB_ALL, 2), np.float32)
    tokc[:, :, 0] = valid.reshape(NB_ALL, 128).T
    tokc[:, :, 1] = (tokc[:, :, 0] - 1.0) * 30000.0
    m["tokc"] = tokc
    cp = np.asarray(inp["c_prompt"], np.float32)[seq]
    cs = np.asarray(inp["c_sample"], np.float32)[c * NSEQ_S:(c + 1) * NSEQ_S]
    cs_tok = np.repeat(cs, 4, axis=0)
    ctok = np.concatenate([cp[None, :], cs_tok], axis=0)
    m["cTtok"] = np.ascontiguousarray(ctok.reshape(65, KC, 128).transpose(2, 1, 0))
    crep = np.concatenate([np.repeat(cp[None, :], 128, axis=0), cs_tok], axis=0)
    m["cTrep"] = np.ascontiguousarray(crep.reshape(192, KC, 128).transpose(2, 1, 0))
    m["xsamp"] = np.ascontiguousarray(np.asarray(inp["x_sample"], np.float32)[c * NSEQ_S:(c + 1) * NSEQ_S].reshape(NTS, D))
    cs_ = np.asarray(inp["state_ffn_conv"], np.float32)[0, c * NSEQ_S:(c + 1) * NSEQ_S]
    m["convst"] = np.ascontiguousarray(cs_.reshape(NSEQ_S, 2, 2 * FC, 128).transpose(3, 2, 0, 1).reshape(128, 2 * FC, 32))
    if cfg.sample:
        b0, b1 = c * NSEQ_S, (c + 1) * NSEQ_S
        C0 = np.asarray(inp["state_mlstm_C"], np.float32)[0, b0:b1]
        n0 = np.asarray(inp["state_mlstm_n"], np.float32)[0, b0:b1]
        m0 = np.asarray(inp["state_mlstm_m"], np.float32)[0, b0:b1]
        cn = np.concatenate([C0.transpose(0, 3, 1, 2), n0.transpose(0, 2, 1)[:, :, :, None]], axis=3)
        m["sC0"] = np.ascontiguousarray(cn.reshape(NSEQ_S, 128, 4 * 257))
        m["sm0rep"] = np.ascontiguousarray(np.broadcast_to(m0.reshape(1, 64), (128, 64)))
        m["sm0T"] = np.ascontiguousarray(m0.T)
        ck = np.asarray(inp["cache_k_win"], np.float32)[0, b0:b1]
        cv = np.asarray(inp["cache_v_win"], np.float32)[0, b0:b1]
        ckt = ck.transpose(3, 0, 2, 1)
        m["ckT"] = np.ascontiguousarray(ckt)
        m["cvn"] = np.ascontiguousarray(cv.reshape(NSEQ_S, 128, 256).transpose(1, 0, 2))
        m["ck_nat"] = np.ascontiguousarray(ck.reshape(NSEQ_S, 128, 256))
        m["cv_nat"] = np.ascontiguousarray(cv.reshape(NSEQ_S, 128, 256))
    m["first_bias"] = np.full((128, 128), NEG if j == 0 else 0.0, np.float32)
    m["convvalid"] = np.full((128, 1), 0.0 if j == 0 else 1.0, np.float32)
    return m


_CACHE = {}


def _get_program(seq):
    if seq not in _CACHE:
        cfg = Cfg(seq, sample=True)
        nc, S, st, L = build(cfg, debug=False)
        S.finish([v.b for v in L["outs"]])
        st.close()
        _CACHE[seq] = (cfg, nc)
    return _CACHE[seq]


def kernel(**inputs):
    inp = {k: np.asarray(v) for k, v in inputs.items()}
    seq = inp["x_prompt"].shape[1]
    cfg, nc = _get_program(seq)
    sh = prep_shared(cfg, inp)
    maps = []
    for c in range(8):
        m = dict(sh)
        m.update(prep_core(cfg, inp, c))
        maps.append(m)
    res = run_bass_kernel_spmd(nc, maps, core_ids=list(range(8))).results
    SEG = cfg.SEG
    f32 = np.float32
    yp = np.zeros((2, seq, D), f32)
    ys = np.zeros((128, 4, D), f32)
    pC = np.zeros((1, 2, 4, 256, 128), f32); pn = np.zeros((1, 2, 4, 128), f32); pm = np.zeros((1, 2, 4), f32)
    pk = np.zeros((1, 2, 128, 4, 64), f32); pv = np.zeros((1, 2, 128, 4, 64), f32); pcv = np.zeros((1, 2, 2, 2 * DFF), f32)
    sC = np.zeros((1, 128, 4, 256, 128), f32); sn = np.zeros((1, 128, 4, 128), f32); sm = np.zeros((1, 128, 4), f32)
    sk = np.zeros((1, 128, 128, 4, 64), f32); sv = np.zeros((1, 128, 128, 4, 64), f32); scv = np.zeros((1, 128, 2, 2 * DFF), f32)
    for c in range(8):
        r = res[c]
        s_, j = c // 4, c % 4
        yp[s_, j * SEG:(j + 1) * SEG] = np.asarray(r["y_out"], f32)
        b0, b1 = c * NSEQ_S, (c + 1) * NSEQ_S
        ys[b0:b1] = np.asarray(r["ys_out"], f32).reshape(NSEQ_S, 4, D)
        if j == 3:
            pc = np.asarray(r["pC_out"], f32)
            pC[0, s_] = pc[:, :, :256].transpose(1, 2, 0)
            pn[0, s_] = pc[:, :, 256].T
            pm[0, s_] = np.asarray(r["pm_out"], f32)[:, 0]
            kv = np.asarray(r["pkv_out"], f32)
            pk[0, s_] = kv[:, :256].reshape(128, 4, 64)
            pv[0, s_] = kv[:, 256:].reshape(128, 4, 64)
            pcv[0, s_] = np.asarray(r["pconv_out"], f32).transpose(2, 1, 0).reshape(2, 2 * DFF)
        sc = np.asarray(r["sC_out"], f32).reshape(NSEQ_S, 128, 4, 257)
        sC[0, b0:b1] = sc[:, :, :, :256].transpose(0, 2, 3, 1)
        sn[0, b0:b1] = sc[:, :, :, 256].transpose(0, 2, 1)
        sm[0, b0:b1] = np.asarray(r["sm_out"], f32).T
        sk[0, b0:b1] = np.asarray(r["sk_out"], f32).reshape(NSEQ_S, 128, 4, 64)
        sv[0, b0:b1] = np.asarray(r["sv_out"], f32).reshape(NSEQ_S, 128, 4, 64)
        scv[0, b0:b1] = np.asarray(r["sconv_out"], f32).reshape(128, 2 * FC, NSEQ_S, 2).transpose(2, 3, 1, 0).reshape(NSEQ_S, 2, 2 * DFF)
    return (yp, ys, pC, pn, pm, pk, pv, pcv, sC, sn, sm, sk, sv, scv)
```

```python
import contextlib
import numpy as np
import ml_dtypes
import concourse.bass as bass
import concourse.mybir as mybir
from concourse.bass_utils import run_bass_kernel_spmd

F32 = mybir.dt.float32
BF16 = mybir.dt.bfloat16
AF = mybir.ActivationFunctionType
ALU = mybir.AluOpType
AX = mybir.AxisListType

D = 1024
KC = 8
DIN = 6664
DFF = 2816
FC = 22
NEG = -30000.0
LN_EPS = 1e-5
ALPHA = 2.0 ** 0.25
NSEQ_S = 16
NTS = 64


class Buf:
    __slots__ = ("w", "r", "name", "excl")

    def __init__(self, name="", init=None):
        self.w = {}
        self.r = dict(init) if init else {}
        self.name = name
        self.excl = False


class _StopBuild(Exception):
    pass


class Sched:
    stop_at = None

    def checkpoint(self, name):
        if Sched.stop_at is not None and name == Sched.stop_at:
            self.stopped = True

    def __init__(self, nc, stack, n_dma=40):
        self.nc = nc
        self.eng = dict(pe=nc.tensor, act=nc.scalar, dve=nc.vector, pool=nc.gpsimd, sp=nc.sync)
        self.sem = {}
        self.cnt = {}
        self.waited = {k: {} for k in self.eng}
        for k in self.eng:
            self.sem[k] = stack.enter_context(nc.semaphore("e_" + k))
            self.cnt[k] = 0
        self.dsem = [stack.enter_context(nc.semaphore("d%d" % i)) for i in range(n_dma)]
        self.dcnt = [0] * n_dma
        self.dpool = {"sp": list(range(0, n_dma - 12)), "pool": list(range(n_dma - 12, n_dma))}
        self.dnext = {"sp": 0, "pool": 0}
        self.pending_pe = False
        self.ninst = 0
        self.epoch = {}
        self.stopped = False
        self.marks = []
        self.npe = 0
        self.log = {k: [] for k in self.eng}

    def check_deadlock(self):
        sem = {}
        pc = {k: 0 for k in self.log}
        progress = True
        while progress:
            progress = False
            for e, lg in self.log.items():
                while pc[e] < len(lg):
                    kind, key, val = lg[pc[e]]
                    if kind == "w":
                        if sem.get(key, 0) < val:
                            break
                    else:
                        sem[key] = sem.get(key, 0) + val
                    pc[e] += 1
                    progress = True
        stuck = {e: (pc[e], self.log[e][pc[e]]) for e in self.log if pc[e] < len(self.log[e])}
        return stuck

    def phase_end(self):
        self.marks.append(self.npe)
        self._phase_end()

    def _phase_end(self):
        ep = {k: v for k, v in self.cnt.items() if v > 0 and k != "sp"}
        for i, v in enumerate(self.dcnt):
            if v > 0:
                ep[i] = v
        self.epoch = ep

    def _wait(self, e, key, val):
        if val <= 0:
            return
        w = self.waited[e]
        if w.get(key, 0) >= val:
            return
        w[key] = val
        semh = self.sem[key] if isinstance(key, str) else self.dsem[key]
        self.eng[e].wait_ge(semh, val)
        self.log[e].append(("w", key, val))

    def _deps(self, e, reads, writes):
        need = {}
        for b in reads:
            for k, v in b.w.items():
                if need.get(k, 0) < v:
                    need[k] = v
            if b.excl:
                for k, v in b.r.items():
                    if k != e and need.get(k, 0) < v:
                        need[k] = v
        for b in writes:
            for k, v in b.w.items():
                if need.get(k, 0) < v:
                    need[k] = v
            for k, v in b.r.items():
                if need.get(k, 0) < v:
                    need[k] = v
        for k, v in need.items():
            if k == e and e == "pe":
                continue
            self._wait(e, k, v)

    def _mark(self, tok, reads, writes):
        k, v = tok
        for b in reads:
            if b.r.get(k, 0) < v:
                b.r[k] = v
        for b in writes:
            b.w = {k: v}
            b.r = {}

    def _collect(self, e, reads, writes):
        need = {}
        for b in reads:
            for k, v in b.w.items():
                if need.get(k, 0) < v:
                    need[k] = v
            if b.excl:
                for k, v in b.r.items():
                    if k != e and need.get(k, 0) < v:
                        need[k] = v
        for b in writes:
            for k, v in b.w.items():
                if need.get(k, 0) < v:
                    need[k] = v
            for k, v in b.r.items():
                if need.get(k, 0) < v:
                    need[k] = v
        out = []
        w = self.waited[e]
        for k, v in need.items():
            if k == e and e == "pe":
                continue
            if v <= 0 or w.get(k, 0) >= v:
                continue
            w[k] = v
            out.append((k, v))
        return out

    def op(self, e, fn, reads=(), writes=(), inc=True):
        if self.stopped:
            return (e, 0)
        pend = self._collect(e, reads, writes)
        for k, v in pend[:-1]:
            semh = self.sem[k] if isinstance(k, str) else self.dsem[k]
            self.eng[e].wait_ge(semh, v)
            self.log[e].append(("w", k, v))
        ins = fn(self.eng[e])
        if pend:
            k, v = pend[-1]
            semh = self.sem[k] if isinstance(k, str) else self.dsem[k]
            ins._wait_ge(semh, v)
            self.log[e].append(("w", k, v))
        self.ninst += 1
        if e == "pe":
            self.npe += 1
        if inc:
            self.cnt[e] += 1
            ins.then_inc(self.sem[e], 1)
            self.log[e].append(("i", e, 1))
            tok = (e, self.cnt[e])
        else:
            tok = (e, self.cnt[e] + 1)
        self._mark(tok, reads, writes)
        return tok

    def dma(self, q, out, in_, reads=(), writes=(), acc=False, **kw):
        if self.stopped:
            return (q, 0)
        pl = self.dpool[q]
        i = pl[self.dnext[q]]
        self.dnext[q] = (self.dnext[q] + 1) % len(pl)
        self._wait(q, i, self.dcnt[i])
        if acc:
            self._deps(q, reads, ())
        else:
            self._deps(q, reads, writes)
        self.dcnt[i] += 16
        self.eng[q].dma_start(out=out, in_=in_, **kw).then_inc(self.dsem[i], 16)
        self.log[q].append(("i", i, 16))
        self.ninst += 1
        tok = (i, self.dcnt[i])
        if acc:
            self._mark(tok, reads, ())
            for b in writes:
                b.w[i] = self.dcnt[i]
        else:
            self._mark(tok, reads, writes)
        return tok

    def finish(self, bufs):
        for b in bufs:
            for k, v in list(b.w.items()) + list(b.r.items()):
                self._wait("sp", k, v)
        for i in range(len(self.dsem)):
            self._wait("sp", i, self.dcnt[i])


class T:
    def __init__(self, h, name="", init=None):
        self.h = h
        self.b = Buf(name, init)

    def __getitem__(self, idx):
        return self.h[idx]


class Cfg:
    def __init__(self, seq=8192, sample=True):
        self.SEQ = seq
        self.SEG = seq // 4
        self.NOWN = self.SEG // 128
        self.NSCAN = max(0, (3 * self.SEG - 256) // 128)
        self.NB_ALL = self.NSCAN + 2 + self.NOWN
        self.sample = sample


def build(cfg, debug=False, stop_after=None):
    nc = bass.Bass("TRN2", target_bir_lowering=False)
    st = contextlib.ExitStack()
    S = Sched(nc, st)
    NOWN, NSCAN, NB_ALL = cfg.NOWN, cfg.NSCAN, cfg.NB_ALL

    def din(name, shape, dt=F32):
        return T(nc.dram_tensor(name, list(shape), dt, kind="ExternalInput").ap(), name)

    def dout(name, shape, dt=F32):
        return T(nc.dram_tensor(name, list(shape), dt, kind="ExternalOutput").ap(), name)

    def dscr(name, shape, dt=F32):
        return T(nc.dram_tensor(name, list(shape), dt, kind="Internal").ap(), name)

    uniq = {"n": 0}

    def sb(name, shape, dt=F32, stack=None):
        uniq["n"] += 1
        return T((stack or st).enter_context(nc.sbuf_tensor("%s_%d" % (name, uniq["n"]), list(shape), dt)), name, S.epoch)

    def ps(name, shape, dt=F32):
        t_ = T(st.enter_context(nc.psum_tensor(name, list(shape), dt)), name)
        t_.b.excl = True
        return t_

    xs = din("xs", [NB_ALL * 128, D])
    tokc = din("tokc", [128, NB_ALL, 2])
    cTtok = din("cTtok", [128, KC, 65])
    cTrep = din("cTrep", [128, KC, 192])
    first_bias = din("first_bias", [128, 128])
    convvalid = din("convvalid", [128, 1])
    w_ada = din("w_ada", [D, 6 * D])
    b_ada = din("b_ada", [1, 6 * D])
    b_adaT = din("b_adaT", [128, 48])
    w_in = din("w_in", [D, DIN])
    b_in = din("b_in", [1, DIN])
    b_inT = din("b_inT", [128, 32])
    b_qk64 = din("b_qk64", [64, 20])
    norm_wT = din("norm_wT", [128, KC])
    sinks_rep = din("sinks_rep", [128, 16])
    w_bm = din("w_bm", [D, D])
    w_ba = din("w_ba", [D, D])
    w_out = din("w_out", [D, D])
    ln1_g = din("ln1_g", [1, D])
    ln1_b = din("ln1_b", [1, D])
    w_upr = din("w_upr", [FC, 128, KC * 256])
    b_upT = din("b_upT", [128, 2 * FC])
    conv_wT = din("conv_wT", [128, 2 * FC, 3])
    conv_bT = din("conv_bT", [128, 2 * FC])
    w_down = din("w_down", [DFF, D])
    b_down = din("b_down", [1, D])
    ln2_g = din("ln2_g", [1, D])
    ln2_b = din("ln2_b", [1, D])
    c_ident = din("c_ident", [128, 128])
    c_tri = din("c_tri", [128, 128])
    c_swab = din("c_swab", [128, 16, 256])

    xsamp = din("xsamp", [NTS, D])
    convst = din("convst", [128, 2 * FC, 32])
    y_out = dout("y_out", [NOWN * 128, D])
    ys_out = dout("ys_out", [NTS, D])
    sconv_out = dout("sconv_out", [128, 2 * FC, 32])
    pC_out = dout("pC_out", [128, 4, 257])
    pm_out = dout("pm_out", [4, 1])
    pkv_out = dout("pkv_out", [128, 512])
    pconv_out = dout("pconv_out", [128, 2 * FC, 2])
    x1_scr = dscr("x1_scr", [(NOWN + 1) * 128 + NTS, D])
    dbg = {}

    psf = [ps("psf%d" % i, [128, 512], F32) for i in range(6)]
    psb = [ps("psb%d" % i, [128, 1024], BF16) for i in range(2)]
    rot = {"f": 0, "b": 0}

    def PF():
        rot["f"] = (rot["f"] + 1) % 4
        return psf[rot["f"]]

    def PL(i):
        return psf[4 + i]

    def PB():
        rot["b"] = (rot["b"] + 1) % len(psb)
        return psb[rot["b"]]

    ident = sb("ident", [128, 128])
    identb = sb("identb", [128, 128], BF16)
    tri = sb("tri", [128, 128])
    trib = sb("trib", [128, 128], BF16)
    ones = sb("ones", [128, 128])
    onesb = sb("onesb", [128, 128], BF16)
    tokc_sb = sb("tokc_sb", [128, NB_ALL, 2])
    modF = sb("modF", [128, 32, 1])
    modR_scr = dscr("modR_scr", [128, 4, D])
    SAMP = {}
    modFs_scr = dscr("modFs_scr", [128, 32, NTS])
    gS_scr = dscr("gS_scr", [NTS, 2, D])
    gP = sb("gP", [128, 2, D])
    mst = sb("mst", [4, 1])
    CnT = sb("CnT", [128, 4, 257])
    CnTb = sb("CnTb", [128, 4, 257], BF16)
    ones_col = sb("ones_col", [128, 1], BF16)

    S.dma("sp", ident[:], c_ident[:, :], writes=[ident.b])
    S.dma("sp", tri[:], c_tri[:, :], writes=[tri.b])
    S.dma("sp", tokc_sb[:], tokc[:, :, :], writes=[tokc_sb.b])
    S.op("dve", lambda e: e.tensor_copy(identb[:], ident[:]), [ident.b], [identb.b])
    S.op("dve", lambda e: e.tensor_copy(trib[:], tri[:]), [tri.b], [trib.b])
    S.op("pool", lambda e: e.memset(ones[:], 1.0), [], [ones.b])
    S.op("pool", lambda e: e.memset(onesb[:], 1.0), [], [onesb.b])
    S.op("pool", lambda e: e.memset(ones_col[:], 1.0), [], [ones_col.b])
    S.op("pool", lambda e: e.memset(mst[:], 0.0), [], [mst.b])
    S.op("pool", lambda e: e.memset(CnT[:], 0.0), [], [CnT.b])
    S.op("pool", lambda e: e.memset(CnTb[:], 0.0), [], [CnTb.b])

    def load_w(dst, src, c0, c1, rows=D, stack=None):
        v = src.h.rearrange("(k p) c -> p k c", p=128)
        nk = rows // 128
        for k in range(nk):
            S.dma("pool", dst[:, k, 0:c1 - c0], v[:, k, c0:c1], writes=[dst.b], acc=(k > 0))

    def bias_hilo(name, src, c0, c1, stack):
        n = c1 - c0
        stg = sb(name + "_stg", [64, n], F32, stack)
        tb = sb(name + "_tb", [64, n], BF16, stack)
        out = sb(name, [64, n], BF16, stack)
        S.op("pool", lambda e: e.memset(out[:], 0.0), [], [out.b])
        S.dma("sp", stg[0:1, :], src[0:1, c0:c1], writes=[stg.b])
        S.dma("sp", stg[32:33, :], src[0:1, c0:c1], writes=[stg.b])
        S.op("dve", lambda e: e.tensor_copy(out[0:1, :], stg[0:1, :]), [stg.b], [out.b])
        S.op("dve", lambda e: e.tensor_copy(tb[32:33, :], stg[32:33, :]), [stg.b], [tb.b])
        S.op("dve", lambda e: e.tensor_tensor(stg[32:33, :], stg[32:33, :], tb[32:33, :], ALU.subtract),
             [tb.b], [stg.b])
        S.op("dve", lambda e: e.tensor_copy(out[32:33, :], stg[32:33, :]), [stg.b], [out.b])
        return out

    def mm_tok(pst, n0, n, hT, c0, nt, W, wc0, bias, bc0):
        for k in range(KC):
            S.op("pe", lambda e, k=k: e.matmul(pst[:nt, n0:n0 + n], hT[:, k, c0:c0 + nt], W[:, k, wc0:wc0 + n],
                                               start=(k == 0), stop=(k == KC - 1 and bias is None)),
                 [hT.b, W.b], [pst.b], inc=(k == KC - 1 and bias is None))
        if bias is not None:
            S.op("pe", lambda e: e.matmul(pst[:nt, n0:n0 + n], onesb[0:64, 0:nt], bias[:, bc0:bc0 + n],
                                          start=False, stop=True), [bias.b, onesb.b], [pst.b])

    def ln_stats(xt, nt, tmp):
        S.op("dve", lambda e: e.bn_stats(tmp[:nt, 0:6], xt[:nt, 0:512]), [xt.b], [tmp.b])
        S.op("dve", lambda e: e.bn_stats(tmp[:nt, 6:12], xt[:nt, 512:1024]), [xt.b], [tmp.b])
        S.op("dve", lambda e: e.bn_aggr(tmp[:nt, 12:14], tmp[:nt, 0:12]), [tmp.b], [tmp.b])
        S.op("act", lambda e: e.activation(tmp[:nt, 14:15], tmp[:nt, 13:14], AF.Ln, bias=epsc[:nt, 0:1], scale=1.0),
             [tmp.b, epsc.b], [tmp.b])
        S.op("act", lambda e: e.activation(tmp[:nt, 15:16], tmp[:nt, 14:15], AF.Exp, scale=-0.5), [tmp.b], [tmp.b])
        S.op("dve", lambda e: e.tensor_scalar(tmp[:nt, 16:17], tmp[:nt, 12:13], tmp[:nt, 15:16], -1.0, ALU.mult, ALU.mult),
             [tmp.b], [tmp.b])
        return tmp[:nt, 15:16], tmp[:nt, 16:17]

    epsc = sb("epsc", [128, 1])
    S.op("pool", lambda e: e.memset(epsc[:], LN_EPS), [], [epsc.b])

    xbl = [sb("xb%d" % i, [128, D], BF16) for i in range(2)]
    xbrot = {"i": 0}

    def ln_mod_rows(xn, nt, mrow):
        xbrot["i"] ^= 1
        xb = xbl[xbrot["i"]]
        S.op("dve", lambda e: e.tensor_tensor(xn[:nt, :], xn[:nt, :], mrow[:nt, 1, :], ALU.mult), [xn.b, mrow.b], [xn.b])
        S.op("dve", lambda e: e.tensor_tensor(xb[:nt, :], xn[:nt, :], mrow[:nt, 0, :], ALU.add), [xn.b, mrow.b], [xb.b])
        return xb

    def xb_to_hT(xb, nt, hT, c0):
        ptb = PB()
        for k in range(KC):
            S.op("pe", lambda e, k=k: e.transpose(ptb[:, k * 128:k * 128 + nt], xb[:nt, k * 128:(k + 1) * 128], identb[:nt, :nt]),
                 [xb.b, identb.b], [ptb.b], inc=(k == KC - 1))
        S.op("act", lambda e: e.activation(hT[:, :, c0:c0 + nt], ptb[:, :].rearrange("p (k t) -> p k t", k=KC)[:, :, 0:nt], AF.Identity),
             [ptb.b], [hT.b])

    def ln_rows_part1(xt, nt, xn, tmp, mrow):
        rstd, nmr = ln_stats(xt, nt, tmp)
        S.op("act", lambda e: e.activation(xn[:nt, :], xt[:nt, :], AF.Identity, bias=nmr, scale=rstd),
             [xt.b, tmp.b], [xn.b])
        return ln_mod_rows(xn, nt, mrow)

    def ln_to_hT(xt, nt, hT, c0, msel, samp, xn, tmp, mrow=None):
        rstd, nmr = ln_stats(xt, nt, tmp)
        S.op("act", lambda e: e.activation(xn[:nt, :], xt[:nt, :], AF.Identity, bias=nmr, scale=rstd),
             [xt.b, tmp.b], [xn.b])
        if mrow is not None:
            xb = ln_mod_rows(xn, nt, mrow)
            xb_to_hT(xb, nt, hT, c0)
            return
        for half in range(2):
            pst = PF()
            for kk in range(4):
                k = half * 4 + kk
                S.op("pe", lambda e, k=k, kk=kk: e.transpose(pst[:, kk * 128:kk * 128 + nt], xn[:nt, k * 128:(k + 1) * 128],
                                                             ident[:nt, :nt]), [xn.b, ident.b], [pst.b], inc=(kk == 3))
            for kk in range(4):
                k = half * 4 + kk
                if not samp:
                    S.op("act", lambda e, k=k, kk=kk: e.activation(
                        hT[:, k, c0:c0 + nt], pst[:, kk * 128:kk * 128 + nt], AF.Identity,
                        bias=modF[:, (msel) * 8 + k, 0:1], scale=modF[:, (msel + 1) * 8 + k, 0:1]),
                        [pst.b, modF.b], [hT.b])
                else:
                    S.op("dve", lambda e, k=k, kk=kk: e.tensor_tensor(
                        xn[:, 0:nt], pst[:, kk * 128:kk * 128 + nt], SAMP["modFs"][:, (msel + 1) * 8 + k, :], ALU.mult),
                        [pst.b, SAMP["modFs"].b], [xn.b])
                    S.op("dve", lambda e, k=k: e.tensor_tensor(
                        hT[:, k, c0:c0 + nt], xn[:, 0:nt], SAMP["modFs"][:, msel * 8 + k, :], ALU.add),
                        [xn.b, SAMP["modFs"].b], [hT.b])

    with contextlib.ExitStack() as ph:
        wada = sb("wada", [128, KC, 2048], BF16, ph)
        scT = sb("scT", [128, KC, 65], F32, ph)
        scTb = sb("scTb", [128, KC, 65], BF16, ph)
        scR = sb("scR", [128, KC, 192], F32, ph)
        scRb = sb("scRb", [128, KC, 192], BF16, ph)
        badaT = sb("badaT", [128, 48], F32, ph)
        modFt = sb("modFt", [128, 32, 65], F32, ph)
        gSt = sb("gSt", [NTS, 2, D], F32, ph)
        S.dma("sp", scT[:], cTtok[:, :, :], writes=[scT.b])
        S.dma("sp", scR[:], cTrep[:, :, :], writes=[scR.b])
        S.dma("sp", badaT[:], b_adaT[:, :], writes=[badaT.b])
        S.op("act", lambda e: e.activation(scTb[:], scT[:], AF.Silu), [scT.b], [scTb.b])
        S.op("act", lambda e: e.activation(scRb[:], scR[:], AF.Silu), [scR.b], [scRb.b])
        mrt = sb("mrt", [128, D], F32, ph)
        for gi, (cbase, addone) in enumerate([(0, False), (1024, True), (3072, False), (4096, True)]):
            if gi % 2 == 0:
                load_w(wada, w_ada, cbase if gi == 0 else 3072, (cbase if gi == 0 else 3072) + 2048)
            for oc in range(8):
                pst = PF()
                wc = (gi % 2) * 1024 + oc * 128
                for k in range(KC):
                    S.op("pe", lambda e, k=k, wc=wc: e.matmul(pst[:, 0:65], wada[:, k, wc:wc + 128], scTb[:, k, :],
                                                              start=(k == 0), stop=(k == KC - 1)),
                         [wada.b, scTb.b], [pst.b], inc=(k == KC - 1))
                bcol = (cbase // 128) + oc
                if addone:
                    S.op("dve", lambda e, bcol=bcol, gi=gi, oc=oc: e.tensor_scalar(
                        modFt[:, gi * 8 + oc, :], pst[:, 0:65], badaT[:, bcol:bcol + 1], 1.0, ALU.add, ALU.add),
                        [pst.b, badaT.b], [modFt.b])
                else:
                    S.op("dve", lambda e, bcol=bcol, gi=gi, oc=oc: e.tensor_scalar(
                        modFt[:, gi * 8 + oc, :], pst[:, 0:65], badaT[:, bcol:bcol + 1], None, ALU.add),
                        [pst.b, badaT.b], [modFt.b])
            bh = bias_hilo("bada_r%d" % gi, b_ada, cbase, cbase + 1024, ph)
            for half in range(2):
                pst = PF()
                mm_tok(pst, 0, 512, scRb, 0, 128, wada, (gi % 2) * 1024 + half * 512, bh, half * 512)
                S.op("dve", lambda e, half=half, pst=pst, addone=addone: e.tensor_scalar(
                    mrt[:, half * 512:(half + 1) * 512], pst[:, :], 1.0 if addone else 0.0, None, ALU.add), [pst.b], [mrt.b])
            S.dma("sp", modR_scr[:, gi, :], mrt[:], reads=[mrt.b], writes=[modR_scr.b], acc=(gi > 0))
        for gi, cbase in enumerate([2048, 5120]):
            load_w(wada, w_ada, cbase, cbase + 1024)
            bh = bias_hilo("bada_g%d" % gi, b_ada, cbase, cbase + 1024, ph)
            for (dst, r0, nt) in [(gP, 0, 128), (gSt, 128, NTS)]:
                for half in range(2):
                    pst = PF()
                    mm_tok(pst, 0, 512, scRb, r0, nt, wada, half * 512, bh, half * 512)
                    S.op("act", lambda e, dst=dst, nt=nt, half=half, gi=gi: e.activation(
                        dst[:nt, gi, half * 512:(half + 1) * 512], pst[:nt, :], AF.Identity),
                        [pst.b], [dst.b])
        S.op("dve", lambda e: e.tensor_copy(modF[:], modFt[:, :, 0:1]), [modFt.b], [modF.b])
        S.dma("sp", modFs_scr[:, :, :], modFt[:, :, 1:65], reads=[modFt.b], writes=[modFs_scr.b])
        S.dma("sp", gS_scr[:, :, :], gSt[:], reads=[gSt.b], writes=[gS_scr.b])

    S.phase_end()

    def mlstm_gates(zg, nt, blk_all, gt, tri_t=None, ones_t=None):
        tri_t = tri_t or tri
        ones_t = ones_t or ones
        S.op("act", lambda e: e.activation(gt[:nt, 32:36], zg[:nt, 4:8], AF.Exp, scale=-1.0), [zg.b], [gt.b])
        S.op("act", lambda e: e.activation(gt[:nt, 32:36], gt[:nt, 32:36], AF.Ln, bias=onec[:nt, 0:1], scale=1.0),
             [gt.b, onec.b], [gt.b])
        S.op("dve", lambda e: e.tensor_scalar(gt[:nt, 0:4], gt[:nt, 32:36], tokc_sb[:nt, blk_all, 0:1], -1.0,
                                              ALU.mult, ALU.mult), [gt.b, tokc_sb.b], [gt.b])
        S.op("dve", lambda e: e.tensor_scalar(gt[:nt, 4:8], zg[:nt, 0:4], tokc_sb[:nt, blk_all, 1:2], None, ALU.add),
             [zg.b, tokc_sb.b], [gt.b])
        pst = PF()
        S.op("pe", lambda e: e.matmul(pst[:nt, 0:4], tri_t[:nt, :nt], gt[:nt, 0:4], start=True, stop=True),
             [tri_t.b, gt.b], [pst.b], inc=False)
        S.op("pe", lambda e: e.matmul(pst[:nt, 4:8], ones_t[:nt, :nt], gt[:nt, 0:4], start=True, stop=True),
             [ones_t.b, gt.b], [pst.b])
        S.op("dve", lambda e: e.tensor_copy(gt[:nt, 8:16], pst[:nt, 0:8]), [pst.b], [gt.b])
        S.op("act", lambda e: e.activation(gt[:nt, 16:20], gt[:nt, 8:12], AF.Exp), [gt.b], [gt.b])
        S.op("dve", lambda e: e.tensor_tensor(gt[:nt, 36:40], gt[:nt, 4:8], gt[:nt, 8:12], ALU.subtract), [gt.b], [gt.b])
        S.op("act", lambda e: e.activation(gt[:nt, 20:24], gt[:nt, 36:40], AF.Exp), [gt.b], [gt.b])
        S.op("act", lambda e: e.activation(gt[:nt, 24:28], gt[:nt, 12:16], AF.Exp), [gt.b], [gt.b])
        S.op("dve", lambda e: e.tensor_tensor(gt[:nt, 28:32], gt[:nt, 36:40], gt[:nt, 12:16], ALU.add), [gt.b], [gt.b])

    onec = sb("onec", [128, 1])
    S.op("pool", lambda e: e.memset(onec[:], 1.0), [], [onec.b])
    mtmp = sb("mtmp", [4, 8])

    def m_update(gt, nt):
        p1 = PF()
        S.op("pe", lambda e: e.transpose(p1[0:4, 0:nt], gt[:nt, 28:32], ident[:nt, :nt]), [gt.b, ident.b], [p1.b], inc=False)
        S.op("pe", lambda e: e.transpose(p1[0:4, 128:128 + nt], gt[:nt, 12:16], ident[:nt, :nt]), [gt.b, ident.b], [p1.b])
        S.op("dve", lambda e: e.reduce_max(mtmp[:, 0:1], p1[0:4, 0:nt], AX.X), [p1.b], [mtmp.b])
        S.op("dve", lambda e: e.tensor_tensor(mtmp[:, 1:2], p1[0:4, 128:129], mst[:, 0:1], ALU.add), [p1.b, mst.b], [mtmp.b])
        S.op("dve", lambda e: e.tensor_tensor(mst[:, 0:1], mtmp[:, 0:1], mtmp[:, 1:2], ALU.max), [mtmp.b], [mst.b])

    def state_update(ks, v1, gt, nt):
        for hp in range(2):
            pst = PF()
            for hh in range(2):
                h = hp * 2 + hh
                S.op("pe", lambda e, h=h, hh=hh: e.matmul(pst[:, hh * 256:(hh + 1) * 256], ks[:nt, h, :], v1[:nt, h, 0:256],
                                                          start=True, stop=True), [ks.b, v1.b], [pst.b], inc=(hh == 1))
            for hh in range(2):
                h = hp * 2 + hh
                S.op("dve", lambda e, h=h: e.tensor_scalar(CnT[:, h, 0:256], CnT[:, h, 0:256], gt[:, 24 + h:25 + h], None, ALU.mult),
                     [gt.b], [CnT.b])
                S.op("dve", lambda e, h=h, hh=hh: e.scalar_tensor_tensor(
                    CnT[:, h, 0:256], pst[:, hh * 256:(hh + 1) * 256], gt[:, 24 + h:25 + h], CnT[:, h, 0:256], ALU.mult, ALU.add),
                    [pst.b, gt.b], [CnT.b])
        pst = PF()
        for h in range(4):
            S.op("pe", lambda e, h=h: e.matmul(pst[:, h:h + 1], ks[:nt, h, :], ones_col[:nt, 0:1], start=True, stop=True),
                 [ks.b, ones_col.b], [pst.b], inc=(h == 3))
        for h in range(4):
            S.op("dve", lambda e, h=h: e.tensor_scalar(CnT[:, h, 256:257], CnT[:, h, 256:257], gt[:, 24 + h:25 + h], None, ALU.mult),
                 [gt.b], [CnT.b])
            S.op("dve", lambda e, h=h: e.scalar_tensor_tensor(
                CnT[:, h, 256:257], pst[:, h:h + 1], gt[:, 24 + h:25 + h], CnT[:, h, 256:257], ALU.mult, ALU.add),
                [pst.b, gt.b], [CnT.b])
        S.op("act", lambda e: e.activation(CnTb[:], CnT[:], AF.Identity), [CnT.b], [CnTb.b])

    gt = sb("gt", [128, 40])
    lnt = sb("lnt", [128, 20])
    ks = sb("ks", [128, 4, 128], BF16)
    v1 = sb("v1", [128, 4, 257], BF16)
    S.op("pool", lambda e: e.memset(v1[:], 1.0), [], [v1.b])
    xtl = [sb("xt%d" % i, [128, D]) for i in range(2)]
    xn = sb("xn", [128, D])
    xrot = {"i": 0}
    ks_g, v1_g, gt_g = ks, v1, gt

    def XT():
        xrot["i"] ^= 1
        return xtl[xrot["i"]]

    def kv_from_psum(pk, pv0, pv1, nt, ks=None, v1=None, gt=None):
        ks = ks or ks_g
        v1 = v1 or v1_g
        gt = gt or gt_g
        for h in range(4):
            S.op("dve", lambda e, h=h: e.tensor_scalar(ks[:nt, h, :], pk[:nt, h * 128:(h + 1) * 128], gt[:nt, 20 + h:21 + h], None, ALU.mult),
                 [pk.b, gt.b], [ks.b])
        S.op("act", lambda e: e.activation(v1[:nt, 0:2, 0:256], pv0[:nt, :].rearrange("p (h d) -> p h d", h=2), AF.Identity), [pv0.b], [v1.b])
        S.op("act", lambda e: e.activation(v1[:nt, 2:4, 0:256], pv1[:nt, :].rearrange("p (h d) -> p h d", h=2), AF.Identity), [pv1.b], [v1.b])

    def scan_block(hT, c0, ba, W, wk0, bias):
        pg = PF()
        mm_tok(pg, 0, 8, hT, c0, 128, W, wk0 + 1536, bias, wk0 + 1536)
        mlstm_gates(pg, 128, ba, gt)
        pk, pv0, pv1 = PF(), PF(), PF()
        mm_tok(pk, 0, 512, hT, c0, 128, W, wk0, bias, wk0)
        mm_tok(pv0, 0, 512, hT, c0, 128, W, wk0 + 512, bias, wk0 + 512)
        mm_tok(pv1, 0, 512, hT, c0, 128, W, wk0 + 1024, bias, wk0 + 1024)
        kv_from_psum(pk, pv0, pv1, 128)
        state_update(ks, v1, gt, 128)
        m_update(gt, 128)


    def head_ln_s(banks, hst, hmn, nt):
        for h in range(4):
            S.op("dve", lambda e, h=h: e.tensor_copy(hst[:nt, 56 + h:57 + h], banks[h][:nt, 256:257]), [banks[h].b], [hst.b])
        head_ln([banks[0], banks[2]], None, hst, hmn, nt, banks=banks)

    def head_ln(pnum, pden, hst, hmn, nt, banks=None):
        if banks is None:
            S.op("dve", lambda e: e.tensor_copy(hst[:nt, 56:60], pden[:nt, 0:4]), [pden.b], [hst.b])
        S.op("dve", lambda e: e.tensor_scalar(hst[:nt, 52:56], hst[:nt, 56:60], -1.0, None, ALU.mult), [hst.b], [hst.b])
        S.op("dve", lambda e: e.tensor_tensor(hst[:nt, 0:4], hst[:nt, 56:60], hst[:nt, 52:56], ALU.max), [hst.b], [hst.b])
        S.op("dve", lambda e: e.tensor_scalar(hst[:nt, 0:4], hst[:nt, 0:4], 1.0, None, ALU.max), [hst.b], [hst.b])
        S.op("dve", lambda e: e.scalar_tensor_tensor(hst[:nt, 4:8], hst[:nt, 0:4], LN_EPS, hst[:nt, 0:4], ALU.mult, ALU.mult), [hst.b], [hst.b])
        for h in range(4):
            pn = pnum[h // 2] if banks is None else banks[h]
            cs = (h % 2) * 256 if banks is None else 0
            S.op("dve", lambda e, h=h, pn=pn, cs=cs: e.bn_stats(hst[:nt, 8 + 6 * h:14 + 6 * h], pn[:nt, cs:cs + 256]), [pn.b], [hst.b])
            S.op("dve", lambda e, h=h: e.bn_aggr(hst[:nt, 32 + 2 * h:34 + 2 * h], hst[:nt, 8 + 6 * h:14 + 6 * h]), [hst.b], [hst.b])
        mvv = hst[:nt, 32:40].rearrange("p (h t) -> p h t", t=2)
        S.op("dve", lambda e: e.tensor_tensor(hst[:nt, 40:44], mvv[:, :, 1], hst[:nt, 4:8], ALU.add), [hst.b], [hst.b])
        S.op("act", lambda e: e.activation(hst[:nt, 40:44], hst[:nt, 40:44], AF.Ln), [hst.b], [hst.b])
        S.op("act", lambda e: e.activation(hst[:nt, 44:48], hst[:nt, 40:44], AF.Exp, scale=-0.5), [hst.b], [hst.b])
        S.op("dve", lambda e: e.scalar_tensor_tensor(hst[:nt, 48:52], mvv[:, :, 0], -1.0, hst[:nt, 44:48], ALU.mult, ALU.mult), [hst.b], [hst.b])
        for h in range(4):
            pn = pnum[h // 2] if banks is None else banks[h]
            cs = (h % 2) * 256 if banks is None else 0
            S.op("act", lambda e, h=h, pn=pn, cs=cs: e.activation(hmn[:nt, h * 256:(h + 1) * 256], pn[:nt, cs:cs + 256], AF.Identity,
                                                                  bias=hst[:nt, 48 + h:49 + h], scale=hst[:nt, 44 + h:45 + h]),
                 [pn.b, hst.b], [hmn.b])

    def hm_transpose_out(hmn, nt, hmT, c0, nwT, sgog):
        for half in range(2):
            pst = PF()
            for kk in range(4):
                k = half * 4 + kk
                S.op("pe", lambda e, k=k, kk=kk, pst=pst: e.transpose(pst[:, kk * 128:kk * 128 + nt], hmn[:nt, k * 128:(k + 1) * 128], ident[:nt, :nt]),
                     [hmn.b, ident.b], [pst.b], inc=(kk == 3))
            for kk in range(4):
                k = half * 4 + kk
                S.op("dve", lambda e, k=k, kk=kk, pst=pst: e.scalar_tensor_tensor(
                    hmT[:, k, c0:c0 + nt], pst[:, kk * 128:kk * 128 + nt], nwT[:, k:k + 1], sgog[:, k, c0:c0 + nt], ALU.mult, ALU.mult),
                    [pst.b, nwT.b, sgog.b], [hmT.b])

    if cfg.sample:
        sC0 = din("sC0", [NSEQ_S, 128, 4 * 257])
        sm0rep = din("sm0rep", [128, 64])
        sm0T = din("sm0T", [4, NSEQ_S])
        c_tri_s = din("c_tri_s", [NTS, NTS])
        c_ones_s = din("c_ones_s", [NTS, NTS])
        c_seqmask = din("c_seqmask", [128, NSEQ_S, NTS])
        c_seqmaskT = din("c_seqmaskT", [NTS, NSEQ_S])
        c_pick = din("c_pick", [NTS, 128])
        ckT = din("ckT", [64, NSEQ_S, 4, 128])
        cvn = din("cvn", [128, NSEQ_S, 256])
        ck_nat = din("ck_nat", [NSEQ_S, 128, 256])
        cv_nat = din("cv_nat", [NSEQ_S, 128, 256])
        c_sbias = din("c_sbias", [NTS, 16, 192])
        sC_out = dout("sC_out", [NSEQ_S, 128, 4 * 257])
        sm_out = dout("sm_out", [4, NSEQ_S])
        sk_out = dout("sk_out", [NSEQ_S, 128, 256])
        sv_out = dout("sv_out", [NSEQ_S, 128, 256])

    def samp_mlstm(hT, c0, Wm, bm, hmT, sgog, nwT, ph):
        nt = NTS
        tri_s = sb("tri_s", [NTS, NTS], F32, ph)
        ones_s = sb("ones_s", [NTS, NTS], F32, ph)
        smaskb = sb("smaskb", [128, NSEQ_S, NTS], BF16, ph)
        smaskT = sb("smaskT", [NTS, NSEQ_S], F32, ph)
        pick = sb("pick", [NTS, 128], F32, ph)
        em0 = sb("em0", [128, 64], F32, ph)
        m0T = sb("m0T", [4, NSEQ_S], F32, ph)
        S.dma("sp", tri_s[:], c_tri_s[:, :], writes=[tri_s.b])
        S.dma("sp", ones_s[:], c_ones_s[:, :], writes=[ones_s.b])
        S.dma("pool", smaskb[:], c_seqmask[:, :, :], writes=[smaskb.b])
        S.dma("sp", smaskT[:], c_seqmaskT[:, :], writes=[smaskT.b])
        S.dma("sp", pick[:], c_pick[:, :], writes=[pick.b])
        S.dma("sp", em0[:], sm0rep[:, :], writes=[em0.b])
        S.dma("sp", m0T[:], sm0T[:, :], writes=[m0T.b])
        S.op("act", lambda e: e.activation(em0[:], em0[:], AF.Exp), [em0.b], [em0.b])
        qs = sb("sqs", [128, 4, 128], BF16, ph)
        qkT = sb("sqkT", [128, 8, NTS], BF16, ph)
        qm = sb("sqm", [128, 4, NSEQ_S, NTS], BF16, ph)
        SmT = sb("sSmT", [NTS, 4, NTS], BF16, ph)
        hmn = sb("shmn", [128, D], F32, ph)
        hst = sb("shst", [128, 64], F32, ph)
        Cin = [sb("Cin%d" % i, [128, 4, 257], F32, ph) for i in range(2)]
        Cbf = [sb("Cbf%d" % i, [128, 4, 257], BF16, ph) for i in range(2)]
        pg = PF()
        mm_tok(pg, 0, 8, hT, c0, nt, Wm, 2048, bm, 2048)
        mlstm_gates(pg, nt, NB_ALL - 1, gt, tri_s, ones_s)
        pq = PF()
        mm_tok(pq, 0, 512, hT, c0, nt, Wm, 0, bm, 0)
        for h in range(4):
            S.op("dve", lambda e, h=h, pq=pq: e.tensor_scalar(qs[:nt, h, :], pq[:nt, h * 128:(h + 1) * 128], gt[:nt, 16 + h:17 + h],
                                                              128.0 ** -0.5, ALU.mult, ALU.mult), [pq.b, gt.b], [qs.b])
        pk, pv0, pv1 = PF(), PF(), PF()
        mm_tok(pk, 0, 512, hT, c0, nt, Wm, 512, bm, 512)
        mm_tok(pv0, 0, 512, hT, c0, nt, Wm, 1024, bm, 1024)
        mm_tok(pv1, 0, 512, hT, c0, nt, Wm, 1536, bm, 1536)
        kv_from_psum(pk, pv0, pv1, nt)
        ptb = PB()
        for h in range(4):
            S.op("pe", lambda e, h=h: e.transpose(ptb[:, h * 64:(h + 1) * 64], qs[:nt, h, :], identb[:nt, :nt]),
                 [qs.b, identb.b], [ptb.b], inc=False)
        for h in range(4):
            S.op("pe", lambda e, h=h: e.transpose(ptb[:, 256 + h * 64:256 + (h + 1) * 64], ks[:nt, h, :], identb[:nt, :nt]),
                 [ks.b, identb.b], [ptb.b], inc=(h == 3))
        S.op("act", lambda e: e.activation(qkT[:].rearrange("p a b -> p (a b)"), ptb[:, 0:512], AF.Identity), [ptb.b], [qkT.b])
        for h in range(4):
            S.op("dve", lambda e, h=h: e.tensor_tensor(qm[:, h, :, :], qkT[:, h, :].unsqueeze(1).broadcast_to([128, NSEQ_S, NTS]), smaskb[:], ALU.mult),
                 [qkT.b, smaskb.b], [qm.b])
        pS = PF()
        for h in range(4):
            S.op("pe", lambda e, h=h: e.matmul(pS[:nt, h * 64:(h + 1) * 64], qkT[:, 4 + h, :], qkT[:, h, :], start=True, stop=True),
                 [qkT.b], [pS.b], inc=(h == 3))
        S.op("dve", lambda e: e.tensor_tensor(SmT[:], pS[:nt, 0:256].rearrange("p (h t) -> p h t", h=4),
                                              tri_s[:, :].unsqueeze(1).broadcast_to([NTS, 4, NTS]), ALU.mult), [pS.b, tri_s.b], [SmT.b])
        banks = [PL(0), PL(1), psf[2], psf[3]]
        for h in range(4):
            S.op("pe", lambda e, h=h: e.matmul(banks[h][:nt, 0:257], SmT[:, h, :], v1[:nt, h, 0:257], start=True, stop=False),
                 [SmT.b, v1.b], [banks[h].b], inc=True)
        for b in range(NSEQ_S):
            ci_ = Cin[b % 2]
            cb_ = Cbf[b % 2]
            S.dma("sp", ci_[:].rearrange("p h v -> p (h v)"), sC0[b, :, :], writes=[ci_.b])
            for h in range(4):
                S.op("dve", lambda e, b=b, h=h, ci_=ci_, cb_=cb_: e.tensor_scalar(cb_[:, h, :], ci_[:, h, :], em0[:, b * 4 + h:b * 4 + h + 1], None, ALU.mult),
                     [ci_.b, em0.b], [cb_.b])
            for h in range(4):
                S.op("pe", lambda e, h=h, b=b, cb_=cb_: e.matmul(banks[h][:nt, 0:257], qm[:, h, b, :], cb_[:, h, 0:257],
                                                               start=False, stop=(b == NSEQ_S - 1)), [qm.b, cb_.b], [banks[h].b], inc=True)
        head_ln_s(banks, hst, hmn, nt)
        hm_transpose_out(hmn, nt, hmT, c0, nwT, sgog)
        p1 = PF()
        S.op("pe", lambda e: e.transpose(p1[0:4, 0:nt], gt[:nt, 28:32], ident[:nt, :nt]), [gt.b, ident.b], [p1.b], inc=False)
        S.op("pe", lambda e: e.transpose(p1[0:4, 128:128 + nt], gt[:nt, 12:16], ident[:nt, :nt]), [gt.b, ident.b], [p1.b])
        mn = sb("smn", [4, 3, NSEQ_S], F32, ph)
        S.op("dve", lambda e: e.reduce_max(mn[:, 0, :], p1[0:4, 0:nt].rearrange("p (b t) -> p b t", t=4), AX.X), [p1.b], [mn.b])
        S.op("dve", lambda e: e.tensor_tensor(mn[:, 1, :], p1[0:4, 128:128 + nt].rearrange("p (b t) -> p b t", t=4)[:, :, 0], m0T[:, :], ALU.add),
             [p1.b, m0T.b], [mn.b])
        S.op("dve", lambda e: e.tensor_tensor(mn[:, 2, :], mn[:, 0, :], mn[:, 1, :], ALU.max), [mn.b], [mn.b])
        S.dma("sp", sm_out[:, :], mn[:, 2, :], reads=[mn.b], writes=[sm_out.b])
        dg = sb("sdg", [4, 4, NSEQ_S], F32, ph)
        S.op("dve", lambda e: e.tensor_tensor(dg[:], ident[0:4, 0:4].unsqueeze(2).broadcast_to([4, 4, NSEQ_S]),
                                              mn[:, 2, :].unsqueeze(1).broadcast_to([4, 4, NSEQ_S]), ALU.mult), [ident.b, mn.b], [dg.b])
        dg2 = sb("sdg2", [NTS, NSEQ_S, 4], F32, ph)
        S.op("dve", lambda e: e.tensor_tensor(dg2[:], gt[:nt, 12:16].unsqueeze(1).broadcast_to([NTS, NSEQ_S, 4]),
                                              smaskT[:, :].unsqueeze(2).broadcast_to([NTS, NSEQ_S, 4]), ALU.mult), [gt.b, smaskT.b], [dg2.b])
        pr = PF()
        S.op("pe", lambda e: e.matmul(pr[:, 0:64], ones[0:4, 0:128], dg[:].rearrange("p a b -> p (a b)"), start=True, stop=True),
             [ones.b, dg.b], [pr.b], inc=False)
        S.op("pe", lambda e: e.matmul(pr[:, 64:128], pick[:, :], dg2[:].rearrange("p a b -> p (a b)"), start=True, stop=True),
             [pick.b, dg2.b], [pr.b])
        fac = sb("sfac", [128, NSEQ_S, 4], F32, ph)
        S.op("act", lambda e: e.activation(fac[:], pr[:, 64:128].rearrange("p (b h) -> p b h", h=4), AF.Identity), [pr.b], [fac.b])
        S.op("dve", lambda e: e.tensor_tensor(fac[:], fac[:], pr[:, 0:64].rearrange("p (h b) -> p b h", h=4), ALU.subtract), [pr.b, fac.b], [fac.b])
        S.op("act", lambda e: e.activation(fac[:], fac[:], AF.Exp), [fac.b], [fac.b])
        ksm = [sb("ksm%d" % i, [NTS, 4, 128], BF16, ph) for i in range(2)]
        Cout = [sb("Cout%d" % i, [128, 4, 257], F32, ph) for i in range(2)]
        for b in range(NSEQ_S):
            km = ksm[b % 2]
            co = Cout[b % 2]
            ci_ = Cin[b % 2]
            S.dma("sp", ci_[:].rearrange("p h v -> p (h v)"), sC0[b, :, :], writes=[ci_.b])
            S.op("dve", lambda e, b=b, km=km: e.tensor_scalar(km[:], ks[:nt, :, :], smaskT[:, b:b + 1], None, ALU.mult), [ks.b, smaskT.b], [km.b])
            for hp in range(2):
                pst = PF()
                for hh in range(2):
                    h = hp * 2 + hh
                    S.op("pe", lambda e, h=h, hh=hh, km=km, pst=pst: e.matmul(pst[:, hh * 256:(hh + 1) * 256], km[:, h, :], v1[:nt, h, 0:256],
                                                                             start=True, stop=True), [km.b, v1.b], [pst.b], inc=(hh == 1))
                for hh in range(2):
                    h = hp * 2 + hh
                    S.op("dve", lambda e, h=h, b=b, ci_=ci_, co=co: e.tensor_scalar(co[:, h, 0:256], ci_[:, h, 0:256], em0[:, b * 4 + h:b * 4 + h + 1],
                                                                                   fac[:, b, h:h + 1], ALU.mult, ALU.mult), [ci_.b, em0.b, fac.b], [co.b])
                    S.op("dve", lambda e, h=h, hh=hh, b=b, co=co, pst=pst: e.scalar_tensor_tensor(
                        co[:, h, 0:256], pst[:, hh * 256:(hh + 1) * 256], fac[:, b, h:h + 1], co[:, h, 0:256], ALU.mult, ALU.add),
                        [pst.b, fac.b], [co.b])
            pst = PF()
            for h in range(4):
                S.op("pe", lambda e, h=h, km=km, pst=pst: e.matmul(pst[:, h:h + 1], km[:, h, :], ones_col[:nt, 0:1], start=True, stop=True),
                     [km.b, ones_col.b], [pst.b], inc=(h == 3))
            for h in range(4):
                S.op("dve", lambda e, h=h, b=b, ci_=ci_, co=co: e.tensor_scalar(co[:, h, 256:257], ci_[:, h, 256:257], em0[:, b * 4 + h:b * 4 + h + 1],
                                                                               fac[:, b, h:h + 1], ALU.mult, ALU.mult), [ci_.b, em0.b, fac.b], [co.b])
                S.op("dve", lambda e, h=h, b=b, co=co, pst=pst: e.scalar_tensor_tensor(
                    co[:, h, 256:257], pst[:, h:h + 1], fac[:, b, h:h + 1], co[:, h, 256:257], ALU.mult, ALU.add), [pst.b, fac.b], [co.b])
            S.dma("sp", sC_out[b, :, :], co[:].rearrange("p h v -> p (h v)"), reads=[co.b], writes=[sC_out.b])

    def samp_swa(hT, c0, Wqa, Wkd, Wkv2, bkv2, bqk, snk, haT, ph):
        nt = NTS
        qaS = sb("qaS", [64, 16, NTS], BF16, ph)
        kdS = sb("kdS", [64, 4, NTS], BF16, ph)
        kvS = sb("kvS", [NTS, 512], BF16, ph)
        kvSf = sb("kvSf", [NTS, 512], F32, ph)
        ckb = sb("ckb", [64, NSEQ_S, 4, 128], BF16, ph)
        cvb = sb("cvb", [128, NSEQ_S, 256], BF16, ph)
        sbias = sb("sbias", [NTS, 16, 192], F32, ph)
        smask = sb("smask2", [128, NSEQ_S, NTS], F32, ph)
        smaskb = sb("smaskb2", [128, NSEQ_S, NTS], BF16, ph)
        S.dma("pool", ckb[:], ckT[:, :, :, :], writes=[ckb.b])
        S.dma("pool", cvb[:], cvn[:, :, :], writes=[cvb.b])
        S.dma("sp", sbias[:], c_sbias[:, :, :], writes=[sbias.b])
        S.dma("sp", smask[:], c_seqmask[:, :, :], writes=[smask.b])
        S.op("dve", lambda e: e.tensor_copy(smaskb[:], smask[:]), [smask.b], [smaskb.b])
        S.dma("sp", sk_out[:, 0:124, :], ck_nat[:, 4:128, :], writes=[sk_out.b])
        S.dma("sp", sv_out[:, 0:124, :], cv_nat[:, 4:128, :], writes=[sv_out.b])
        for h in range(16):
            pst = PF()
            for k in range(KC):
                S.op("pe", lambda e, k=k, h=h, pst=pst: e.matmul(pst[0:64, 0:nt], Wqa[:, k, h * 64:(h + 1) * 64], hT[:, k, c0:c0 + nt],
                                                                start=(k == 0), stop=(k == KC - 1)), [Wqa.b, hT.b], [pst.b], inc=(k == KC - 1))
            S.op("act", lambda e, h=h, pst=pst: e.activation(qaS[:, h, :], pst[0:64, 0:nt], AF.Identity, bias=bqk[:, h:h + 1], scale=1.0),
                 [pst.b, bqk.b], [qaS.b])
        for jj in range(4):
            pst = PF()
            for k in range(KC):
                S.op("pe", lambda e, k=k, jj=jj, pst=pst: e.matmul(pst[0:64, 0:nt], Wkd[:, k, jj * 64:(jj + 1) * 64], hT[:, k, c0:c0 + nt],
                                                                  start=(k == 0), stop=(k == KC - 1)), [Wkd.b, hT.b], [pst.b], inc=(k == KC - 1))
            S.op("act", lambda e, jj=jj, pst=pst: e.activation(kdS[:, jj, :], pst[0:64, 0:nt], AF.Identity, bias=bqk[:, 16 + jj:17 + jj], scale=1.0),
                 [pst.b, bqk.b], [kdS.b])
        pst = PF()
        mm_tok(pst, 0, 512, hT, c0, nt, Wkv2, 0, bkv2, 0)
        S.op("act", lambda e, pst=pst: e.activation(kvS[:, :], pst[:nt, :], AF.Identity), [pst.b], [kvS.b])
        S.op("dve", lambda e, pst=pst: e.tensor_copy(kvSf[:, :], pst[:nt, :]), [pst.b], [kvSf.b])
        for b in range(NSEQ_S):
            S.dma("sp", sk_out[b, 124:128, :], kvSf[4 * b:4 * b + 4, 0:256], reads=[kvSf.b], writes=[sk_out.b], acc=True)
            S.dma("sp", sv_out[b, 124:128, :], kvSf[4 * b:4 * b + 4, 256:512], reads=[kvSf.b], writes=[sv_out.b], acc=True)
        qmS = sb("qmS", [64, NSEQ_S, NTS], BF16, ph)
        Sb_ = sb("sSb", [NTS, 192], F32, ph)
        Pb_ = sb("sPb", [NTS, 192], BF16, ph)
        PTc = sb("sPTc", [128, NTS], BF16, ph)
        PTn = sb("sPTn", [NTS, NTS], BF16, ph)
        PTm = sb("sPTm", [128, NSEQ_S, NTS], BF16, ph)
        ast = sb("sast", [NTS, 96], F32, ph)
        hab = sb("shab", [NTS, D], BF16, ph)
        po = [PL(0), PL(1)]
        for h in range(16):
            hp, hh = h // 2, h % 2
            jj = h // 4
            b0 = hh * 64
            S.op("dve", lambda e, h=h: e.tensor_tensor(qmS[:], qaS[:, h, :].unsqueeze(1).broadcast_to([64, NSEQ_S, NTS]), smaskb[0:64, :, :], ALU.mult),
                 [qaS.b, smaskb.b], [qmS.b])
            pS = PF()
            for b in range(NSEQ_S):
                S.op("pe", lambda e, b=b, jj=jj, pS=pS: e.matmul(pS[:nt, 0:128], qmS[:, b, :], ckb[:, b, jj, :],
                                                                start=(b == 0), stop=(b == NSEQ_S - 1)), [qmS.b, ckb.b], [pS.b], inc=False)
            S.op("pe", lambda e, h=h, jj=jj, pS=pS: e.matmul(pS[:nt, 128:192], qaS[:, h, :], kdS[:, jj, :], start=True, stop=True),
                 [qaS.b, kdS.b], [pS.b])
            S.op("dve", lambda e, h=h, pS=pS: e.scalar_tensor_tensor(Sb_[:, :], pS[:nt, 0:192], 0.125, sbias[:, h, :], ALU.mult, ALU.add),
                 [pS.b, sbias.b], [Sb_.b])
            S.op("dve", lambda e, h=h: e.reduce_max(ast[:, h:h + 1], Sb_[:, :], AX.X), [Sb_.b], [ast.b])
            S.op("dve", lambda e, h=h: e.tensor_tensor(ast[:, 16 + h:17 + h], ast[:, h:h + 1], snk[:nt, h:h + 1], ALU.max), [ast.b, snk.b], [ast.b])
            S.op("dve", lambda e, h=h: e.tensor_scalar(ast[:, 32 + h:33 + h], ast[:, 16 + h:17 + h], -1.0, None, ALU.mult), [ast.b], [ast.b])
            S.op("dve", lambda e, h=h: e.tensor_tensor(ast[:, 64 + h:65 + h], snk[:nt, h:h + 1], ast[:, 16 + h:17 + h], ALU.subtract), [ast.b, snk.b], [ast.b])
            S.op("act", lambda e, h=h: e.activation(ast[:, 64 + h:65 + h], ast[:, 64 + h:65 + h], AF.Exp), [ast.b], [ast.b])
            S.op("act", lambda e, h=h: e.activation(Pb_[:, :], Sb_[:, :], AF.Exp, bias=ast[:, 32 + h:33 + h], scale=1.0, accum_out=ast[:, 48 + h:49 + h]),
                 [Sb_.b, ast.b], [Pb_.b, ast.b])
            ptb = PB()
            S.op("pe", lambda e, ptb=ptb: e.transpose(ptb[:, 0:nt], Pb_[:, 0:128], identb[:nt, :nt]), [Pb_.b, identb.b], [ptb.b], inc=False)
            S.op("pe", lambda e, ptb=ptb: e.transpose(ptb[0:nt, 64:64 + nt], Pb_[:, 128:192], identb[:nt, :nt]), [Pb_.b, identb.b], [ptb.b])
            S.op("act", lambda e, ptb=ptb: e.activation(PTc[:, :], ptb[:, 0:nt], AF.Identity), [ptb.b], [PTc.b])
            S.op("act", lambda e, ptb=ptb: e.activation(PTn[:, :], ptb[0:nt, 64:64 + nt], AF.Identity), [ptb.b], [PTn.b])
            S.op("dve", lambda e: e.tensor_tensor(PTm[:], PTc[:, :].unsqueeze(1).broadcast_to([128, NSEQ_S, NTS]), smaskb[:], ALU.mult),
                 [PTc.b, smaskb.b], [PTm.b])
            pod = po[h // 8]
            oc0 = (h % 8) * 64
            for b in range(NSEQ_S):
                S.op("pe", lambda e, b=b, jj=jj, pod=pod, oc0=oc0: e.matmul(pod[:nt, oc0:oc0 + 64], PTm[:, b, :], cvb[:, b, jj * 64:(jj + 1) * 64],
                                                                           start=(b == 0), stop=False), [PTm.b, cvb.b], [pod.b], inc=False)
            S.op("pe", lambda e, jj=jj, pod=pod, oc0=oc0: e.matmul(pod[:nt, oc0:oc0 + 64], PTn[:, :], kvS[:, 256 + jj * 64:256 + (jj + 1) * 64],
                                                                  start=False, stop=True), [PTn.b, kvS.b], [pod.b])
        S.op("dve", lambda e: e.tensor_tensor(ast[:, 80:96], ast[:, 48:64], ast[:, 64:80], ALU.add), [ast.b], [ast.b])
        S.op("dve", lambda e: e.reciprocal(ast[:, 80:96], ast[:, 80:96]), [ast.b], [ast.b])
        for g in range(2):
            S.op("dve", lambda e, g=g: e.tensor_tensor(
                hab[:, g * 512:(g + 1) * 512].rearrange("p (h d) -> p h d", h=8), po[g][:nt, :].rearrange("p (h d) -> p h d", h=8),
                ast[:, 80 + 8 * g:88 + 8 * g].unsqueeze(2).broadcast_to([NTS, 8, 64]), ALU.mult), [po[g].b, ast.b], [hab.b])
        ptb = PB()
        for k in range(KC):
            S.op("pe", lambda e, k=k, ptb=ptb: e.transpose(ptb[:, k * 64:(k + 1) * 64], hab[:, k * 128:(k + 1) * 128], identb[:nt, :nt]),
                 [hab.b, identb.b], [ptb.b], inc=(k == KC - 1))
        S.op("act", lambda e, ptb=ptb: e.activation(haT[:, :, c0:c0 + nt], ptb[:, 0:512].rearrange("p (k t) -> p k t", k=KC), AF.Identity),
             [ptb.b], [haT.b])

    outs = []
    try:
        if NSCAN > 0:
            with contextlib.ExitStack() as ph:
                Wkv = sb("Wkv", [128, KC, 1544], BF16, ph)
                load_w(Wkv, w_in, 512, 2056)
                bkv = bias_hilo("bkv", b_in, 512, 2056, ph)
                hTs = [sb("hTs%d" % i, [128, KC, 128], BF16, ph) for i in range(2)]
                mrow1 = sb("mrow_s", [128, 2, D], F32, ph)
                S.dma("sp", mrow1[:], modR_scr[:, 0:2, :], reads=[modR_scr.b], writes=[mrow1.b])
                xn2 = [xn, sb("xn_b", [128, D], F32, ph)]
                lnt2 = [lnt, sb("lnt_b", [128, 20], F32, ph)]
                gt2 = [gt, sb("gt_b", [128, 40], F32, ph)]
                ks2 = [ks, sb("ks_b", [128, 4, 128], BF16, ph)]
                v12 = [v1, sb("v1_b", [128, 4, 257], BF16, ph)]
                S.op("pool", lambda e: e.memset(v12[1][:], 1.0), [], [v12[1].b])

                xbs = {}

                def stA1(i):
                    xt = XT()
                    S.dma("sp", xt[:], xs[i * 128:(i + 1) * 128, :], writes=[xt.b])
                    xbs[i] = ln_rows_part1(xt, 128, xn2[i % 2], lnt2[i % 2], mrow1)

                def stA2(i):
                    xb_to_hT(xbs.pop(i), 128, hTs[i % 2], 0)
                    pg = PF()
                    mm_tok(pg, 0, 8, hTs[i % 2], 0, 128, Wkv, 1536, bkv, 1536)
                    mlstm_gates(pg, 128, i, gt2[i % 2])

                def stB(i):
                    pk, pv0, pv1 = PF(), PF(), PF()
                    mm_tok(pk, 0, 512, hTs[i % 2], 0, 128, Wkv, 0, bkv, 0)
                    mm_tok(pv0, 0, 512, hTs[i % 2], 0, 128, Wkv, 512, bkv, 512)
                    mm_tok(pv1, 0, 512, hTs[i % 2], 0, 128, Wkv, 1024, bkv, 1024)
                    kv_from_psum(pk, pv0, pv1, 128, ks2[i % 2], v12[i % 2], gt2[i % 2])

                def stC(i):
                    state_update(ks2[i % 2], v12[i % 2], gt2[i % 2], 128)
                    m_update(gt2[i % 2], 128)

                stA1(0)
                stA2(0)
                if NSCAN > 1:
                    stA1(1)
                stB(0)
                for ba in range(NSCAN):
                    if ba + 2 < NSCAN:
                        stA1(ba + 2)
                    if ba + 1 < NSCAN:
                        stA2(ba + 1)
                    stC(ba)
                    if ba + 1 < NSCAN:
                        stB(ba + 1)
            S.phase_end()

        own_blocks = [("h2", NSCAN, 128), ("h1", NSCAN + 1, 128)] + [("own", NSCAN + 2 + i, 128) for i in range(NOWN)]
        npass = 2 if NOWN >= 8 else 1
        per = (len(own_blocks) + npass - 1) // npass
        passes = [own_blocks[i * per:(i + 1) * per] for i in range(npass)]
        if cfg.sample:
            passes.append([("samp", -1, NTS)])
        NTPMAX = max(sum(b[2] for b in p) for p in passes)
        NBPMAX = max(len(p) for p in passes)

        kd_carry = sb("kd_carry", [64, 4, 128], BF16)
        kv_carry = sb("kv_carry", [128, 512], BF16)
        cv_carry = sb("cv_carry", [128, 2 * FC, 2])
        S.op("pool", lambda e: e.memset(kd_carry[:], 0.0), [], [kd_carry.b])
        S.op("pool", lambda e: e.memset(kv_carry[:], 0.0), [], [kv_carry.b])
        S.op("pool", lambda e: e.memset(cv_carry[:], 0.0), [], [cv_carry.b])
        cvv = sb("cvv", [128, 1])
        S.dma("sp", cvv[:], convvalid[:, :], writes=[cvv.b])
        x1row = {"n": 0}

        for pi, blocks in enumerate(passes):
            offs = []
            o = 0
            for b in blocks:
                offs.append(o)
                o += b[2]
            NTP = o
            NBP = len(blocks)
            prompt_passes = [i for i, p_ in enumerate(passes) if any(b_[0] != "samp" for b_ in p_)]
            blocks_last_prompt_pass = (pi == prompt_passes[-1])
            with contextlib.ExitStack() as pp:
                hT = sb("hT", [128, KC, NTP], BF16, pp)
                if any(b_[0] == "samp" for b_ in blocks):
                    SAMP["modFs"] = sb("modFs", [128, 32, NTS], F32, pp)
                    SAMP["gS"] = sb("gS", [NTS, 2, D], F32, pp)
                    S.dma("sp", SAMP["modFs"][:], modFs_scr[:, :, :], reads=[modFs_scr.b], writes=[SAMP["modFs"].b])
                    S.dma("sp", SAMP["gS"][:], gS_scr[:, :, :], reads=[gS_scr.b], writes=[SAMP["gS"].b])
                pm_ = pp.enter_context(contextlib.ExitStack())
                hmT = sb("hmT", [128, KC, NTP], BF16, pm_)
                haT = sb("haT", [128, KC, NTP], BF16, pm_)
                S.op("pool", lambda e: e.memset(hmT[:], 0.0), [], [hmT.b])
                S.op("pool", lambda e: e.memset(haT[:], 0.0), [], [haT.b])
                phB = pm_.enter_context(contextlib.ExitStack())
                Wm = sb("Wm", [128, KC, 2056], BF16, phB)
                Wog = sb("Wog", [128, KC, 1024], BF16, phB)
                load_w(Wm, w_in, 0, 2056)
                load_w(Wog, w_in, 2056, 3080)
                bm = bias_hilo("bm", b_in, 0, 2056, phB)
                with contextlib.ExitStack() as ph:
                    mrowA = sb("mrowA", [128, 2, D], F32, ph)
                    S.dma("sp", mrowA[:], modR_scr[:, 0:2, :], reads=[modR_scr.b], writes=[mrowA.b])
                    xnA = [xn, sb("xnA", [128, D], F32, ph)]
                    lntA = [lnt, sb("lntA", [128, 20], F32, ph)]
                    pend = None
                    for bi, (kind, ba, nt) in enumerate(blocks):
                        xt = XT()
                        if kind == "samp":
                            S.dma("sp", xt[:nt, :], xsamp[:, :], writes=[xt.b])
                            ln_to_hT(xt, nt, hT, offs[bi], 0, True, xn, lnt, None)
                            continue
                        S.dma("sp", xt[:], xs[ba * 128:(ba + 1) * 128, :], writes=[xt.b])
                        xb_ = ln_rows_part1(xt, nt, xnA[bi % 2], lntA[bi % 2], mrowA)
                        if pend is not None:
                            xb_to_hT(pend[0], 128, hT, pend[1])
                        pend = (xb_, offs[bi])
                    if pend is not None:
                        xb_to_hT(pend[0], 128, hT, pend[1])
                S.phase_end()

                if True:
                    ph = phB
                    sgog = sb("sgog", [128, KC, NTP], BF16, ph)
                    binT = sb("binT", [128, 32], F32, ph)
                    nwT = sb("nwT", [128, KC], F32, ph)
                    S.dma("sp", binT[:], b_inT[:, :], writes=[binT.b])
                    S.dma("sp", nwT[:], norm_wT[:, :], writes=[nwT.b])
                    qs = sb("qs", [128, 4, 128], BF16, ph)
                    qkT = sb("qkT", [128, 8, 128], BF16, ph)
                    SmT = sb("SmT", [128, 4, 128], BF16, ph)
                    hmn = sb("hmn", [128, D], F32, ph)
                    hst = sb("hst", [128, 64], F32, ph)
                    t0 = 0
                    while t0 < NTP:
                        n = min(512, NTP - t0)
                        for oc in range(KC):
                            pst = PF()
                            for k in range(KC):
                                S.op("pe", lambda e, k=k, oc=oc, t0=t0, n=n: e.matmul(
                                    pst[:, 0:n], Wog[:, k, oc * 128:(oc + 1) * 128], hT[:, k, t0:t0 + n],
                                    start=(k == 0), stop=(k == KC - 1)), [Wog.b, hT.b], [pst.b], inc=(k == KC - 1))
                            S.op("act", lambda e, oc=oc, t0=t0, n=n: e.activation(
                                sgog[:, oc, t0:t0 + n], pst[:, 0:n], AF.Sigmoid, bias=binT[:, oc:oc + 1], scale=1.0),
                                [pst.b, binT.b], [sgog.b])
                        t0 += n
                    for bi, (kind, ba, nt) in enumerate(blocks):
                        c0 = offs[bi]
                        if kind == "h2":
                            scan_block(hT, c0, ba, Wm, 512, bm)
                            continue
                        if kind == "samp":
                            samp_mlstm(hT, c0, Wm, bm, hmT, sgog, nwT, ph)
                            continue
                        pg = PF()
                        mm_tok(pg, 0, 8, hT, c0, 128, Wm, 2048, bm, 2048)
                        mlstm_gates(pg, 128, ba, gt)
                        pq = PF()
                        mm_tok(pq, 0, 512, hT, c0, 128, Wm, 0, bm, 0)
                        for h in range(4):
                            S.op("dve", lambda e, h=h, pq=pq: e.tensor_scalar(qs[:, h, :], pq[:, h * 128:(h + 1) * 128], gt[:, 16 + h:17 + h],
                                                                              128.0 ** -0.5, ALU.mult, ALU.mult), [pq.b, gt.b], [qs.b])
                        pk, pv0, pv1 = PF(), PF(), PF()
                        mm_tok(pk, 0, 512, hT, c0, 128, Wm, 512, bm, 512)
                        mm_tok(pv0, 0, 512, hT, c0, 128, Wm, 1024, bm, 1024)
                        mm_tok(pv1, 0, 512, hT, c0, 128, Wm, 1536, bm, 1536)
                        kv_from_psum(pk, pv0, pv1, 128)
                        ptb = PB()
                        for h in range(4):
                            S.op("pe", lambda e, h=h: e.transpose(ptb[:, h * 128:(h + 1) * 128], qs[:, h, :], identb[:, :]),
                                 [qs.b, identb.b], [ptb.b], inc=False)
                        for h in range(4):
                            S.op("pe", lambda e, h=h: e.transpose(ptb[:, 512 + h * 128:512 + (h + 1) * 128], ks[:, h, :], identb[:, :]),
                                 [ks.b, identb.b], [ptb.b], inc=(h == 3))
                        S.op("act", lambda e: e.activation(qkT[:].rearrange("p a b -> p (a b)"), ptb[:, :], AF.Identity), [ptb.b], [qkT.b])
                        pS = PF()
                        for h in range(4):
                            S.op("pe", lambda e, h=h: e.matmul(pS[:, h * 128:(h + 1) * 128], qkT[:, 4 + h, :], qkT[:, h, :], start=True, stop=True),
                                 [qkT.b], [pS.b], inc=(h == 3))
                        S.op("dve", lambda e: e.tensor_tensor(SmT[:], pS[:, :].rearrange("p (h t) -> p h t", h=4),
                                                              tri[:, :].unsqueeze(1).broadcast_to([128, 4, 128]), ALU.mult),
                             [pS.b, tri.b], [SmT.b])
                        pnum = [PL(0), PL(1)]
                        for h in range(4):
                            pn = pnum[h // 2]
                            cs = (h % 2) * 256
                            S.op("pe", lambda e, h=h, pn=pn, cs=cs: e.matmul(pn[:, cs:cs + 256], SmT[:, h, :], v1[:, h, 0:256], start=True, stop=False),
                                 [SmT.b, v1.b], [pn.b], inc=False)
                            S.op("pe", lambda e, h=h, pn=pn, cs=cs: e.matmul(pn[:, cs:cs + 256], qkT[:, h, :], CnTb[:, h, 0:256], start=False, stop=True),
                                 [qkT.b, CnTb.b], [pn.b], inc=True)
                        pden = PF()
                        for h in range(4):
                            S.op("pe", lambda e, h=h: e.matmul(pden[:, h:h + 1], SmT[:, h, :], ones_col[:, 0:1], start=True, stop=False),
                                 [SmT.b, ones_col.b], [pden.b], inc=False)
                            S.op("pe", lambda e, h=h: e.matmul(pden[:, h:h + 1], qkT[:, h, :], CnTb[:, h, 256:257], start=False, stop=True),
                                 [qkT.b, CnTb.b], [pden.b], inc=True)
                        head_ln(pnum, pden, hst, hmn, 128)
                        state_update(ks, v1, gt, 128)
                        m_update(gt, 128)
                        hm_transpose_out(hmn, 128, hmT, c0, nwT, sgog)
                phB.close()
                S.phase_end()
                if debug and pi == 0:
                    dbg["hmT"] = dout("dbg_hmT", [128, KC, NTP], BF16)
                    S.dma("sp", dbg["hmT"][:, :, :], hmT[:], reads=[hmT.b], writes=[dbg["hmT"].b])
                    dbg["hT"] = dout("dbg_hT", [128, KC, NTP], BF16)
                    S.dma("sp", dbg["hT"][:, :, :], hT[:], reads=[hT.b], writes=[dbg["hT"].b])
                    dbg["CnT"] = dout("dbg_CnT", [128, 4, 257])
                    S.dma("sp", dbg["CnT"][:, :, :], CnT[:], reads=[CnT.b], writes=[dbg["CnT"].b])
                if stop_after == "B":
                    pm_.close()
                    break
                with contextlib.ExitStack() as ph:
                    Wqa = sb("Wqa", [128, KC, 1024], BF16, ph)
                    Wkd = sb("Wkd", [128, KC, 256], BF16, ph)
                    Wkv2 = sb("Wkv2", [128, KC, 512], BF16, ph)
                    load_w(Wqa, w_in, 3080, 4104)
                    load_w(Wkd, w_in, 4104, 4360)
                    load_w(Wkv2, w_in, 4104, 4616)
                    bkv2 = bias_hilo("bkv2", b_in, 4104, 4616, ph)
                    binT = sb("binTc", [128, 32], F32, ph)
                    bqk = sb("bqk", [64, 20], F32, ph)
                    swab = sb("swab", [128, 16, 256], F32, ph)
                    fbias = sb("fbias", [128, 128], F32, ph)
                    snk = sb("snk", [128, 16], F32, ph)
                    S.dma("sp", binT[:], b_inT[:, :], writes=[binT.b])
                    S.dma("sp", bqk[:], b_qk64[:, :], writes=[bqk.b])
                    S.dma("sp", swab[:], c_swab[:, :, :], writes=[swab.b])
                    S.dma("sp", fbias[:], first_bias[:, :], writes=[fbias.b])
                    S.dma("sp", snk[:], sinks_rep[:, :], writes=[snk.b])
                    qaT = sb("qaT", [64, 16, 128], BF16, ph)
                    kdT = sb("kdT", [64, 4, 128 + NTP], BF16, ph)
                    kvA = sb("kvA", [128, NBP + 1, 512], BF16, ph)
                    kvf = sb("kvf", [128, 512], F32, ph)
                    Sb = [sb("Sb%d" % i, [128, 2, 256], F32, ph) for i in range(3)]
                    Pb = [sb("Pb%d" % i, [128, 256], BF16, ph) for i in range(6)]
                    astl = [sb("ast%d" % i, [128, 12], F32, ph) for i in range(3)]
                    PT2 = [sb("PT2_%d" % i, [128, 4, 128], BF16, ph) for i in range(2)]
                    ast = sb("ast", [128, 96], F32, ph)
                    hab = sb("hab", [128, D], BF16, ph)
                    S.op("dve", lambda e: e.tensor_copy(kdT[:, :, 0:128], kd_carry[:]), [kd_carry.b], [kdT.b])
                    S.op("dve", lambda e: e.tensor_copy(kvA[:, 0, :], kv_carry[:]), [kv_carry.b], [kvA.b])
                    nprompt = sum(b[2] for b in blocks if b[0] != "samp")
                    t0 = 0
                    while t0 < nprompt:
                        n = min(512, nprompt - t0)
                        for jj in range(4):
                            pst = PF()
                            for k in range(KC):
                                S.op("pe", lambda e, k=k, jj=jj, t0=t0, n=n, pst=pst: e.matmul(
                                    pst[0:64, 0:n], Wkd[:, k, jj * 64:(jj + 1) * 64], hT[:, k, t0:t0 + n],
                                    start=(k == 0), stop=(k == KC - 1)), [Wkd.b, hT.b], [pst.b], inc=(k == KC - 1))
                            S.op("act", lambda e, jj=jj, t0=t0, n=n, pst=pst: e.activation(
                                kdT[:, jj, 128 + t0:128 + t0 + n], pst[0:64, 0:n], AF.Identity, bias=bqk[:, 16 + jj:17 + jj], scale=1.0),
                                [pst.b, bqk.b], [kdT.b])
                        t0 += n
                    S.checkpoint("C1")
                    last_prompt_bi = max([bi for bi, b in enumerate(blocks) if b[0] != "samp"] + [-1])
                    for bi, (kind, ba, nt) in enumerate(blocks):
                        if kind == "samp":
                            continue
                        pst = PF()
                        mm_tok(pst, 0, 512, hT, offs[bi], 128, Wkv2, 0, bkv2, 0)
                        S.op("act", lambda e, bi=bi, pst=pst: e.activation(kvA[:, bi + 1, :], pst[:, :], AF.Identity), [pst.b], [kvA.b])
                        if blocks_last_prompt_pass and bi == last_prompt_bi:
                            S.op("dve", lambda e, pst=pst: e.tensor_copy(kvf[:], pst[:, :]), [pst.b], [kvf.b])
                            S.dma("sp", pkv_out[:, :], kvf[:], reads=[kvf.b], writes=[pkv_out.b])
                    S.checkpoint("C2a")
                    if nprompt > 0:
                        S.op("dve", lambda e: e.tensor_copy(kd_carry[:], kdT[:, :, nprompt:nprompt + 128]), [kdT.b], [kd_carry.b])
                        S.op("dve", lambda e: e.tensor_copy(kv_carry[:], kvA[:, last_prompt_bi + 1, :]), [kvA.b], [kv_carry.b])
                    S.checkpoint("C2")
                    for bi, (kind, ba, nt) in enumerate(blocks):
                        c0 = offs[bi]
                        if kind == "h2":
                            continue
                        if kind == "samp":
                            samp_swa(hT, c0, Wqa, Wkd, Wkv2, bkv2, bqk, snk, haT, ph)
                            continue
                        first_own = (kind == "own" and ba == NSCAN + 2)
                        for h in range(16):
                            pst = PF()
                            for k in range(KC):
                                S.op("pe", lambda e, k=k, h=h, pst=pst: e.matmul(
                                    pst[0:64, 0:128], Wqa[:, k, h * 64:(h + 1) * 64], hT[:, k, c0:c0 + 128],
                                    start=(k == 0), stop=(k == KC - 1)), [Wqa.b, hT.b], [pst.b], inc=(k == KC - 1))
                            S.op("dve", lambda e, h=h, pst=pst: e.tensor_scalar(
                                qaT[:, h, :], pst[0:64, 0:128], bqk[:, h:h + 1], None, ALU.add),
                                [pst.b, bqk.b], [qaT.b])
                        po = [PL(0), PL(1)]
                        def stage1a(hp):
                            sbt = Sb[hp % 3]
                            a_ = astl[hp % 3]
                            for hh in range(2):
                                h = 2 * hp + hh
                                jj = h // 4
                                pS = PF()
                                S.op("pe", lambda e, h=h, jj=jj, pS=pS: e.matmul(
                                    pS[:, 0:256], qaT[:, h, :], kdT[:, jj, c0:c0 + 256],
                                    start=True, stop=True), [qaT.b, kdT.b], [pS.b])
                                S.op("dve", lambda e, h=h, hh=hh, sbt=sbt, pS=pS: e.scalar_tensor_tensor(
                                    sbt[:, hh, :], pS[:, 0:256], 0.125, swab[:, h, :], ALU.mult, ALU.add), [pS.b, swab.b], [sbt.b])
                            if first_own:
                                S.op("dve", lambda e, sbt=sbt: e.tensor_tensor(
                                    sbt[:, :, 0:128], sbt[:, :, 0:128], fbias[:, :].unsqueeze(1).broadcast_to([128, 2, 128]), ALU.add),
                                    [sbt.b, fbias.b], [sbt.b])
                            S.op("dve", lambda e, sbt=sbt, a_=a_: e.reduce_max(a_[:, 0:2], sbt[:], AX.X), [sbt.b], [a_.b])
                            S.op("dve", lambda e, hp=hp, a_=a_: e.tensor_tensor(a_[:, 2:4], a_[:, 0:2], snk[:, 2 * hp:2 * hp + 2], ALU.max), [a_.b, snk.b], [a_.b])
                            S.op("dve", lambda e, a_=a_: e.tensor_scalar(a_[:, 4:6], a_[:, 2:4], -1.0, None, ALU.mult), [a_.b], [a_.b])
                            S.op("dve", lambda e, hp=hp, a_=a_: e.tensor_tensor(a_[:, 6:8], snk[:, 2 * hp:2 * hp + 2], a_[:, 2:4], ALU.subtract), [a_.b, snk.b], [a_.b])

                        def stage1b(hp):
                            sbt = Sb[hp % 3]
                            a_ = astl[hp % 3]
                            S.op("act", lambda e, hp=hp, a_=a_: e.activation(ast[:, 64 + 2 * hp:66 + 2 * hp], a_[:, 6:8], AF.Exp), [a_.b], [ast.b])
                            for hh in range(2):
                                h = 2 * hp + hh
                                pb_ = Pb[(hp % 3) * 2 + hh]
                                S.op("act", lambda e, h=h, hh=hh, sbt=sbt, pb_=pb_, a_=a_: e.activation(
                                    pb_[:], sbt[:, hh, :], AF.Exp, bias=a_[:, 4 + hh:5 + hh], scale=1.0, accum_out=ast[:, 48 + h:49 + h]),
                                    [sbt.b, a_.b], [pb_.b, ast.b])

                        def stage2(hp):
                            pt_ = PT2[hp % 2]
                            ptb = PB()
                            for hh in range(2):
                                pb_ = Pb[(hp % 3) * 2 + hh]
                                for half in range(2):
                                    q_ = hh * 2 + half
                                    S.op("pe", lambda e, half=half, pb_=pb_, ptb=ptb, q_=q_: e.transpose(
                                        ptb[:, q_ * 128:(q_ + 1) * 128], pb_[:, half * 128:(half + 1) * 128], identb[:, :]),
                                        [pb_.b, identb.b], [ptb.b], inc=(q_ == 3))
                            S.op("act", lambda e, pt_=pt_, ptb=ptb: e.activation(pt_[:].rearrange("p a b -> p (a b)"), ptb[:, 0:512], AF.Identity),
                                 [ptb.b], [pt_.b])
                            for hh in range(2):
                                h = 2 * hp + hh
                                jj = h // 4
                                pod = po[h // 8]
                                oc0 = (h % 8) * 64
                                for half in range(2):
                                    S.op("pe", lambda e, half=half, hh=hh, pt_=pt_, pod=pod, oc0=oc0, jj=jj: e.matmul(
                                        pod[:, oc0:oc0 + 64], pt_[:, hh * 2 + half, :], kvA[:, bi + half, 256 + jj * 64:256 + (jj + 1) * 64],
                                        start=(half == 0), stop=(half == 1)), [pt_.b, kvA.b], [pod.b], inc=(half == 1))

                        stage1a(0)
                        stage1b(0)
                        stage1a(1)
                        for hp in range(8):
                            if hp + 2 < 8:
                                stage1a(hp + 2)
                            stage2(hp)
                            if hp + 1 < 8:
                                stage1b(hp + 1)
                        S.op("dve", lambda e: e.tensor_tensor(ast[:, 80:96], ast[:, 48:64], ast[:, 64:80], ALU.add), [ast.b], [ast.b])
                        S.op("dve", lambda e: e.reciprocal(ast[:, 80:96], ast[:, 80:96]), [ast.b], [ast.b])
                        for g in range(2):
                            S.op("dve", lambda e, g=g: e.tensor_tensor(
                                hab[:, g * 512:(g + 1) * 512].rearrange("p (h d) -> p h d", h=8), po[g][:, :].rearrange("p (h d) -> p h d", h=8),
                                ast[:, 80 + 8 * g:88 + 8 * g].unsqueeze(2).broadcast_to([128, 8, 64]), ALU.mult), [po[g].b, ast.b], [hab.b])
                        ptb = PB()
                        for k in range(KC):
                            S.op("pe", lambda e, k=k, ptb=ptb: e.transpose(ptb[:, k * 128:(k + 1) * 128], hab[:, k * 128:(k + 1) * 128], identb[:, :]),
                                 [hab.b, identb.b], [ptb.b], inc=(k == KC - 1))
                        S.op("act", lambda e, ptb=ptb: e.activation(haT[:, :, c0:c0 + 128], ptb[:, :].rearrange("p (k t) -> p k t", k=KC), AF.Identity),
                             [ptb.b], [haT.b])
                    if debug and pi == 0 and stop_after == "C":
                        for nm, tt, shp, dt_ in [("qaT", qaT, [64, 16, 128], BF16), ("kdT", kdT, [64, 4, 128 + NTP], BF16),
                                                 ("kvA", kvA, [128, NBP + 1, 512], BF16), ("ast", ast, [128, 96], F32),
                                                 ("Sb0", Sb[0], [128, 2, 256], F32), ("Sb1", Sb[1], [128, 2, 256], F32), ("hab", hab, [128, D], BF16)]:
                            dbg[nm] = dout("dbg_" + nm, shp, dt_)
                            S.dma("sp", dbg[nm][:], tt[:], reads=[tt.b], writes=[dbg[nm].b])
                S.phase_end()
                if debug and pi == 0:
                    dbg["haT"] = dout("dbg_haT", [128, KC, NTP], BF16)
                    S.dma("sp", dbg["haT"][:, :, :], haT[:], reads=[haT.b], writes=[dbg["haT"].b])
                if stop_after == "C":
                    pm_.close()
                    break

                first_tok = offs[1] if blocks[0][0] == "h2" else 0
                x1rows = {}
                with contextlib.ExitStack() as ph:
                    Wgm = sb("Wgm", [128, KC, 1024], BF16, ph)
                    Wga = sb("Wga", [128, KC, 1024], BF16, ph)
                    Wbm = sb("Wbm", [128, KC, 1024], BF16, ph)
                    Wba = sb("Wba", [128, KC, 1024], BF16, ph)
                    Wout = sb("Wout", [128, KC, 1024], BF16, ph)
                    load_w(Wgm, w_in, 4616, 5640)
                    load_w(Wga, w_in, 5640, 6664)
                    load_w(Wbm, w_bm, 0, 1024)
                    load_w(Wba, w_ba, 0, 1024)
                    load_w(Wout, w_out, 0, 1024)
                    binT = sb("binTd", [128, 32], F32, ph)
                    S.dma("sp", binT[:], b_inT[:, :], writes=[binT.b])
                    l1g = sb("l1g", [128, D], F32, ph)
                    l1b = sb("l1b", [128, D], F32, ph)
                    S.dma("sp", l1g[:], ln1_g[0:1, :].broadcast_to([128, D]), writes=[l1g.b])
                    S.dma("sp", l1b[:], ln1_b[0:1, :].broadcast_to([128, D]), writes=[l1b.b])
                    mgT = sb("mgT", [128, KC, 512], BF16, ph)
                    sgm = sb("sgm", [128, 512], F32, ph)
                    sga = sb("sga", [128, 512], F32, ph)
                    t1 = sgm
                    t2 = sga
                    pre = xn
                    x1t = sb("x1t", [128, D], F32, ph)
                    t0 = first_tok
                    while t0 < NTP:
                        n = min(512, NTP - t0)
                        for oc in range(KC):
                            for (Wg, Wb, src, sg, tt, bofs) in [(Wgm, Wbm, hmT, sgm, t1, 16), (Wga, Wba, haT, sga, t2, 24)]:
                                pg_ = PF()
                                for k in range(KC):
                                    S.op("pe", lambda e, k=k, Wg=Wg, pg_=pg_: e.matmul(pg_[:, 0:n], Wg[:, k, oc * 128:(oc + 1) * 128], hT[:, k, t0:t0 + n],
                                                                                       start=(k == 0), stop=(k == KC - 1)), [Wg.b, hT.b], [pg_.b], inc=(k == KC - 1))
                                S.op("act", lambda e, sg=sg, pg_=pg_, bofs=bofs: e.activation(sg[:, 0:n], pg_[:, 0:n], AF.Sigmoid,
                                                                                               bias=binT[:, bofs + oc:bofs + oc + 1], scale=1.0),
                                     [pg_.b, binT.b], [sg.b])
                                pb2 = PF()
                                for k in range(KC):
                                    S.op("pe", lambda e, k=k, Wb=Wb, src=src, pb2=pb2: e.matmul(pb2[:, 0:n], Wb[:, k, oc * 128:(oc + 1) * 128], src[:, k, t0:t0 + n],
                                                                                                start=(k == 0), stop=(k == KC - 1)), [Wb.b, src.b], [pb2.b], inc=(k == KC - 1))
                                S.op("dve", lambda e, sg=sg, tt=tt, pb2=pb2: e.tensor_tensor(tt[:, 0:n], pb2[:, 0:n], sg[:, 0:n], ALU.mult), [pb2.b, sg.b], [tt.b])
                            S.op("pool", lambda e, oc=oc: e.tensor_tensor(mgT[:, oc, 0:n], t1[:, 0:n], t2[:, 0:n], ALU.add), [t1.b, t2.b], [mgT.b])
                        for bi, (kind, ba, nt) in enumerate(blocks):
                            c0 = offs[bi]
                            if c0 < t0 or c0 >= t0 + n:
                                continue
                            samp = kind == "samp"
                            xt = XT()
                            if samp:
                                S.dma("sp", xt[:nt, :], xsamp[:, :], writes=[xt.b])
                            else:
                                S.dma("sp", xt[:], xs[ba * 128:(ba + 1) * 128, :], writes=[xt.b])
                            g1 = SAMP["gS"] if samp else gP
                            for half in range(2):
                                py = PF()
                                mm_tok(py, 0, 512, mgT, c0 - t0, nt, Wout, half * 512, None, 0)
                                S.op("dve", lambda e, py=py, half=half, g1=g1: e.tensor_tensor(
                                    pre[:nt, half * 512:(half + 1) * 512], py[:nt, :], g1[:nt, 0, half * 512:(half + 1) * 512], ALU.mult),
                                    [py.b, g1.b], [pre.b])
                            S.op("dve", lambda e, xt=xt: e.scalar_tensor_tensor(pre[:nt, :], xt[:nt, :], ALPHA, pre[:nt, :], ALU.mult, ALU.add),
                                 [xt.b, pre.b], [pre.b])
                            rstd, nmr = ln_stats(pre, nt, lnt)
                            S.op("act", lambda e, rstd=rstd, nmr=nmr: e.activation(x1t[:nt, :], pre[:nt, :], AF.Identity, bias=nmr, scale=rstd),
                                 [pre.b, lnt.b], [x1t.b])
                            S.op("dve", lambda e: e.tensor_tensor(x1t[:nt, :], x1t[:nt, :], l1g[:nt, :], ALU.mult), [x1t.b, l1g.b], [x1t.b])
                            S.op("pool", lambda e: e.tensor_tensor(x1t[:nt, :], x1t[:nt, :], l1b[:nt, :], ALU.add), [x1t.b, l1b.b], [x1t.b])
                            r0 = x1row["n"]
                            x1row["n"] += nt
                            x1rows[bi] = r0
                            S.dma("sp", x1_scr[r0:r0 + nt, :], x1t[:nt, :], reads=[x1t.b], writes=[x1_scr.b])
                            ln_to_hT(x1t, nt, hT, c0, 2, samp, xn, lnt)
                        t0 += n
                S.phase_end()
                if debug and pi == 0:
                    dbg["h2T"] = dout("dbg_h2T", [128, KC, NTP], BF16)
                    S.dma("sp", dbg["h2T"][:, :, :], hT[:], reads=[hT.b], writes=[dbg["h2T"].b])
                pm_.close()
                S.phase_end()
                if stop_after == "D":
                    break

                nprompt = sum(b[2] for b in blocks if b[0] != "samp")
                tiles = []
                t0 = first_tok
                while t0 < nprompt:
                    n = min(512, nprompt - t0)
                    tiles.append((t0, n, False))
                    t0 += n
                if any(b[0] == "samp" for b in blocks):
                    tiles.append((nprompt, NTS, True))
                with contextlib.ExitStack() as pe_:
                    actT = sb("actT", [128, FC, NTP], BF16, pe_)
                    Wdn = sb("Wdn", [128, FC, D], BF16, pe_)
                    load_w(Wdn, w_down, 0, D, rows=DFF)
                    with contextlib.ExitStack() as ph:
                        Wupc = [sb("Wupc%d" % i, [128, KC, 256], BF16, ph) for i in range(3)]
                        bupT = sb("bupT", [128, 2 * FC], F32, ph)
                        cwT = sb("cwT", [128, 2 * FC, 3], F32, ph)
                        cbT = sb("cbT", [128, 2 * FC], F32, ph)
                        S.dma("sp", bupT[:], b_upT[:, :], writes=[bupT.b])
                        S.dma("sp", cwT[:], conv_wT[:, :, :], writes=[cwT.b])
                        S.dma("sp", cbT[:], conv_bT[:, :], writes=[cbT.b])
                        ue = [sb("ue%d" % i, [128, 520], F32, ph) for i in range(4)]
                        yc = [sb("yc%d" % i, [128, 512], F32, ph) for i in range(4)]
                        eit = {"n": 0}
                        if cfg.sample:
                            cst = sb("cst", [128, 2 * FC, 32], F32, ph)
                            S.dma("sp", cst[:], convst[:, :, :], writes=[cst.b])
                            cso = sb("cso", [128, 2 * FC, 32], F32, ph)
                        S.dma("pool", Wupc[0][:].rearrange("p k c -> p (k c)"), w_upr[0, :, :], writes=[Wupc[0].b])
                        for c in range(FC):
                            Wc = Wupc[c % 3]
                            if c + 1 < FC:
                                Wn_ = Wupc[(c + 1) % 3]
                                S.dma("pool", Wn_[:].rearrange("p k c -> p (k c)"), w_upr[c + 1, :, :], writes=[Wn_.b])
                            for (t0, n, samp) in tiles:
                                has_h1 = (not samp) and pi == 0 and t0 == first_tok
                                eit["n"] += 1
                                eb = (eit["n"] % 2) * 2
                                for part in range(2):
                                    ci = part * FC + c
                                    u_ = ue[eb + part]
                                    y_ = yc[eb + part]
                                    pu = PF()
                                    for k in range(KC):
                                        S.op("pe", lambda e, k=k, part=part, pu=pu, Wc=Wc, t0=t0, n=n: e.matmul(
                                            pu[:, 0:n], Wc[:, k, part * 128:(part + 1) * 128], hT[:, k, t0:t0 + n],
                                            start=(k == 0), stop=(k == KC - 1)), [Wc.b, hT.b], [pu.b], inc=(k == KC - 1))
                                    if not samp:
                                        S.op("pool", lambda e, u_=u_, ci=ci: e.tensor_copy(u_[:, 0:2], cv_carry[:, ci, :]), [cv_carry.b], [u_.b])
                                        S.op("act", lambda e, u_=u_, ci=ci, pu=pu, n=n: e.activation(u_[:, 2:2 + n], pu[:, 0:n], AF.Identity, bias=bupT[:, ci:ci + 1], scale=1.0),
                                             [pu.b, bupT.b], [u_.b])
                                        if has_h1:
                                            S.op("dve", lambda e, u_=u_: e.tensor_scalar(u_[:, 2:130], u_[:, 2:130], cvv[:, 0:1], None, ALU.mult), [u_.b, cvv.b], [u_.b])
                                        S.op("pool", lambda e, u_=u_, ci=ci, n=n: e.tensor_copy(cv_carry[:, ci, :], u_[:, n:n + 2]), [u_.b], [cv_carry.b])
                                        S.op("dve", lambda e, u_=u_, y_=y_, ci=ci, n=n: e.tensor_scalar(y_[:, 0:n], u_[:, 0:n], cwT[:, ci, 0:1], cbT[:, ci:ci + 1], ALU.mult, ALU.add),
                                             [u_.b, cwT.b, cbT.b], [y_.b])
                                        S.op("dve", lambda e, u_=u_, y_=y_, ci=ci, n=n: e.scalar_tensor_tensor(y_[:, 0:n], u_[:, 1:n + 1], cwT[:, ci, 1:2], y_[:, 0:n], ALU.mult, ALU.add),
                                             [u_.b, cwT.b], [y_.b])
                                        S.op("dve", lambda e, u_=u_, y_=y_, ci=ci, n=n: e.scalar_tensor_tensor(y_[:, 0:n], u_[:, 2:n + 2], cwT[:, ci, 2:3], y_[:, 0:n], ALU.mult, ALU.add),
                                             [u_.b, cwT.b], [y_.b])
                                    else:
                                        u3 = u_[:, 0:96].rearrange("p (b t) -> p b t", t=6)
                                        y3 = y_[:, 0:64].rearrange("p (b t) -> p b t", t=4)
                                        S.op("pool", lambda e, u3=u3, ci=ci: e.tensor_copy(u3[:, :, 0:2], cst[:, ci, :].rearrange("p (b t) -> p b t", t=2)), [cst.b], [u_.b])
                                        S.op("act", lambda e, u3=u3, ci=ci, pu=pu: e.activation(u3[:, :, 2:6], pu[:, 0:64].rearrange("p (b t) -> p b t", t=4), AF.Identity,
                                                                                                bias=bupT[:, ci:ci + 1], scale=1.0), [pu.b, bupT.b], [u_.b])
                                        S.op("pool", lambda e, u3=u3, ci=ci: e.tensor_copy(cso[:, ci, :].rearrange("p (b t) -> p b t", t=2), u3[:, :, 4:6]), [u_.b], [cso.b])
                                        S.op("dve", lambda e, u3=u3, y3=y3, ci=ci: e.tensor_scalar(y3, u3[:, :, 0:4], cwT[:, ci, 0:1], cbT[:, ci:ci + 1], ALU.mult, ALU.add),
                                             [u_.b, cwT.b, cbT.b], [y_.b])
                                        S.op("dve", lambda e, u3=u3, y3=y3, ci=ci: e.scalar_tensor_tensor(y3, u3[:, :, 1:5], cwT[:, ci, 1:2], y3, ALU.mult, ALU.add),
                                             [u_.b, cwT.b], [y_.b])
                                        S.op("dve", lambda e, u3=u3, y3=y3, ci=ci: e.scalar_tensor_tensor(y3, u3[:, :, 2:6], cwT[:, ci, 2:3], y3, ALU.mult, ALU.add),
                                             [u_.b, cwT.b], [y_.b])
                                ya_, yg_ = yc[eb], yc[eb + 1]
                                S.op("act", lambda e, n=n, ya_=ya_: e.activation(ya_[:, 0:n], ya_[:, 0:n], AF.Gelu_apprx_tanh), [ya_.b], [ya_.b])
                                S.op("dve", lambda e, c=c, t0=t0, n=n, ya_=ya_, yg_=yg_: e.tensor_tensor(actT[:, c, t0:t0 + n], ya_[:, 0:n], yg_[:, 0:n], ALU.mult),
                                     [ya_.b, yg_.b], [actT.b])
                        if pi == len(passes) - 1:
                            S.dma("sp", pconv_out[:, :, :], cv_carry[:], reads=[cv_carry.b], writes=[pconv_out.b])
                            if cfg.sample:
                                S.dma("sp", sconv_out[:, :, :], cso[:], reads=[cso.b], writes=[sconv_out.b])
                    S.phase_end()
                    with contextlib.ExitStack() as ph:
                        bdn = bias_hilo("bdn", b_down, 0, D, ph)
                        l2g = sb("l2g", [128, D], F32, ph)
                        l2b = sb("l2b", [128, D], F32, ph)
                        S.dma("sp", l2g[:], ln2_g[0:1, :].broadcast_to([128, D]), writes=[l2g.b])
                        S.dma("sp", l2b[:], ln2_b[0:1, :].broadcast_to([128, D]), writes=[l2b.b])
                        pre = sb("pre2", [128, D], F32, ph)
                        yt = sb("yt", [128, D], F32, ph)
                        for bi, (kind, ba, nt) in enumerate(blocks):
                            c0 = offs[bi]
                            if kind in ("h1", "h2"):
                                continue
                            samp = kind == "samp"
                            g2 = SAMP["gS"] if samp else gP
                            xt = XT()
                            r0 = x1rows[bi]
                            S.dma("sp", xt[:nt, :], x1_scr[r0:r0 + nt, :], reads=[x1_scr.b], writes=[xt.b])
                            for half in range(2):
                                pf_ = PF()
                                for k in range(FC):
                                    S.op("pe", lambda e, k=k, pf_=pf_, half=half: e.matmul(pf_[:nt, :], actT[:, k, c0:c0 + nt], Wdn[:, k, half * 512:(half + 1) * 512],
                                                                                           start=(k == 0), stop=False), [actT.b, Wdn.b], [pf_.b], inc=False)
                                S.op("pe", lambda e, pf_=pf_, half=half: e.matmul(pf_[:nt, :], onesb[0:64, 0:nt], bdn[:, half * 512:(half + 1) * 512], start=False, stop=True),
                                     [bdn.b, onesb.b], [pf_.b])
                                S.op("dve", lambda e, pf_=pf_, half=half, g2=g2: e.tensor_tensor(
                                    pre[:nt, half * 512:(half + 1) * 512], pf_[:nt, :], g2[:nt, 1, half * 512:(half + 1) * 512], ALU.mult), [pf_.b, g2.b], [pre.b])
                            S.op("dve", lambda e, xt=xt: e.scalar_tensor_tensor(pre[:nt, :], xt[:nt, :], ALPHA, pre[:nt, :], ALU.mult, ALU.add), [xt.b, pre.b], [pre.b])
                            rstd, nmr = ln_stats(pre, nt, lnt)
                            S.op("act", lambda e, rstd=rstd, nmr=nmr: e.activation(yt[:nt, :], pre[:nt, :], AF.Identity, bias=nmr, scale=rstd), [pre.b, lnt.b], [yt.b])
                            S.op("dve", lambda e: e.tensor_tensor(yt[:nt, :], yt[:nt, :], l2g[:nt, :], ALU.mult), [yt.b, l2g.b], [yt.b])
                            S.op("pool", lambda e: e.tensor_tensor(yt[:nt, :], yt[:nt, :], l2b[:nt, :], ALU.add), [yt.b, l2b.b], [yt.b])
                            if samp:
                                S.dma("sp", ys_out[:, :], yt[:nt, :], reads=[yt.b], writes=[ys_out.b])
                            else:
                                ob = ba - (NSCAN + 2)
                                S.dma("sp", y_out[ob * 128:(ob + 1) * 128, :], yt[:, :], reads=[yt.b], writes=[y_out.b])
                    S.phase_end()
            S.phase_end()

        pmd = sb("pmd", [4, 4])
        pen = sb("pen", [128, 4])
        pCs = sb("pCs", [128, 4, 257])
        S.op("dve", lambda e: e.tensor_scalar(pmd[:], ident[0:4, 0:4], mst[:, 0:1], None, ALU.mult), [ident.b, mst.b], [pmd.b])
        pst = PF()
        S.op("pe", lambda e: e.matmul(pst[:, 0:4], ones[0:4, 0:128], pmd[:, :], start=True, stop=True), [ones.b, pmd.b], [pst.b])
        S.op("act", lambda e: e.activation(pen[:], pst[:, 0:4], AF.Exp, scale=-1.0), [pst.b], [pen.b])
        for h in range(4):
            S.op("dve", lambda e, h=h: e.tensor_scalar(pCs[:, h, :], CnT[:, h, :], pen[:, h:h + 1], None, ALU.mult), [CnT.b, pen.b], [pCs.b])
        S.dma("sp", pC_out[:, :, :], pCs[:], reads=[pCs.b], writes=[pC_out.b])
        S.dma("sp", pm_out[:, :], mst[:], reads=[mst.b], writes=[pm_out.b])
        outs = [y_out, pC_out, pm_out, pkv_out, pconv_out]
        if cfg.sample:
            outs += [ys_out, sconv_out, sC_out, sm_out, sk_out, sv_out]

    except _StopBuild:
        outs = []

    return nc, S, st, locals()


def _fm(vec, nch):
    return np.ascontiguousarray(np.asarray(vec, np.float32).reshape(nch, 128).T)


def alibi_slopes():
    return np.exp2(-8.0 * np.arange(1, 17, dtype=np.float32) / 16).astype(np.float32)


def prep_shared(cfg, inp):
    sh = {}
    f = lambda a: np.ascontiguousarray(np.asarray(a, np.float32))
    b_in = f(inp["b_in"][0])
    w_in = f(inp["w_in"][0])
    sh["w_ada"] = f(inp["w_ada"][0]); sh["b_ada"] = f(inp["b_ada"]); sh["b_adaT"] = _fm(inp["b_ada"][0], 48)
    sh["w_in"] = w_in; sh["b_in"] = f(inp["b_in"])
    sh["b_inT"] = np.concatenate([_fm(b_in[2056:3080], 8), _fm(b_in[3080:4104], 8),
                                  _fm(b_in[4616:5640], 8), _fm(b_in[5640:6664], 8)], axis=1)
    sh["b_qk64"] = np.ascontiguousarray(np.concatenate([b_in[3080:4104].reshape(16, 64).T, b_in[4104:4360].reshape(4, 64).T], axis=1))
    sh["norm_wT"] = _fm(inp["mlstm_norm_w"][0], 8)
    sh["sinks_rep"] = np.ascontiguousarray(np.broadcast_to(f(inp["attn_sinks"][0])[None, :], (128, 16)))
    sh["w_bm"] = f(inp["w_branch_m"][0]); sh["w_ba"] = f(inp["w_branch_a"][0]); sh["w_out"] = f(inp["w_out"][0])
    sh["ln1_g"] = f(inp["ln1_g"]); sh["ln1_b"] = f(inp["ln1_b"]); sh["ln2_g"] = f(inp["ln2_g"]); sh["ln2_b"] = f(inp["ln2_b"])
    wu = f(inp["w_up"][0]).reshape(KC, 128, 2, FC, 128)
    sh["w_upr"] = np.ascontiguousarray(wu.transpose(3, 1, 0, 2, 4).reshape(FC, 128, KC * 256))
    sh["b_upT"] = _fm(inp["b_up"][0], 44)
    cw = f(inp["conv_w"][0])
    sh["conv_wT"] = np.ascontiguousarray(cw.reshape(3, 44, 128).transpose(2, 1, 0))
    sh["conv_bT"] = _fm(inp["conv_b"][0], 44)
    sh["w_down"] = f(inp["w_down"][0]); sh["b_down"] = f(inp["b_down"])
    sh["c_ident"] = np.eye(128, dtype=np.float32)
    sh["c_tri"] = np.triu(np.ones((128, 128), np.float32))
    t = np.arange(128)[:, None]; s_ = np.arange(256)[None, :]
    delta = (t + 128 - s_).astype(np.float32)
    ok = (delta >= 0) & (delta < 128)
    sl = alibi_slopes()
    sw = np.where(ok[:, None, :], -sl[None, :, None] * delta[:, None, :], NEG).astype(np.float32)
    sh["c_swab"] = np.ascontiguousarray(sw)
    if cfg.sample:
        tok = np.arange(NTS)
        sq = tok // 4
        ii = tok % 4
        same = (sq[:, None] == sq[None, :])
        sh["c_tri_s"] = (same & (tok[:, None] <= tok[None, :])).astype(np.float32)
        sh["c_ones_s"] = same.astype(np.float32)
        mT = (sq[:, None] == np.arange(NSEQ_S)[None, :]).astype(np.float32)
        sh["c_seqmaskT"] = np.ascontiguousarray(mT)
        sh["c_seqmask"] = np.ascontiguousarray(np.broadcast_to(mT.T[None, :, :], (128, NSEQ_S, NTS)))
        pk = np.zeros((NTS, 128), np.float32); pk[ii == 0, :] = 1.0
        sh["c_pick"] = pk
        sb_ = np.full((NTS, 16, 192), NEG, np.float32)
        s_c = np.arange(128)[None, :]
        dl = (128 + ii[:, None] - s_c).astype(np.float32)
        okc = (dl >= 0) & (dl < 128)
        sb_[:, :, 0:128] = np.where(okc[:, None, :], -sl[None, :, None] * dl[:, None, :], NEG)
        dn = (ii[:, None] - ii[None, :]).astype(np.float32)
        okn = same & (dn >= 0)
        sb_[:, :, 128:192] = np.where(okn[:, None, :], -sl[None, :, None] * dn[:, None, :], NEG)
        sh["c_sbias"] = np.ascontiguousarray(sb_)
    return sh


def prep_core(cfg, inp, c):
    seq, j = c // 4, c % 4
    SEG, NSCAN, NOWN, NB_ALL = cfg.SEG, cfg.NSCAN, cfg.NOWN, cfg.NB_ALL
    P = j * SEG
    x = np.asarray(inp["x_prompt"], np.float32)[seq]
    npre = (NSCAN + 2) * 128
    xs = np.zeros((NB_ALL * 128, D), np.float32)
    valid = np.zeros((NB_ALL * 128,), np.float32)
    if P > 0:
        xs[npre - P:npre] = x[0:P]; valid[npre - P:npre] = 1.0
    xs[npre:] = x[P:P + SEG]; valid[npre:] = 1.0
    m = {"xs": xs}
    tokc = np.zeros((128, NB_ALL, 2), np.float32)
    tokc[:, :, 0] = valid.reshape(NB_ALL, 128).T
    tokc[:, :, 1] = (tokc[:, :, 0] - 1.0) * 30000.0
    m["tokc"] = tokc
    cp = np.asarray(inp["c_prompt"], np.float32)[seq]
    cs = np.asarray(inp["c_sample"], np.float32)[c * NSEQ_S:(c + 1) * NSEQ_S]
    cs_tok = np.repeat(cs, 4, axis=0)
    ctok = np.concatenate([cp[None, :], cs_tok], axis=0)
    m["cTtok"] = np.ascontiguousarray(ctok.reshape(65, KC, 128).transpose(2, 1, 0))
    crep = np.concatenate([np.repeat(cp[None, :], 128, axis=0), cs_tok], axis=0)
    m["cTrep"] = np.ascontiguousarray(crep.reshape(192, KC, 128).transpose(2, 1, 0))
    m["xsamp"] = np.ascontiguousarray(np.asarray(inp["x_sample"], np.float32)[c * NSEQ_S:(c + 1) * NSEQ_S].reshape(NTS, D))
    cs_ = np.asarray(inp["state_ffn_conv"], np.float32)[0, c * NSEQ_S:(c + 1) * NSEQ_S]
    m["convst"] = np.ascontiguousarray(cs_.reshape(NSEQ_S, 2, 2 * FC, 128).transpose(3, 2, 0, 1).reshape(128, 2 * FC, 32))
    if cfg.sample:
        b0, b1 = c * NSEQ_S, (c + 1) * NSEQ_S
        C0 = np.asarray(inp["state_mlstm_C"], np.float32)[0, b0:b1]
        n0 = np.asarray(inp["state_mlstm_n"], np.float32)[0, b0:b1]
        m0 = np.asarray(inp["state_mlstm_m"], np.float32)[0, b0:b1]
        cn = np.concatenate([C0.transpose(0, 3, 1, 2), n0.transpose(0, 2, 1)[:, :, :, None]], axis=3)
        m["sC0"] = np.ascontiguousarray(cn.reshape(NSEQ_S, 128, 4 * 257))
        m["sm0rep"] = np.ascontiguousarray(np.broadcast_to(m0.reshape(1, 64), (128, 64)))
        m["sm0T"] = np.ascontiguousarray(m0.T)
        ck = np.asarray(inp["cache_k_win"], np.float32)[0, b0:b1]
        cv = np.asarray(inp["cache_v_win"], np.float32)[0, b0:b1]
        ckt = ck.transpose(3, 0, 2, 1)
        m["ckT"] = np.ascontiguousarray(ckt)
        m["cvn"] = np.ascontiguousarray(cv.reshape(NSEQ_S, 128, 256).transpose(1, 0, 2))
        m["ck_nat"] = np.ascontiguousarray(ck.reshape(NSEQ_S, 128, 256))
        m["cv_nat"] = np.ascontiguousarray(cv.reshape(NSEQ_S, 128, 256))
    m["first_bias"] = np.full((128, 128), NEG if j == 0 else 0.0, np.float32)
    m["convvalid"] = np.full((128, 1), 0.0 if j == 0 else 1.0, np.float32)
    return m


_CACHE = {}


def _get_program(seq):
    if seq not in _CACHE:
        cfg = Cfg(seq, sample=True)
        nc, S, st, L = build(cfg, debug=False)
        S.finish([v.b for v in L["outs"]])
        st.close()
        _CACHE[seq] = (cfg, nc)
    return _CACHE[seq]


def kernel(**inputs):
    inp = {k: np.asarray(v) for k, v in inputs.items()}
    seq = inp["x_prompt"].shape[1]
    cfg, nc = _get_program(seq)
    sh = prep_shared(cfg, inp)
    maps = []
    for c in range(8):
        m = dict(sh)
        m.update(prep_core(cfg, inp, c))
        maps.append(m)
    res = run_bass_kernel_spmd(nc, maps, core_ids=list(range(8))).results
    SEG = cfg.SEG
    f32 = np.float32
    yp = np.zeros((2, seq, D), f32)
    ys = np.zeros((128, 4, D), f32)
    pC = np.zeros((1, 2, 4, 256, 128), f32); pn = np.zeros((1, 2, 4, 128), f32); pm = np.zeros((1, 2, 4), f32)
    pk = np.zeros((1, 2, 128, 4, 64), f32); pv = np.zeros((1, 2, 128, 4, 64), f32); pcv = np.zeros((1, 2, 2, 2 * DFF), f32)
    sC = np.zeros((1, 128, 4, 256, 128), f32); sn = np.zeros((1, 128, 4, 128), f32); sm = np.zeros((1, 128, 4), f32)
    sk = np.zeros((1, 128, 128, 4, 64), f32); sv = np.zeros((1, 128, 128, 4, 64), f32); scv = np.zeros((1, 128, 2, 2 * DFF), f32)
    for c in range(8):
        r = res[c]
        s_, j = c // 4, c % 4
        yp[s_, j * SEG:(j + 1) * SEG] = np.asarray(r["y_out"], f32)
        b0, b1 = c * NSEQ_S, (c + 1) * NSEQ_S
        ys[b0:b1] = np.asarray(r["ys_out"], f32).reshape(NSEQ_S, 4, D)
        if j == 3:
            pc = np.asarray(r["pC_out"], f32)
            pC[0, s_] = pc[:, :, :256].transpose(1, 2, 0)
            pn[0, s_] = pc[:, :, 256].T
            pm[0, s_] = np.asarray(r["pm_out"], f32)[:, 0]
            kv = np.asarray(r["pkv_out"], f32)
            pk[0, s_] = kv[:, :256].reshape(128, 4, 64)
            pv[0, s_] = kv[:, 256:].reshape(128, 4, 64)
            pcv[0, s_] = np.asarray(r["pconv_out"], f32).transpose(2, 1, 0).reshape(2, 2 * DFF)
        sc = np.asarray(r["sC_out"], f32).reshape(NSEQ_S, 128, 4, 257)
        sC[0, b0:b1] = sc[:, :, :, :256].transpose(0, 2, 3, 1)
        sn[0, b0:b1] = sc[:, :, :, 256].transpose(0, 2, 1)
        sm[0, b0:b1] = np.asarray(r["sm_out"], f32).T
        sk[0, b0:b1] = np.asarray(r["sk_out"], f32).reshape(NSEQ_S, 128, 4, 64)
        sv[0, b0:b1] = np.asarray(r["sv_out"], f32).reshape(NSEQ_S, 128, 4, 64)
        scv[0, b0:b1] = np.asarray(r["sconv_out"], f32).reshape(128, 2 * FC, NSEQ_S, 2).transpose(2, 3, 1, 0).reshape(NSEQ_S, 2, 2 * DFF)
    return (yp, ys, pC, pn, pm, pk, pv, pcv, sC, sn, sm, sk, sv, scv)
```

```python
import contextlib
import numpy as np
import ml_dtypes
import concourse.bass as bass
import concourse.mybir as mybir
from concourse.bass_utils import run_bass_kernel_spmd

F32 = mybir.dt.float32
BF16 = mybir.dt.bfloat16
AF = mybir.ActivationFunctionType
ALU = mybir.AluOpType
AX = mybir.AxisListType

D = 1024
KC = 8
DIN = 6664
DFF = 2816
FC = 22
NEG = -30000.0
LN_EPS = 1e-5
ALPHA = 2.0 ** 0.25
NSEQ_S = 16
NTS = 64


class Buf:
    __slots__ = ("w", "r", "name", "excl")

    def __init__(self, name="", init=None):
        self.w = {}
        self.r = dict(init) if init else {}
        self.name = name
        self.excl = False


class _StopBuild(Exception):
    pass


class Sched:
    stop_at = None

    def checkpoint(self, name):
        if Sched.stop_at is not None and name == Sched.stop_at:
            self.stopped = True

    def __init__(self, nc, stack, n_dma=40):
        self.nc = nc
        self.eng = dict(pe=nc.tensor, act=nc.scalar, dve=nc.vector, pool=nc.gpsimd, sp=nc.sync)
        self.sem = {}
        self.cnt = {}
        self.waited = {k: {} for k in self.eng}
        for k in self.eng:
            self.sem[k] = stack.enter_context(nc.semaphore("e_" + k))
            self.cnt[k] = 0
        self.dsem = [stack.enter_context(nc.semaphore("d%d" % i)) for i in range(n_dma)]
        self.dcnt = [0] * n_dma
        self.dpool = {"sp": list(range(0, n_dma - 12)), "pool": list(range(n_dma - 12, n_dma))}
        self.dnext = {"sp": 0, "pool": 0}
        self.pending_pe = False
        self.ninst = 0
        self.epoch = {}
        self.stopped = False
        self.marks = []
        self.npe = 0
        self.log = {k: [] for k in self.eng}

    def check_deadlock(self):
        sem = {}
        pc = {k: 0 for k in self.log}
        progress = True
        while progress:
            progress = False
            for e, lg in self.log.items():
                while pc[e] < len(lg):
                    kind, key, val = lg[pc[e]]
                    if kind == "w":
                        if sem.get(key, 0) < val:
                            break
                    else:
                        sem[key] = sem.get(key, 0) + val
                    pc[e] += 1
                    progress = True
        stuck = {e: (pc[e], self.log[e][pc[e]]) for e in self.log if pc[e] < len(self.log[e])}
        return stuck

    def phase_end(self):
        self.marks.append(self.npe)
        self._phase_end()

    def _phase_end(self):
        ep = {k: v for k, v in self.cnt.items() if v > 0 and k != "sp"}
        for i, v in enumerate(self.dcnt):
            if v > 0:
                ep[i] = v
        self.epoch = ep

    def _wait(self, e, key, val):
        if val <= 0:
            return
        w = self.waited[e]
        if w.get(key, 0) >= val:
            return
        w[key] = val
        semh = self.sem[key] if isinstance(key, str) else self.dsem[key]
        self.eng[e].wait_ge(semh, val)
        self.log[e].append(("w", key, val))

    def _deps(self, e, reads, writes):
        need = {}
        for b in reads:
            for k, v in b.w.items():
                if need.get(k, 0) < v:
                    need[k] = v
            if b.excl:
                for k, v in b.r.items():
                    if k != e and need.get(k, 0) < v:
                        need[k] = v
        for b in writes:
            for k, v in b.w.items():
                if need.get(k, 0) < v:
                    need[k] = v
            for k, v in b.r.items():
                if need.get(k, 0) < v:
                    need[k] = v
        for k, v in need.items():
            if k == e and e == "pe":
                continue
            self._wait(e, k, v)

    def _mark(self, tok, reads, writes):
        k, v = tok
        for b in reads:
            if b.r.get(k, 0) < v:
                b.r[k] = v
        for b in writes:
            b.w = {k: v}
            b.r = {}

    def _collect(self, e, reads, writes):
        need = {}
        for b in reads:
            for k, v in b.w.items():
                if need.get(k, 0) < v:
                    need[k] = v
            if b.excl:
                for k, v in b.r.items():
                    if k != e and need.get(k, 0) < v:
                        need[k] = v
        for b in writes:
            for k, v in b.w.items():
                if need.get(k, 0) < v:
                    need[k] = v
            for k, v in b.r.items():
                if need.get(k, 0) < v:
                    need[k] = v
        out = []
        w = self.waited[e]
        for k, v in need.items():
            if k == e and e == "pe":
                continue
            if v <= 0 or w.get(k, 0) >= v:
                continue
            w[k] = v
            out.append((k, v))
        return out

    def op(self, e, fn, reads=(), writes=(), inc=True):
        if self.stopped:
            return (e, 0)
        pend = self._collect(e, reads, writes)
        for k, v in pend[:-1]:
            semh = self.sem[k] if isinstance(k, str) else self.dsem[k]
            self.eng[e].wait_ge(semh, v)
            self.log[e].append(("w", k, v))
        ins = fn(self.eng[e])
        if pend:
            k, v = pend[-1]
            semh = self.sem[k] if isinstance(k, str) else self.dsem[k]
            ins._wait_ge(semh, v)
            self.log[e].append(("w", k, v))
        self.ninst += 1
        if e == "pe":
            self.npe += 1
        if inc:
            self.cnt[e] += 1
            ins.then_inc(self.sem[e], 1)
            self.log[e].append(("i", e, 1))
            tok = (e, self.cnt[e])
        else:
            tok = (e, self.cnt[e] + 1)
        self._mark(tok, reads, writes)
        return tok

    def dma(self, q, out, in_, reads=(), writes=(), acc=False, **kw):
        if self.stopped:
            return (q, 0)
        pl = self.dpool[q]
        i = pl[self.dnext[q]]
        self.dnext[q] = (self.dnext[q] + 1) % len(pl)
        self._wait(q, i, self.dcnt[i])
        if acc:
            self._deps(q, reads, ())
        else:
            self._deps(q, reads, writes)
        self.dcnt[i] += 16
        self.eng[q].dma_start(out=out, in_=in_, **kw).then_inc(self.dsem[i], 16)
        self.log[q].append(("i", i, 16))
        self.ninst += 1
        tok = (i, self.dcnt[i])
        if acc:
            self._mark(tok, reads, ())
            for b in writes:
                b.w[i] = self.dcnt[i]
        else:
            self._mark(tok, reads, writes)
        return tok

    def finish(self, bufs):
        for b in bufs:
            for k, v in list(b.w.items()) + list(b.r.items()):
                self._wait("sp", k, v)
        for i in range(len(self.dsem)):
            self._wait("sp", i, self.dcnt[i])


class T:
    def __init__(self, h, name="", init=None):
        self.h = h
        self.b = Buf(name, init)

    def __getitem__(self, idx):
        return self.h[idx]


class Cfg:
    def __init__(self, seq=8192, sample=True):
        self.SEQ = seq
        self.SEG = seq // 4
        self.NOWN = self.SEG // 128
        self.NSCAN = max(0, (3 * self.SEG - 256) // 128)
        self.NB_ALL = self.NSCAN + 2 + self.NOWN
        self.sample = sample


def build(cfg, debug=False, stop_after=None):
    nc = bass.Bass("TRN2", target_bir_lowering=False)
    st = contextlib.ExitStack()
    S = Sched(nc, st)
    NOWN, NSCAN, NB_ALL = cfg.NOWN, cfg.NSCAN, cfg.NB_ALL

    def din(name, shape, dt=F32):
        return T(nc.dram_tensor(name, list(shape), dt, kind="ExternalInput").ap(), name)

    def dout(name, shape, dt=F32):
        return T(nc.dram_tensor(name, list(shape), dt, kind="ExternalOutput").ap(), name)

    def dscr(name, shape, dt=F32):
        return T(nc.dram_tensor(name, list(shape), dt, kind="Internal").ap(), name)

    uniq = {"n": 0}

    def sb(name, shape, dt=F32, stack=None):
        uniq["n"] += 1
        return T((stack or st).enter_context(nc.sbuf_tensor("%s_%d" % (name, uniq["n"]), list(shape), dt)), name, S.epoch)

    def ps(name, shape, dt=F32):
        t_ = T(st.enter_context(nc.psum_tensor(name, list(shape), dt)), name)
        t_.b.excl = True
        return t_

    xs = din("xs", [NB_ALL * 128, D])
    tokc = din("tokc", [128, NB_ALL, 2])
    cTtok = din("cTtok", [128, KC, 65])
    cTrep = din("cTrep", [128, KC, 192])
    first_bias = din("first_bias", [128, 128])
    convvalid = din("convvalid", [128, 1])
    w_ada = din("w_ada", [D, 6 * D])
    b_ada = din("b_ada", [1, 6 * D])
    b_adaT = din("b_adaT", [128, 48])
    w_in = din("w_in", [D, DIN])
    b_in = din("b_in", [1, DIN])
    b_inT = din("b_inT", [128, 32])
    b_qk64 = din("b_qk64", [64, 20])
    norm_wT = din("norm_wT", [128, KC])
    sinks_rep = din("sinks_rep", [128, 16])
    w_bm = din("w_bm", [D, D])
    w_ba = din("w_ba", [D, D])
    w_out = din("w_out", [D, D])
    ln1_g = din("ln1_g", [1, D])
    ln1_b = din("ln1_b", [1, D])
    w_upr = din("w_upr", [FC, 128, KC * 256])
    b_upT = din("b_upT", [128, 2 * FC])
    conv_wT = din("conv_wT", [128, 2 * FC, 3])
    conv_bT = din("conv_bT", [128, 2 * FC])
    w_down = din("w_down", [DFF, D])
    b_down = din("b_down", [1, D])
    ln2_g = din("ln2_g", [1, D])
    ln2_b = din("ln2_b", [1, D])
    c_ident = din("c_ident", [128, 128])
    c_tri = din("c_tri", [128, 128])
    c_swab = din("c_swab", [128, 16, 256])

    xsamp = din("xsamp", [NTS, D])
    convst = din("convst", [128, 2 * FC, 32])
    y_out = dout("y_out", [NOWN * 128, D])
    ys_out = dout("ys_out", [NTS, D])
    sconv_out = dout("sconv_out", [128, 2 * FC, 32])
    pC_out = dout("pC_out", [128, 4, 257])
    pm_out = dout("pm_out", [4, 1])
    pkv_out = dout("pkv_out", [128, 512])
    pconv_out = dout("pconv_out", [128, 2 * FC, 2])
    x1_scr = dscr("x1_scr", [(NOWN + 1) * 128 + NTS, D])
    dbg = {}

    psf = [ps("psf%d" % i, [128, 512], F32) for i in range(6)]
    psb = [ps("psb%d" % i, [128, 1024], BF16) for i in range(2)]
    rot = {"f": 0, "b": 0}

    def PF():
        rot["f"] = (rot["f"] + 1) % 4
        return psf[rot["f"]]

    def PL(i):
        return psf[4 + i]

    def PB():
        rot["b"] = (rot["b"] + 1) % len(psb)
        return psb[rot["b"]]

    ident = sb("ident", [128, 128])
    identb = sb("identb", [128, 128], BF16)
    tri = sb("tri", [128, 128])
    trib = sb("trib", [128, 128], BF16)
    ones = sb("ones", [128, 128])
    onesb = sb("onesb", [128, 128], BF16)
    tokc_sb = sb("tokc_sb", [128, NB_ALL, 2])
    modF = sb("modF", [128, 32, 1])
    modR_scr = dscr("modR_scr", [128, 4, D])
    SAMP = {}
    modFs_scr = dscr("modFs_scr", [128, 32, NTS])
    gS_scr = dscr("gS_scr", [NTS, 2, D])
    gP = sb("gP", [128, 2, D])
    mst = sb("mst", [4, 1])
    CnT = sb("CnT", [128, 4, 257])
    CnTb = sb("CnTb", [128, 4, 257], BF16)
    ones_col = sb("ones_col", [128, 1], BF16)

    S.dma("sp", ident[:], c_ident[:, :], writes=[ident.b])
    S.dma("sp", tri[:], c_tri[:, :], writes=[tri.b])
    S.dma("sp", tokc_sb[:], tokc[:, :, :], writes=[tokc_sb.b])
    S.op("dve", lambda e: e.tensor_copy(identb[:], ident[:]), [ident.b], [identb.b])
    S.op("dve", lambda e: e.tensor_copy(trib[:], tri[:]), [tri.b], [trib.b])
    S.op("pool", lambda e: e.memset(ones[:], 1.0), [], [ones.b])
    S.op("pool", lambda e: e.memset(onesb[:], 1.0), [], [onesb.b])
    S.op("pool", lambda e: e.memset(ones_col[:], 1.0), [], [ones_col.b])
    S.op("pool", lambda e: e.memset(mst[:], 0.0), [], [mst.b])
    S.op("pool", lambda e: e.memset(CnT[:], 0.0), [], [CnT.b])
    S.op("pool", lambda e: e.memset(CnTb[:], 0.0), [], [CnTb.b])

    def load_w(dst, src, c0, c1, rows=D, stack=None):
        v = src.h.rearrange("(k p) c -> p k c", p=128)
        nk = rows // 128
        for k in range(nk):
            S.dma("pool", dst[:, k, 0:c1 - c0], v[:, k, c0:c1], writes=[dst.b], acc=(k > 0))

    def bias_hilo(name, src, c0, c1, stack):
        n = c1 - c0
        stg = sb(name + "_stg", [64, n], F32, stack)
        tb = sb(name + "_tb", [64, n], BF16, stack)
        out = sb(name, [64, n], BF16, stack)
        S.op("pool", lambda e: e.memset(out[:], 0.0), [], [out.b])
        S.dma("sp", stg[0:1, :], src[0:1, c0:c1], writes=[stg.b])
        S.dma("sp", stg[32:33, :], src[0:1, c0:c1], writes=[stg.b])
        S.op("dve", lambda e: e.tensor_copy(out[0:1, :], stg[0:1, :]), [stg.b], [out.b])
        S.op("dve", lambda e: e.tensor_copy(tb[32:33, :], stg[32:33, :]), [stg.b], [tb.b])
        S.op("dve", lambda e: e.tensor_tensor(stg[32:33, :], stg[32:33, :], tb[32:33, :], ALU.subtract),
             [tb.b], [stg.b])
        S.op("dve", lambda e: e.tensor_copy(out[32:33, :], stg[32:33, :]), [stg.b], [out.b])
        return out

    def mm_tok(pst, n0, n, hT, c0, nt, W, wc0, bias, bc0):
        for k in range(KC):
            S.op("pe", lambda e, k=k: e.matmul(pst[:nt, n0:n0 + n], hT[:, k, c0:c0 + nt], W[:, k, wc0:wc0 + n],
                                               start=(k == 0), stop=(k == KC - 1 and bias is None)),
                 [hT.b, W.b], [pst.b], inc=(k == KC - 1 and bias is None))
        if bias is not None:
            S.op("pe", lambda e: e.matmul(pst[:nt, n0:n0 + n], onesb[0:64, 0:nt], bias[:, bc0:bc0 + n],
                                          start=False, stop=True), [bias.b, onesb.b], [pst.b])

    def ln_stats(xt, nt, tmp):
        S.op("dve", lambda e: e.bn_stats(tmp[:nt, 0:6], xt[:nt, 0:512]), [xt.b], [tmp.b])
        S.op("dve", lambda e: e.bn_stats(tmp[:nt, 6:12], xt[:nt, 512:1024]), [xt.b], [tmp.b])
        S.op("dve", lambda e: e.bn_aggr(tmp[:nt, 12:14], tmp[:nt, 0:12]), [tmp.b], [tmp.b])
        S.op("act", lambda e: e.activation(tmp[:nt, 14:15], tmp[:nt, 13:14], AF.Ln, bias=epsc[:nt, 0:1], scale=1.0),
             [tmp.b, epsc.b], [tmp.b])
        S.op("act", lambda e: e.activation(tmp[:nt, 15:16], tmp[:nt, 14:15], AF.Exp, scale=-0.5), [tmp.b], [tmp.b])
        S.op("dve", lambda e: e.tensor_scalar(tmp[:nt, 16:17], tmp[:nt, 12:13], tmp[:nt, 15:16], -1.0, ALU.mult, ALU.mult),
             [tmp.b], [tmp.b])
        return tmp[:nt, 15:16], tmp[:nt, 16:17]

    epsc = sb("epsc", [128, 1])
    S.op("pool", lambda e: e.memset(epsc[:], LN_EPS), [], [epsc.b])

    xbl = [sb("xb%d" % i, [128, D], BF16) for i in range(2)]
    xbrot = {"i": 0}

    def ln_mod_rows(xn, nt, mrow):
        xbrot["i"] ^= 1
        xb = xbl[xbrot["i"]]
        S.op("dve", lambda e: e.tensor_tensor(xn[:nt, :], xn[:nt, :], mrow[:nt, 1, :], ALU.mult), [xn.b, mrow.b], [xn.b])
        S.op("dve", lambda e: e.tensor_tensor(xb[:nt, :], xn[:nt, :], mrow[:nt, 0, :], ALU.add), [xn.b, mrow.b], [xb.b])
        return xb

    def xb_to_hT(xb, nt, hT, c0):
        ptb = PB()
        for k in range(KC):
            S.op("pe", lambda e, k=k: e.transpose(ptb[:, k * 128:k * 128 + nt], xb[:nt, k * 128:(k + 1) * 128], identb[:nt, :nt]),
                 [xb.b, identb.b], [ptb.b], inc=(k == KC - 1))
        S.op("act", lambda e: e.activation(hT[:, :, c0:c0 + nt], ptb[:, :].rearrange("p (k t) -> p k t", k=KC)[:, :, 0:nt], AF.Identity),
             [ptb.b], [hT.b])

    def ln_rows_part1(xt, nt, xn, tmp, mrow):
        rstd, nmr = ln_stats(xt, nt, tmp)
        S.op("act", lambda e: e.activation(xn[:nt, :], xt[:nt, :], AF.Identity, bias=nmr, scale=rstd),
             [xt.b, tmp.b], [xn.b])
        return ln_mod_rows(xn, nt, mrow)

    def ln_to_hT(xt, nt, hT, c0, msel, samp, xn, tmp, mrow=None):
        rstd, nmr = ln_stats(xt, nt, tmp)
        S.op("act", lambda e: e.activation(xn[:nt, :], xt[:nt, :], AF.Identity, bias=nmr, scale=rstd),
             [xt.b, tmp.b], [xn.b])
        if mrow is not None:
            xb = ln_mod_rows(xn, nt, mrow)
            xb_to_hT(xb, nt, hT, c0)
            return
        for half in range(2):
            pst = PF()
            for kk in range(4):
                k = half * 4 + kk
                S.op("pe", lambda e, k=k, kk=kk: e.transpose(pst[:, kk * 128:kk * 128 + nt], xn[:nt, k * 128:(k + 1) * 128],
                                                             ident[:nt, :nt]), [xn.b, ident.b], [pst.b], inc=(kk == 3))
            for kk in range(4):
                k = half * 4 + kk
                if not samp:
                    S.op("act", lambda e, k=k, kk=kk: e.activation(
                        hT[:, k, c0:c0 + nt], pst[:, kk * 128:kk * 128 + nt], AF.Identity,
                        bias=modF[:, (msel) * 8 + k, 0:1], scale=modF[:, (msel + 1) * 8 + k, 0:1]),
                        [pst.b, modF.b], [hT.b])
                else:
                    S.op("dve", lambda e, k=k, kk=kk: e.tensor_tensor(
                        xn[:, 0:nt], pst[:, kk * 128:kk * 128 + nt], SAMP["modFs"][:, (msel + 1) * 8 + k, :], ALU.mult),
                        [pst.b, SAMP["modFs"].b], [xn.b])
                    S.op("dve", lambda e, k=k: e.tensor_tensor(
                        hT[:, k, c0:c0 + nt], xn[:, 0:nt], SAMP["modFs"][:, msel * 8 + k, :], ALU.add),
                        [xn.b, SAMP["modFs"].b], [hT.b])

    with contextlib.ExitStack() as ph:
        wada = sb("wada", [128, KC, 2048], BF16, ph)
        scT = sb("scT", [128, KC, 65], F32, ph)
        scTb = sb("scTb", [128, KC, 65], BF16, ph)
        scR = sb("scR", [128, KC, 192], F32, ph)
        scRb = sb("scRb", [128, KC, 192], BF16, ph)
        badaT = sb("badaT", [128, 48], F32, ph)
        modFt = sb("modFt", [128, 32, 65], F32, ph)
        gSt = sb("gSt", [NTS, 2, D], F32, ph)
        S.dma("sp", scT[:], cTtok[:, :, :], writes=[scT.b])
        S.dma("sp", scR[:], cTrep[:, :, :], writes=[scR.b])
        S.dma("sp", badaT[:], b_adaT[:, :], writes=[badaT.b])
        S.op("act", lambda e: e.activation(scTb[:], scT[:], AF.Silu), [scT.b], [scTb.b])
        S.op("act", lambda e: e.activation(scRb[:], scR[:], AF.Silu), [scR.b], [scRb.b])
        mrt = sb("mrt", [128, D], F32, ph)
        for gi, (cbase, addone) in enumerate([(0, False), (1024, True), (3072, False), (4096, True)]):
            if gi % 2 == 0:
                load_w(wada, w_ada, cbase if gi == 0 else 3072, (cbase if gi == 0 else 3072) + 2048)
            for oc in range(8):
                pst = PF()
                wc = (gi % 2) * 1024 + oc * 128
                for k in range(KC):
                    S.op("pe", lambda e, k=k, wc=wc: e.matmul(pst[:, 0:65], wada[:, k, wc:wc + 128], scTb[:, k, :],
                                                              start=(k == 0), stop=(k == KC - 1)),
                         [wada.b, scTb.b], [pst.b], inc=(k == KC - 1))
                bcol = (cbase // 128) + oc
                if addone:
                    S.op("dve", lambda e, bcol=bcol, gi=gi, oc=oc: e.tensor_scalar(
                        modFt[:, gi * 8 + oc, :], pst[:, 0:65], badaT[:, bcol:bcol + 1], 1.0, ALU.add, ALU.add),
                        [pst.b, badaT.b], [modFt.b])
                else:
                    S.op("dve", lambda e, bcol=bcol, gi=gi, oc=oc: e.tensor_scalar(
                        modFt[:, gi * 8 + oc, :], pst[:, 0:65], badaT[:, bcol:bcol + 1], None, ALU.add),
                        [pst.b, badaT.b], [modFt.b])
            bh = bias_hilo("bada_r%d" % gi, b_ada, cbase, cbase + 1024, ph)
            for half in range(2):
                pst = PF()
                mm_tok(pst, 0, 512, scRb, 0, 128, wada, (gi % 2) * 1024 + half * 512, bh, half * 512)
                S.op("dve", lambda e, half=half, pst=pst, addone=addone: e.tensor_scalar(
                    mrt[:, half * 512:(half + 1) * 512], pst[:, :], 1.0 if addone else 0.0, None, ALU.add), [pst.b], [mrt.b])
            S.dma("sp", modR_scr[:, gi, :], mrt[:], reads=[mrt.b], writes=[modR_scr.b], acc=(gi > 0))
        for gi, cbase in enumerate([2048, 5120]):
            load_w(wada, w_ada, cbase, cbase + 1024)
            bh = bias_hilo("bada_g%d" % gi, b_ada, cbase, cbase + 1024, ph)
            for (dst, r0, nt) in [(gP, 0, 128), (gSt, 128, NTS)]:
                for half in range(2):
                    pst = PF()
                    mm_tok(pst, 0, 512, scRb, r0, nt, wada, half * 512, bh, half * 512)
                    S.op("act", lambda e, dst=dst, nt=nt, half=half, gi=gi: e.activation(
                        dst[:nt, gi, half * 512:(half + 1) * 512], pst[:nt, :], AF.Identity),
                        [pst.b], [dst.b])
        S.op("dve", lambda e: e.tensor_copy(modF[:], modFt[:, :, 0:1]), [modFt.b], [modF.b])
        S.dma("sp", modFs_scr[:, :, :], modFt[:, :, 1:65], reads=[modFt.b], writes=[modFs_scr.b])
        S.dma("sp", gS_scr[:, :, :], gSt[:], reads=[gSt.b], writes=[gS_scr.b])

    S.phase_end()

    def mlstm_gates(zg, nt, blk_all, gt, tri_t=None, ones_t=None):
        tri_t = tri_t or tri
        ones_t = ones_t or ones
        S.op("act", lambda e: e.activation(gt[:nt, 32:36], zg[:nt, 4:8], AF.Exp, scale=-1.0), [zg.b], [gt.b])
        S.op("act", lambda e: e.activation(gt[:nt, 32:36], gt[:nt, 32:36], AF.Ln, bias=onec[:nt, 0:1], scale=1.0),
             [gt.b, onec.b], [gt.b])
        S.op("dve", lambda e: e.tensor_scalar(gt[:nt, 0:4], gt[:nt, 32:36], tokc_sb[:nt, blk_all, 0:1], -1.0,
                                              ALU.mult, ALU.mult), [gt.b, tokc_sb.b], [gt.b])
        S.op("dve", lambda e: e.tensor_scalar(gt[:nt, 4:8], zg[:nt, 0:4], tokc_sb[:nt, blk_all, 1:2], None, ALU.add),
             [zg.b, tokc_sb.b], [gt.b])
        pst = PF()
        S.op("pe", lambda e: e.matmul(pst[:nt, 0:4], tri_t[:nt, :nt], gt[:nt, 0:4], start=True, stop=True),
             [tri_t.b, gt.b], [pst.b], inc=False)
        S.op("pe", lambda e: e.matmul(pst[:nt, 4:8], ones_t[:nt, :nt], gt[:nt, 0:4], start=True, stop=True),
             [ones_t.b, gt.b], [pst.b])
        S.op("dve", lambda e: e.tensor_copy(gt[:nt, 8:16], pst[:nt, 0:8]), [pst.b], [gt.b])
        S.op("act", lambda e: e.activation(gt[:nt, 16:20], gt[:nt, 8:12], AF.Exp), [gt.b], [gt.b])
        S.op("dve", lambda e: e.tensor_tensor(gt[:nt, 36:40], gt[:nt, 4:8], gt[:nt, 8:12], ALU.subtract), [gt.b], [gt.b])
        S.op("act", lambda e: e.activation(gt[:nt, 20:24], gt[:nt, 36:40], AF.Exp), [gt.b], [gt.b])
        S.op("act", lambda e: e.activation(gt[:nt, 24:28], gt[:nt, 12:16], AF.Exp), [gt.b], [gt.b])
        S.op("dve", lambda e: e.tensor_tensor(gt[:nt, 28:32], gt[:nt, 36:40], gt[:nt, 12:16], ALU.add), [gt.b], [gt.b])

    onec = sb("onec", [128, 1])
    S.op("pool", lambda e: e.memset(onec[:], 1.0), [], [onec.b])
    mtmp = sb("mtmp", [4, 8])

    def m_update(gt, nt):
        p1 = PF()
        S.op("pe", lambda e: e.transpose(p1[0:4, 0:nt], gt[:nt, 28:32], ident[:nt, :nt]), [gt.b, ident.b], [p1.b], inc=False)
        S.op("pe", lambda e: e.transpose(p1[0:4, 128:128 + nt], gt[:nt, 12:16], ident[:nt, :nt]), [gt.b, ident.b], [p1.b])
        S.op("dve", lambda e: e.reduce_max(mtmp[:, 0:1], p1[0:4, 0:nt], AX.X), [p1.b], [mtmp.b])
        S.op("dve", lambda e: e.tensor_tensor(mtmp[:, 1:2], p1[0:4, 128:129], mst[:, 0:1], ALU.add), [p1.b, mst.b], [mtmp.b])
        S.op("dve", lambda e: e.tensor_tensor(mst[:, 0:1], mtmp[:, 0:1], mtmp[:, 1:2], ALU.max), [mtmp.b], [mst.b])

    def state_update(ks, v1, gt, nt):
        for hp in range(2):
            pst = PF()
            for hh in range(2):
                h = hp * 2 + hh
                S.op("pe", lambda e, h=h, hh=hh: e.matmul(pst[:, hh * 256:(hh + 1) * 256], ks[:nt, h, :], v1[:nt, h, 0:256],
                                                          start=True, stop=True), [ks.b, v1.b], [pst.b], inc=(hh == 1))
            for hh in range(2):
                h = hp * 2 + hh
                S.op("dve", lambda e, h=h: e.tensor_scalar(CnT[:, h, 0:256], CnT[:, h, 0:256], gt[:, 24 + h:25 + h], None, ALU.mult),
                     [gt.b], [CnT.b])
                S.op("dve", lambda e, h=h, hh=hh: e.scalar_tensor_tensor(
                    CnT[:, h, 0:256], pst[:, hh * 256:(hh + 1) * 256], gt[:, 24 + h:25 + h], CnT[:, h, 0:256], ALU.mult, ALU.add),
                    [pst.b, gt.b], [CnT.b])
        pst = PF()
        for h in range(4):
            S.op("pe", lambda e, h=h: e.matmul(pst[:, h:h + 1], ks[:nt, h, :], ones_col[:nt, 0:1], start=True, stop=True),
                 [ks.b, ones_col.b], [pst.b], inc=(h == 3))
        for h in range(4):
            S.op("dve", lambda e, h=h: e.tensor_scalar(CnT[:, h, 256:257], CnT[:, h, 256:257], gt[:, 24 + h:25 + h], None, ALU.mult),
                 [gt.b], [CnT.b])
            S.op("dve", lambda e, h=h: e.scalar_tensor_tensor(
                CnT[:, h, 256:257], pst[:, h:h + 1], gt[:, 24 + h:25 + h], CnT[:, h, 256:257], ALU.mult, ALU.add),
                [pst.b, gt.b], [CnT.b])
        S.op("act", lambda e: e.activation(CnTb[:], CnT[:], AF.Identity), [CnT.b], [CnTb.b])

    gt = sb("gt", [128, 40])
    lnt = sb("lnt", [128, 20])
    ks = sb("ks", [128, 4, 128], BF16)
    v1 = sb("v1", [128, 4, 257], BF16)
    S.op("pool", lambda e: e.memset(v1[:], 1.0), [], [v1.b])
    xtl = [sb("xt%d" % i, [128, D]) for i in range(2)]
    xn = sb("xn", [128, D])
    xrot = {"i": 0}
    ks_g, v1_g, gt_g = ks, v1, gt

    def XT():
        xrot["i"] ^= 1
        return xtl[xrot["i"]]

    def kv_from_psum(pk, pv0, pv1, nt, ks=None, v1=None, gt=None):
        ks = ks or ks_g
        v1 = v1 or v1_g
        gt = gt or gt_g
        for h in range(4):
            S.op("dve", lambda e, h=h: e.tensor_scalar(ks[:nt, h, :], pk[:nt, h * 128:(h + 1) * 128], gt[:nt, 20 + h:21 + h], None, ALU.mult),
                 [pk.b, gt.b], [ks.b])
        S.op("act", lambda e: e.activation(v1[:nt, 0:2, 0:256], pv0[:nt, :].rearrange("p (h d) -> p h d", h=2), AF.Identity), [pv0.b], [v1.b])
        S.op("act", lambda e: e.activation(v1[:nt, 2:4, 0:256], pv1[:nt, :].rearrange("p (h d) -> p h d", h=2), AF.Identity), [pv1.b], [v1.b])

    def scan_block(hT, c0, ba, W, wk0, bias):
        pg = PF()
        mm_tok(pg, 0, 8, hT, c0, 128, W, wk0 + 1536, bias, wk0 + 1536)
        mlstm_gates(pg, 128, ba, gt)
        pk, pv0, pv1 = PF(), PF(), PF()
        mm_tok(pk, 0, 512, hT, c0, 128, W, wk0, bias, wk0)
        mm_tok(pv0, 0, 512, hT, c0, 128, W, wk0 + 512, bias, wk0 + 512)
        mm_tok(pv1, 0, 512, hT, c0, 128, W, wk0 + 1024, bias, wk0 + 1024)
        kv_from_psum(pk, pv0, pv1, 128)
        state_update(ks, v1, gt, 128)
        m_update(gt, 128)


    def head_ln_s(banks, hst, hmn, nt):
        for h in range(4):
            S.op("dve", lambda e, h=h: e.tensor_copy(hst[:nt, 56 + h:57 + h], banks[h][:nt, 256:257]), [banks[h].b], [hst.b])
        head_ln([banks[0], banks[2]], None, hst, hmn, nt, banks=banks)

    def head_ln(pnum, pden, hst, hmn, nt, banks=None):
        if banks is None:
            S.op("dve", lambda e: e.tensor_copy(hst[:nt, 56:60], pden[:nt, 0:4]), [pden.b], [hst.b])
        S.op("dve", lambda e: e.tensor_scalar(hst[:nt, 52:56], hst[:nt, 56:60], -1.0, None, ALU.mult), [hst.b], [hst.b])
        S.op("dve", lambda e: e.tensor_tensor(hst[:nt, 0:4], hst[:nt, 56:60], hst[:nt, 52:56], ALU.max), [hst.b], [hst.b])
        S.op("dve", lambda e: e.tensor_scalar(hst[:nt, 0:4], hst[:nt, 0:4], 1.0, None, ALU.max), [hst.b], [hst.b])
        S.op("dve", lambda e: e.scalar_tensor_tensor(hst[:nt, 4:8], hst[:nt, 0:4], LN_EPS, hst[:nt, 0:4], ALU.mult, ALU.mult), [hst.b], [hst.b])
        for h in range(4):
            pn = pnum[h // 2] if banks is None else banks[h]
            cs = (h % 2) * 256 if banks is None else 0
            S.op("dve", lambda e, h=h, pn=pn, cs=cs: e.bn_stats(hst[:nt, 8 + 6 * h:14 + 6 * h], pn[:nt, cs:cs + 256]), [pn.b], [hst.b])
            S.op("dve", lambda e, h=h: e.bn_aggr(hst[:nt, 32 + 2 * h:34 + 2 * h], hst[:nt, 8 + 6 * h:14 + 6 * h]), [hst.b], [hst.b])
        mvv = hst[:nt, 32:40].rearrange("p (h t) -> p h t", t=2)
        S.op("dve", lambda e: e.tensor_tensor(hst[:nt, 40:44], mvv[:, :, 1], hst[:nt, 4:8], ALU.add), [hst.b], [hst.b])
        S.op("act", lambda e: e.activation(hst[:nt, 40:44], hst[:nt, 40:44], AF.Ln), [hst.b], [hst.b])
        S.op("act", lambda e: e.activation(hst[:nt, 44:48], hst[:nt, 40:44], AF.Exp, scale=-0.5), [hst.b], [hst.b])
        S.op("dve", lambda e: e.scalar_tensor_tensor(hst[:nt, 48:52], mvv[:, :, 0], -1.0, hst[:nt, 44:48], ALU.mult, ALU.mult), [hst.b], [hst.b])
        for h in range(4):
            pn = pnum[h // 2] if banks is None else banks[h]
            cs = (h % 2) * 256 if banks is None else 0
            S.op("act", lambda e, h=h, pn=pn, cs=cs: e.activation(hmn[:nt, h * 256:(h + 1) * 256], pn[:nt, cs:cs + 256], AF.Identity,
                                                                  bias=hst[:nt, 48 + h:49 + h], scale=hst[:nt, 44 + h:45 + h]),
                 [pn.b, hst.b], [hmn.b])

    def hm_transpose_out(hmn, nt, hmT, c0, nwT, sgog):
        for half in range(2):
            pst = PF()
            for kk in range(4):
                k = half * 4 + kk
                S.op("pe", lambda e, k=k, kk=kk, pst=pst: e.transpose(pst[:, kk * 128:kk * 128 + nt], hmn[:nt, k * 128:(k + 1) * 128], ident[:nt, :nt]),
                     [hmn.b, ident.b], [pst.b], inc=(kk == 3))
            for kk in range(4):
                k = half * 4 + kk
                S.op("dve", lambda e, k=k, kk=kk, pst=pst: e.scalar_tensor_tensor(
                    hmT[:, k, c0:c0 + nt], pst[:, kk * 128:kk * 128 + nt], nwT[:, k:k + 1], sgog[:, k, c0:c0 + nt], ALU.mult, ALU.mult),
                    [pst.b, nwT.b, sgog.b], [hmT.b])

    if cfg.sample:
        sC0 = din("sC0", [NSEQ_S, 128, 4 * 257])
        sm0rep = din("sm0rep", [128, 64])
        sm0T = din("sm0T", [4, NSEQ_S])
        c_tri_s = din("c_tri_s", [NTS, NTS])
        c_ones_s = din("c_ones_s", [NTS, NTS])
        c_seqmask = din("c_seqmask", [128, NSEQ_S, NTS])
        c_seqmaskT = din("c_seqmaskT", [NTS, NSEQ_S])
        c_pick = din("c_pick", [NTS, 128])
        ckT = din("ckT", [64, NSEQ_S, 4, 128])
        cvn = din("cvn", [128, NSEQ_S, 256])
        ck_nat = din("ck_nat", [NSEQ_S, 128, 256])
        cv_nat = din("cv_nat", [NSEQ_S, 128, 256])
        c_sbias = din("c_sbias", [NTS, 16, 192])
        sC_out = dout("sC_out", [NSEQ_S, 128, 4 * 257])
        sm_out = dout("sm_out", [4, NSEQ_S])
        sk_out = dout("sk_out", [NSEQ_S, 128, 256])
        sv_out = dout("sv_out", [NSEQ_S, 128, 256])

    def samp_mlstm(hT, c0, Wm, bm, hmT, sgog, nwT, ph):
        nt = NTS
        tri_s = sb("tri_s", [NTS, NTS], F32, ph)
        ones_s = sb("ones_s", [NTS, NTS], F32, ph)
        smaskb = sb("smaskb", [128, NSEQ_S, NTS], BF16, ph)
        smaskT = sb("smaskT", [NTS, NSEQ_S], F32, ph)
        pick = sb("pick", [NTS, 128], F32, ph)
        em0 = sb("em0", [128, 64], F32, ph)
        m0T = sb("m0T", [4, NSEQ_S], F32, ph)
        S.dma("sp", tri_s[:], c_tri_s[:, :], writes=[tri_s.b])
        S.dma("sp", ones_s[:], c_ones_s[:, :], writes=[ones_s.b])
        S.dma("pool", smaskb[:], c_seqmask[:, :, :], writes=[smaskb.b])
        S.dma("sp", smaskT[:], c_seqmaskT[:, :], writes=[smaskT.b])
        S.dma("sp", pick[:], c_pick[:, :], writes=[pick.b])
        S.dma("sp", em0[:], sm0rep[:, :], writes=[em0.b])
        S.dma("sp", m0T[:], sm0T[:, :], writes=[m0T.b])
        S.op("act", lambda e: e.activation(em0[:], em0[:], AF.Exp), [em0.b], [em0.b])
        qs = sb("sqs", [128, 4, 128], BF16, ph)
        qkT = sb("sqkT", [128, 8, NTS], BF16, ph)
        qm = sb("sqm", [128, 4, NSEQ_S, NTS], BF16, ph)
        SmT = sb("sSmT", [NTS, 4, NTS], BF16, ph)
        hmn = sb("shmn", [128, D], F32, ph)
        hst = sb("shst", [128, 64], F32, ph)
        Cin = [sb("Cin%d" % i, [128, 4, 257], F32, ph) for i in range(2)]
        Cbf = [sb("Cbf%d" % i, [128, 4, 257], BF16, ph) for i in range(2)]
        pg = PF()
        mm_tok(pg, 0, 8, hT, c0, nt, Wm, 2048, bm, 2048)
        mlstm_gates(pg, nt, NB_ALL - 1, gt, tri_s, ones_s)
        pq = PF()
        mm_tok(pq, 0, 512, hT, c0, nt, Wm, 0, bm, 0)
        for h in range(4):
            S.op("dve", lambda e, h=h, pq=pq: e.tensor_scalar(qs[:nt, h, :], pq[:nt, h * 128:(h + 1) * 128], gt[:nt, 16 + h:17 + h],
                                                              128.0 ** -0.5, ALU.mult, ALU.mult), [pq.b, gt.b], [qs.b])
        pk, pv0, pv1 = PF(), PF(), PF()
        mm_tok(pk, 0, 512, hT, c0, nt, Wm, 512, bm, 512)
        mm_tok(pv0, 0, 512, hT, c0, nt, Wm, 1024, bm, 1024)
        mm_tok(pv1, 0, 512, hT, c0, nt, Wm, 1536, bm, 1536)
        kv_from_psum(pk, pv0, pv1, nt)
        ptb = PB()
        for h in range(4):
            S.op("pe", lambda e, h=h: e.transpose(ptb[:, h * 64:(h + 1) * 64], qs[:nt, h, :], identb[:nt, :nt]),
                 [qs.b, identb.b], [ptb.b], inc=False)
        for h in range(4):
            S.op("pe", lambda e, h=h: e.transpose(ptb[:, 256 + h * 64:256 + (h + 1) * 64], ks[:nt, h, :], identb[:nt, :nt]),
                 [ks.b, identb.b], [ptb.b], inc=(h == 3))
        S.op("act", lambda e: e.activation(qkT[:].rearrange("p a b -> p (a b)"), ptb[:, 0:512], AF.Identity), [ptb.b], [qkT.b])
        for h in range(4):
            S.op("dve", lambda e, h=h: e.tensor_tensor(qm[:, h, :, :], qkT[:, h, :].unsqueeze(1).broadcast_to([128, NSEQ_S, NTS]), smaskb[:], ALU.mult),
                 [qkT.b, smaskb.b], [qm.b])
        pS = PF()
        for h in range(4):
            S.op("pe", lambda e, h=h: e.matmul(pS[:nt, h * 64:(h + 1) * 64], qkT[:, 4 + h, :], qkT[:, h, :], start=True, stop=True),
                 [qkT.b], [pS.b], inc=(h == 3))
        S.op("dve", lambda e: e.tensor_tensor(SmT[:], pS[:nt, 0:256].rearrange("p (h t) -> p h t", h=4),
                                              tri_s[:, :].unsqueeze(1).broadcast_to([NTS, 4, NTS]), ALU.mult), [pS.b, tri_s.b], [SmT.b])
        banks = [PL(0), PL(1), psf[2], psf[3]]
        for h in range(4):
            S.op("pe", lambda e, h=h: e.matmul(banks[h][:nt, 0:257], SmT[:, h, :], v1[:nt, h, 0:257], start=True, stop=False),
                 [SmT.b, v1.b], [banks[h].b], inc=True)
        for b in range(NSEQ_S):
            ci_ = Cin[b % 2]
            cb_ = Cbf[b % 2]
            S.dma("sp", ci_[:].rearrange("p h v -> p (h v)"), sC0[b, :, :], writes=[ci_.b])
            for h in range(4):
                S.op("dve", lambda e, b=b, h=h, ci_=ci_, cb_=cb_: e.tensor_scalar(cb_[:, h, :], ci_[:, h, :], em0[:, b * 4 + h:b * 4 + h + 1], None, ALU.mult),
                     [ci_.b, em0.b], [cb_.b])
            for h in range(4):
                S.op("pe", lambda e, h=h, b=b, cb_=cb_: e.matmul(banks[h][:nt, 0:257], qm[:, h, b, :], cb_[:, h, 0:257],
                                                               start=False, stop=(b == NSEQ_S - 1)), [qm.b, cb_.b], [banks[h].b], inc=True)
        head_ln_s(banks, hst, hmn, nt)
        hm_transpose_out(hmn, nt, hmT, c0, nwT, sgog)
        p1 = PF()
        S.op("pe", lambda e: e.transpose(p1[0:4, 0:nt], gt[:nt, 28:32], ident[:nt, :nt]), [gt.b, ident.b], [p1.b], inc=False)
        S.op("pe", lambda e: e.transpose(p1[0:4, 128:128 + nt], gt[:nt, 12:16], ident[:nt, :nt]), [gt.b, ident.b], [p1.b])
        mn = sb("smn", [4, 3, NSEQ_S], F32, ph)
        S.op("dve", lambda e: e.reduce_max(mn[:, 0, :], p1[0:4, 0:nt].rearrange("p (b t) -> p b t", t=4), AX.X), [p1.b], [mn.b])
        S.op("dve", lambda e: e.tensor_tensor(mn[:, 1, :], p1[0:4, 128:128 + nt].rearrange("p (b t) -> p b t", t=4)[:, :, 0], m0T[:, :], ALU.add),
             [p1.b, m0T.b], [mn.b])
        S.op("dve", lambda e: e.tensor_tensor(mn[:, 2, :], mn[:, 0, :], mn[:, 1, :], ALU.max), [mn.b], [mn.b])
        S.dma("sp", sm_out[:, :], mn[:, 2, :], reads=[mn.b], writes=[sm_out.b])
        dg = sb("sdg", [4, 4, NSEQ_S], F32, ph)
        S.op("dve", lambda e: e.tensor_tensor(dg[:], ident[0:4, 0:4].unsqueeze(2).broadcast_to([4, 4, NSEQ_S]),
                                              mn[:, 2, :].unsqueeze(1).broadcast_to([4, 4, NSEQ_S]), ALU.mult), [ident.b, mn.b], [dg.b])
        dg2 = sb("sdg2", [NTS, NSEQ_S, 4], F32, ph)
        S.op("dve", lambda e: e.tensor_tensor(dg2[:], gt[:nt, 12:16].unsqueeze(1).broadcast_to([NTS, NSEQ_S, 4]),
                                              smaskT[:, :].unsqueeze(2).broadcast_to([NTS, NSEQ_S, 4]), ALU.mult), [gt.b, smaskT.b], [dg2.b])
        pr = PF()
        S.op("pe", lambda e: e.matmul(pr[:, 0:64], ones[0:4, 0:128], dg[:].rearrange("p a b -> p (a b)"), start=True, stop=True),
             [ones.b, dg.b], [pr.b], inc=False)
        S.op("pe", lambda e: e.matmul(pr[:, 64:128], pick[:, :], dg2[:].rearrange("p a b -> p (a b)"), start=True, stop=True),
             [pick.b, dg2.b], [pr.b])
        fac = sb("sfac", [128, NSEQ_S, 4], F32, ph)
        S.op("act", lambda e: e.activation(fac[:], pr[:, 64:128].rearrange("p (b h) -> p b h", h=4), AF.Identity), [pr.b], [fac.b])
        S.op("dve", lambda e: e.tensor_tensor(fac[:], fac[:], pr[:, 0:64].rearrange("p (h b) -> p b h", h=4), ALU.subtract), [pr.b, fac.b], [fac.b])
        S.op("act", lambda e: e.activation(fac[:], fac[:], AF.Exp), [fac.b], [fac.b])
        ksm = [sb("ksm%d" % i, [NTS, 4, 128], BF16, ph) for i in range(2)]
        Cout = [sb("Cout%d" % i, [128, 4, 257], F32, ph) for i in range(2)]
        for b in range(NSEQ_S):
            km = ksm[b % 2]
            co = Cout[b % 2]
            ci_ = Cin[b % 2]
            S.dma("sp", ci_[:].rearrange("p h v -> p (h v)"), sC0[b, :, :], writes=[ci_.b])
            S.op("dve", lambda e, b=b, km=km: e.tensor_scalar(km[:], ks[:nt, :, :], smaskT[:, b:b + 1], None, ALU.mult), [ks.b, smaskT.b], [km.b])
            for hp in range(2):
                pst = PF()
                for hh in range(2):
                    h = hp * 2 + hh
                    S.op("pe", lambda e, h=h, hh=hh, km=km, pst=pst: e.matmul(pst[:, hh * 256:(hh + 1) * 256], km[:, h, :], v1[:nt, h, 0:256],
                                                                             start=True, stop=True), [km.b, v1.b], [pst.b], inc=(hh == 1))
                for hh in range(2):
                    h = hp * 2 + hh
                    S.op("dve", lambda e, h=h, b=b, ci_=ci_, co=co: e.tensor_scalar(co[:, h, 0:256], ci_[:, h, 0:256], em0[:, b * 4 + h:b * 4 + h + 1],
                                                                                   fac[:, b, h:h + 1], ALU.mult, ALU.mult), [ci_.b, em0.b, fac.b], [co.b])
                    S.op("dve", lambda e, h=h, hh=hh, b=b, co=co, pst=pst: e.scalar_tensor_tensor(
                        co[:, h, 0:256], pst[:, hh * 256:(hh + 1) * 256], fac[:, b, h:h + 1], co[:, h, 0:256], ALU.mult, ALU.add),
                        [pst.b, fac.b], [co.b])
            pst = PF()
            for h in range(4):
                S.op("pe", lambda e, h=h, km=km, pst=pst: e.matmul(pst[:, h:h + 1], km[:, h, :], ones_col[:nt, 0:1], start=True, stop=True),
                     [km.b, ones_col.b], [pst.b], inc=(h == 3))
            for h in range(4):
                S.op("dve", lambda e, h=h, b=b, ci_=ci_, co=co: e.tensor_scalar(co[:, h, 256:257], ci_[:, h, 256:257], em0[:, b * 4 + h:b * 4 + h + 1],
                                                                               fac[:, b, h:h + 1], ALU.mult, ALU.mult), [ci_.b, em0.b, fac.b], [co.b])
                S.op("dve", lambda e, h=h, b=b, co=co, pst=pst: e.scalar_tensor_tensor(
                    co[:, h, 256:257], pst[:, h:h + 1], fac[:, b, h:h + 1], co[:, h, 256:257], ALU.mult, ALU.add), [pst.b, fac.b], [co.b])
            S.dma("sp", sC_out[b, :, :], co[:].rearrange("p h v -> p (h v)"), reads=[co.b], writes=[sC_out.b])

    def samp_swa(hT, c0, Wqa, Wkd, Wkv2, bkv2, bqk, snk, haT, ph):
        nt = NTS
        qaS = sb("qaS", [64, 16, NTS], BF16, ph)
        kdS = sb("kdS", [64, 4, NTS], BF16, ph)
        kvS = sb("kvS", [NTS, 512], BF16, ph)
        kvSf = sb("kvSf", [NTS, 512], F32, ph)
        ckb = sb("ckb", [64, NSEQ_S, 4, 128], BF16, ph)
        cvb = sb("cvb", [128, NSEQ_S, 256], BF16, ph)
        sbias = sb("sbias", [NTS, 16, 192], F32, ph)
        smask = sb("smask2", [128, NSEQ_S, NTS], F32, ph)
        smaskb = sb("smaskb2", [128, NSEQ_S, NTS], BF16, ph)
        S.dma("pool", ckb[:], ckT[:, :, :, :], writes=[ckb.b])
        S.dma("pool", cvb[:], cvn[:, :, :], writes=[cvb.b])
        S.dma("sp", sbias[:], c_sbias[:, :, :], writes=[sbias.b])
        S.dma("sp", smask[:], c_seqmask[:, :, :], writes=[smask.b])
        S.op("dve", lambda e: e.tensor_copy(smaskb[:], smask[:]), [smask.b], [smaskb.b])
        S.dma("sp", sk_out[:, 0:124, :], ck_nat[:, 4:128, :], writes=[sk_out.b])
        S.dma("sp", sv_out[:, 0:124, :], cv_nat[:, 4:128, :], writes=[sv_out.b])
        for h in range(16):
            pst = PF()
            for k in range(KC):
                S.op("pe", lambda e, k=k, h=h, pst=pst: e.matmul(pst[0:64, 0:nt], Wqa[:, k, h * 64:(h + 1) * 64], hT[:, k, c0:c0 + nt],
                                                                start=(k == 0), stop=(k == KC - 1)), [Wqa.b, hT.b], [pst.b], inc=(k == KC - 1))
            S.op("act", lambda e, h=h, pst=pst: e.activation(qaS[:, h, :], pst[0:64, 0:nt], AF.Identity, bias=bqk[:, h:h + 1], scale=1.0),
                 [pst.b, bqk.b], [qaS.b])
        for jj in range(4):
            pst = PF()
            for k in range(KC):
                S.op("pe", lambda e, k=k, jj=jj, pst=pst: e.matmul(pst[0:64, 0:nt], Wkd[:, k, jj * 64:(jj + 1) * 64], hT[:, k, c0:c0 + nt],
                                                                  start=(k == 0), stop=(k == KC - 1)), [Wkd.b, hT.b], [pst.b], inc=(k == KC - 1))
            S.op("act", lambda e, jj=jj, pst=pst: e.activation(kdS[:, jj, :], pst[0:64, 0:nt], AF.Identity, bias=bqk[:, 16 + jj:17 + jj], scale=1.0),
                 [pst.b, bqk.b], [kdS.b])
        pst = PF()
        mm_tok(pst, 0, 512, hT, c0, nt, Wkv2, 0, bkv2, 0)
        S.op("act", lambda e, pst=pst: e.activation(kvS[:, :], pst[:nt, :], AF.Identity), [pst.b], [kvS.b])
        S.op("dve", lambda e, pst=pst: e.tensor_copy(kvSf[:, :], pst[:nt, :]), [pst.b], [kvSf.b])
        for b in range(NSEQ_S):
            S.dma("sp", sk_out[b, 124:128, :], kvSf[4 * b:4 * b + 4, 0:256], reads=[kvSf.b], writes=[sk_out.b], acc=True)
            S.dma("sp", sv_out[b, 124:128, :], kvSf[4 * b:4 * b + 4, 256:512], reads=[kvSf.b], writes=[sv_out.b], acc=True)
        qmS = sb("qmS", [64, NSEQ_S, NTS], BF16, ph)
        Sb_ = sb("sSb", [NTS, 192], F32, ph)
        Pb_ = sb("sPb", [NTS, 192], BF16, ph)
        PTc = sb("sPTc", [128, NTS], BF16, ph)
        PTn = sb("sPTn", [NTS, NTS], BF16, ph)
        PTm = sb("sPTm", [128, NSEQ_S, NTS], BF16, ph)
        ast = sb("sast", [NTS, 96], F32, ph)
        hab = sb("shab", [NTS, D], BF16, ph)
        po = [PL(0), PL(1)]
        for h in range(16):
            hp, hh = h // 2, h % 2
            jj = h // 4
            b0 = hh * 64
            S.op("dve", lambda e, h=h: e.tensor_tensor(qmS[:], qaS[:, h, :].unsqueeze(1).broadcast_to([64, NSEQ_S, NTS]), smaskb[0:64, :, :], ALU.mult),
                 [qaS.b, smaskb.b], [qmS.b])
            pS = PF()
            for b in range(NSEQ_S):
                S.op("pe", lambda e, b=b, jj=jj, pS=pS: e.matmul(pS[:nt, 0:128], qmS[:, b, :], ckb[:, b, jj, :],
                                                                start=(b == 0), stop=(b == NSEQ_S - 1)), [qmS.b, ckb.b], [pS.b], inc=False)
            S.op("pe", lambda e, h=h, jj=jj, pS=pS: e.matmul(pS[:nt, 128:192], qaS[:, h, :], kdS[:, jj, :], start=True, stop=True),
                 [qaS.b, kdS.b], [pS.b])
            S.op("dve", lambda e, h=h, pS=pS: e.scalar_tensor_tensor(Sb_[:, :], pS[:nt, 0:192], 0.125, sbias[:, h, :], ALU.mult, ALU.add),
                 [pS.b, sbias.b], [Sb_.b])
            S.op("dve", lambda e, h=h: e.reduce_max(ast[:, h:h + 1], Sb_[:, :], AX.X), [Sb_.b], [ast.b])
            S.op("dve", lambda e, h=h: e.tensor_tensor(ast[:, 16 + h:17 + h], ast[:, h:h + 1], snk[:nt, h:h + 1], ALU.max), [ast.b, snk.b], [ast.b])
            S.op("dve", lambda e, h=h: e.tensor_scalar(ast[:, 32 + h:33 + h], ast[:, 16 + h:17 + h], -1.0, None, ALU.mult), [ast.b], [ast.b])
            S.op("dve", lambda e, h=h: e.tensor_tensor(ast[:, 64 + h:65 + h], snk[:nt, h:h + 1], ast[:, 16 + h:17 + h], ALU.subtract), [ast.b, snk.b], [ast.b])
            S.op("act", lambda e, h=h: e.activation(ast[:, 64 + h:65 + h], ast[:, 64 + h:65 + h], AF.Exp), [ast.b], [ast.b])
            S.op("act", lambda e, h=h: e.activation(Pb_[:, :], Sb_[:, :], AF.Exp, bias=ast[:, 32 + h:33 + h], scale=1.0, accum_out=ast[:, 48 + h:49 + h]),
                 [Sb_.b, ast.b], [Pb_.b, ast.b])
            ptb = PB()
            S.op("pe", lambda e, ptb=ptb: e.transpose(ptb[:, 0:nt], Pb_[:, 0:128], identb[:nt, :nt]), [Pb_.b, identb.b], [ptb.b], inc=False)
            S.op("pe", lambda e, ptb=ptb: e.transpose(ptb[0:nt, 64:64 + nt], Pb_[:, 128:192], identb[:nt, :nt]), [Pb_.b, identb.b], [ptb.b])
            S.op("act", lambda e, ptb=ptb: e.activation(PTc[:, :], ptb[:, 0:nt], AF.Identity), [ptb.b], [PTc.b])
            S.op("act", lambda e, ptb=ptb: e.activation(PTn[:, :], ptb[0:nt, 64:64 + nt], AF.Identity), [ptb.b], [PTn.b])
            S.op("dve", lambda e: e.tensor_tensor(PTm[:], PTc[:, :].unsqueeze(1).broadcast_to([128, NSEQ_S, NTS]), smaskb[:], ALU.mult),
                 [PTc.b, smaskb.b], [PTm.b])
            pod = po[h // 8]
            oc0 = (h % 8) * 64
            for b in range(NSEQ_S):
                S.op("pe", lambda e, b=b, jj=jj, pod=pod, oc0=oc0: e.matmul(pod[:nt, oc0:oc0 + 64], PTm[:, b, :], cvb[:, b, jj * 64:(jj + 1) * 64],
                                                                           start=(b == 0), stop=False), [PTm.b, cvb.b], [pod.b], inc=False)
            S.op("pe", lambda e, jj=jj, pod=pod, oc0=oc0: e.matmul(pod[:nt, oc0:oc0 + 64], PTn[:, :], kvS[:, 256 + jj * 64:256 + (jj + 1) * 64],
                                                                  start=False, stop=True), [PTn.b, kvS.b], [pod.b])
        S.op("dve", lambda e: e.tensor_tensor(ast[:, 80:96], ast[:, 48:64], ast[:, 64:80], ALU.add), [ast.b], [ast.b])
        S.op("dve", lambda e: e.reciprocal(ast[:, 80:96], ast[:, 80:96]), [ast.b], [ast.b])
        for g in range(2):
            S.op("dve", lambda e, g=g: e.tensor_tensor(
                hab[:, g * 512:(g + 1) * 512].rearrange("p (h d) -> p h d", h=8), po[g][:nt, :].rearrange("p (h d) -> p h d", h=8),
                ast[:, 80 + 8 * g:88 + 8 * g].unsqueeze(2).broadcast_to([NTS, 8, 64]), ALU.mult), [po[g].b, ast.b], [hab.b])
        ptb = PB()
        for k in range(KC):
            S.op("pe", lambda e, k=k, ptb=ptb: e.transpose(ptb[:, k * 64:(k + 1) * 64], hab[:, k * 128:(k + 1) * 128], identb[:nt, :nt]),
                 [hab.b, identb.b], [ptb.b], inc=(k == KC - 1))
        S.op("act", lambda e, ptb=ptb: e.activation(haT[:, :, c0:c0 + nt], ptb[:, 0:512].rearrange("p (k t) -> p k t", k=KC), AF.Identity),
             [ptb.b], [haT.b])

    outs = []
    try:
        if NSCAN > 0:
            with contextlib.ExitStack() as ph:
                Wkv = sb("Wkv", [128, KC, 1544], BF16, ph)
                load_w(Wkv, w_in, 512, 2056)
                bkv = bias_hilo("bkv", b_in, 512, 2056, ph)
                hTs = [sb("hTs%d" % i, [128, KC, 128], BF16, ph) for i in range(2)]
                mrow1 = sb("mrow_s", [128, 2, D], F32, ph)
                S.dma("sp", mrow1[:], modR_scr[:, 0:2, :], reads=[modR_scr.b], writes=[mrow1.b])
                xn2 = [xn, sb("xn_b", [128, D], F32, ph)]
                lnt2 = [lnt, sb("lnt_b", [128, 20], F32, ph)]
                gt2 = [gt, sb("gt_b", [128, 40], F32, ph)]
                ks2 = [ks, sb("ks_b", [128, 4, 128], BF16, ph)]
                v12 = [v1, sb("v1_b", [128, 4, 257], BF16, ph)]
                S.op("pool", lambda e: e.memset(v12[1][:], 1.0), [], [v12[1].b])

                xbs = {}

                def stA1(i):
                    xt = XT()
                    S.dma("sp", xt[:], xs[i * 128:(i + 1) * 128, :], writes=[xt.b])
                    xbs[i] = ln_rows_part1(xt, 128, xn2[i % 2], lnt2[i % 2], mrow1)

                def stA2(i):
                    xb_to_hT(xbs.pop(i), 128, hTs[i % 2], 0)
                    pg = PF()
                    mm_tok(pg, 0, 8, hTs[i % 2], 0, 128, Wkv, 1536, bkv, 1536)
                    mlstm_gates(pg, 128, i, gt2[i % 2])

                def stB(i):
                    pk, pv0, pv1 = PF(), PF(), PF()
                    mm_tok(pk, 0, 512, hTs[i % 2], 0, 128, Wkv, 0, bkv, 0)
                    mm_tok(pv0, 0, 512, hTs[i % 2], 0, 128, Wkv, 512, bkv, 512)
                    mm_tok(pv1, 0, 512, hTs[i % 2], 0, 128, Wkv, 1024, bkv, 1024)
                    kv_from_psum(pk, pv0, pv1, 128, ks2[i % 2], v12[i % 2], gt2[i % 2])

                def stC(i):
                    state_update(ks2[i % 2], v12[i % 2], gt2[i % 2], 128)
                    m_update(gt2[i % 2], 128)

                stA1(0)
                stA2(0)
                if NSCAN > 1:
                    stA1(1)
                stB(0)
                for ba in range(NSCAN):
                    if ba + 2 < NSCAN:
                        stA1(ba + 2)
                    if ba + 1 < NSCAN:
                        stA2(ba + 1)
                    stC(ba)
                    if ba + 1 < NSCAN:
                        stB(ba + 1)
            S.phase_end()

        own_blocks = [("h2", NSCAN, 128), ("h1", NSCAN + 1, 128)] + [("own", NSCAN + 2 + i, 128) for i in range(NOWN)]
        npass = 2 if NOWN >= 8 else 1
        per = (len(own_blocks) + npass - 1) // npass
        passes = [own_blocks[i * per:(i + 1) * per] for i in range(npass)]
        if cfg.sample:
            passes.append([("samp", -1, NTS)])
        NTPMAX = max(sum(b[2] for b in p) for p in passes)
        NBPMAX = max(len(p) for p in passes)

        kd_carry = sb("kd_carry", [64, 4, 128], BF16)
        kv_carry = sb("kv_carry", [128, 512], BF16)
        cv_carry = sb("cv_carry", [128, 2 * FC, 2])
        S.op("pool", lambda e: e.memset(kd_carry[:], 0.0), [], [kd_carry.b])
        S.op("pool", lambda e: e.memset(kv_carry[:], 0.0), [], [kv_carry.b])
        S.op("pool", lambda e: e.memset(cv_carry[:], 0.0), [], [cv_carry.b])
        cvv = sb("cvv", [128, 1])
        S.dma("sp", cvv[:], convvalid[:, :], writes=[cvv.b])
        x1row = {"n": 0}

        for pi, blocks in enumerate(passes):
            offs = []
            o = 0
            for b in blocks:
                offs.append(o)
                o += b[2]
            NTP = o
            NBP = len(blocks)
            prompt_passes = [i for i, p_ in enumerate(passes) if any(b_[0] != "samp" for b_ in p_)]
            blocks_last_prompt_pass = (pi == prompt_passes[-1])
            with contextlib.ExitStack() as pp:
                hT = sb("hT", [128, KC, NTP], BF16, pp)
                if any(b_[0] == "samp" for b_ in blocks):
                    SAMP["modFs"] = sb("modFs", [128, 32, NTS], F32, pp)
                    SAMP["gS"] = sb("gS", [NTS, 2, D], F32, pp)
                    S.dma("sp", SAMP["modFs"][:], modFs_scr[:, :, :], reads=[modFs_scr.b], writes=[SAMP["modFs"].b])
                    S.dma("sp", SAMP["gS"][:], gS_scr[:, :, :], reads=[gS_scr.b], writes=[SAMP["gS"].b])
                pm_ = pp.enter_context(contextlib.ExitStack())
                hmT = sb("hmT", [128, KC, NTP], BF16, pm_)
                haT = sb("haT", [128, KC, NTP], BF16, pm_)
                S.op("pool", lambda e: e.memset(hmT[:], 0.0), [], [hmT.b])
                S.op("pool", lambda e: e.memset(haT[:], 0.0), [], [haT.b])
                phB = pm_.enter_context(contextlib.ExitStack())
                Wm = sb("Wm", [128, KC, 2056], BF16, phB)
                Wog = sb("Wog", [128, KC, 1024], BF16, phB)
                load_w(Wm, w_in, 0, 2056)
                load_w(Wog, w_in, 2056, 3080)
                bm = bias_hilo("bm", b_in, 0, 2056, phB)
                with contextlib.ExitStack() as ph:
                    mrowA = sb("mrowA", [128, 2, D], F32, ph)
                    S.dma("sp", mrowA[:], modR_scr[:, 0:2, :], reads=[modR_scr.b], writes=[mrowA.b])
                    xnA = [xn, sb("xnA", [128, D], F32, ph)]
                    lntA = [lnt, sb("lntA", [128, 20], F32, ph)]
                    pend = None
                    for bi, (kind, ba, nt) in enumerate(blocks):
                        xt = XT()
                        if kind == "samp":
                            S.dma("sp", xt[:nt, :], xsamp[:, :], writes=[xt.b])
                            ln_to_hT(xt, nt, hT, offs[bi], 0, True, xn, lnt, None)
                            continue
                        S.dma("sp", xt[:], xs[ba * 128:(ba + 1) * 128, :], writes=[xt.b])
                        xb_ = ln_rows_part1(xt, nt, xnA[bi % 2], lntA[bi % 2], mrowA)
                        if pend is not None:
                            xb_to_hT(pend[0], 128, hT, pend[1])
                        pend = (xb_, offs[bi])
                    if pend is not None:
                        xb_to_hT(pend[0], 128, hT, pend[1])
                S.phase_end()

                if True:
                    ph = phB
                    sgog = sb("sgog", [128, KC, NTP], BF16, ph)
                    binT = sb("binT", [128, 32], F32, ph)
                    nwT = sb("nwT", [128, KC], F32, ph)
                    S.dma("sp", binT[:], b_inT[:, :], writes=[binT.b])
                    S.dma("sp", nwT[:], norm_wT[:, :], writes=[nwT.b])
                    qs = sb("qs", [128, 4, 128], BF16, ph)
                    qkT = sb("qkT", [128, 8, 128], BF16, ph)
                    SmT = sb("SmT", [128, 4, 128], BF16, ph)
                    hmn = sb("hmn", [128, D], F32, ph)
                    hst = sb("hst", [128, 64], F32, ph)
                    t0 = 0
                    while t0 < NTP:
                        n = min(512, NTP - t0)
                        for oc in range(KC):
                            pst = PF()
                            for k in range(KC):
                                S.op("pe", lambda e, k=k, oc=oc, t0=t0, n=n: e.matmul(
                                    pst[:, 0:n], Wog[:, k, oc * 128:(oc + 1) * 128], hT[:, k, t0:t0 + n],
                                    start=(k == 0), stop=(k == KC - 1)), [Wog.b, hT.b], [pst.b], inc=(k == KC - 1))
                            S.op("act", lambda e, oc=oc, t0=t0, n=n: e.activation(
                                sgog[:, oc, t0:t0 + n], pst[:, 0:n], AF.Sigmoid, bias=binT[:, oc:oc + 1], scale=1.0),
                                [pst.b, binT.b], [sgog.b])
                        t0 += n
                    for bi, (kind, ba, nt) in enumerate(blocks):
                        c0 = offs[bi]
                        if kind == "h2":
                            scan_block(hT, c0, ba, Wm, 512, bm)
                            continue
                        if kind == "samp":
                            samp_mlstm(hT, c0, Wm, bm, hmT, sgog, nwT, ph)
                            continue
                        pg = PF()
                        mm_tok(pg, 0, 8, hT, c0, 128, Wm, 2048, bm, 2048)
                        mlstm_gates(pg, 128, ba, gt)
                        pq = PF()
                        mm_tok(pq, 0, 512, hT, c0, 128, Wm, 0, bm, 0)
                        for h in range(4):
                            S.op("dve", lambda e, h=h, pq=pq: e.tensor_scalar(qs[:, h, :], pq[:, h * 128:(h + 1) * 128], gt[:, 16 + h:17 + h],
                                                                              128.0 ** -0.5, ALU.mult, ALU.mult), [pq.b, gt.b], [qs.b])
                        pk, pv0, pv1 = PF(), PF(), PF()
                        mm_tok(pk, 0, 512, hT, c0, 128, Wm, 512, bm, 512)
                        mm_tok(pv0, 0, 512, hT, c0, 128, Wm, 1024, bm, 1024)
                        mm_tok(pv1, 0, 512, hT, c0, 128, Wm, 1536, bm, 1536)
                        kv_from_psum(pk, pv0, pv1, 128)
                        ptb = PB()
                        for h in range(4):
                            S.op("pe", lambda e, h=h: e.transpose(ptb[:, h * 128:(h + 1) * 128], qs[:, h, :], identb[:, :]),
                                 [qs.b, identb.b], [ptb.b], inc=False)
                        for h in range(4):
                            S.op("pe", lambda e, h=h: e.transpose(ptb[:, 512 + h * 128:512 + (h + 1) * 128], ks[:, h, :], identb[:, :]),
                                 [ks.b, identb.b], [ptb.b], inc=(h == 3))
                        S.op("act", lambda e: e.activation(qkT[:].rearrange("p a b -> p (a b)"), ptb[:, :], AF.Identity), [ptb.b], [qkT.b])
                        pS = PF()
                        for h in range(4):
                            S.op("pe", lambda e, h=h: e.matmul(pS[:, h * 128:(h + 1) * 128], qkT[:, 4 + h, :], qkT[:, h, :], start=True, stop=True),
                                 [qkT.b], [pS.b], inc=(h == 3))
                        S.op("dve", lambda e: e.tensor_tensor(SmT[:], pS[:, :].rearrange("p (h t) -> p h t", h=4),
                                                              tri[:, :].unsqueeze(1).broadcast_to([128, 4, 128]), ALU.mult),
                             [pS.b, tri.b], [SmT.b])
                        pnum = [PL(0), PL(1)]
                        for h in range(4):
                            pn = pnum[h // 2]
                            cs = (h % 2) * 256
                            S.op("pe", lambda e, h=h, pn=pn, cs=cs: e.matmul(pn[:, cs:cs + 256], SmT[:, h, :], v1[:, h, 0:256], start=True, stop=False),
                                 [SmT.b, v1.b], [pn.b], inc=False)
                            S.op("pe", lambda e, h=h, pn=pn, cs=cs: e.matmul(pn[:, cs:cs + 256], qkT[:, h, :], CnTb[:, h, 0:256], start=False, stop=True),
                                 [qkT.b, CnTb.b], [pn.b], inc=True)
                        pden = PF()
                        for h in range(4):
                            S.op("pe", lambda e, h=h: e.matmul(pden[:, h:h + 1], SmT[:, h, :], ones_col[:, 0:1], start=True, stop=False),
                                 [SmT.b, ones_col.b], [pden.b], inc=False)
                            S.op("pe", lambda e, h=h: e.matmul(pden[:, h:h + 1], qkT[:, h, :], CnTb[:, h, 256:257], start=False, stop=True),
                                 [qkT.b, CnTb.b], [pden.b], inc=True)
                        head_ln(pnum, pden, hst, hmn, 128)
                        state_update(ks, v1, gt, 128)
                        m_update(gt, 128)
                        hm_transpose_out(hmn, 128, hmT, c0, nwT, sgog)
                phB.close()
                S.phase_end()
                if debug and pi == 0:
                    dbg["hmT"] = dout("dbg_hmT", [128, KC, NTP], BF16)
                    S.dma("sp", dbg["hmT"][:, :, :], hmT[:], reads=[hmT.b], writes=[dbg["hmT"].b])
                    dbg["hT"] = dout("dbg_hT", [128, KC, NTP], BF16)
                    S.dma("sp", dbg["hT"][:, :, :], hT[:], reads=[hT.b], writes=[dbg["hT"].b])
                    dbg["CnT"] = dout("dbg_CnT", [128, 4, 257])
                    S.dma("sp", dbg["CnT"][:, :, :], CnT[:], reads=[CnT.b], writes=[dbg["CnT"].b])
                if stop_after == "B":
                    pm_.close()
                    break
                with contextlib.ExitStack() as ph:
                    Wqa = sb("Wqa", [128, KC, 1024], BF16, ph)
                    Wkd = sb("Wkd", [128, KC, 256], BF16, ph)
                    Wkv2 = sb("Wkv2", [128, KC, 512], BF16, ph)
                    load_w(Wqa, w_in, 3080, 4104)
                    load_w(Wkd, w_in, 4104, 4360)
                    load_w(Wkv2, w_in, 4104, 4616)
                    bkv2 = bias_hilo("bkv2", b_in, 4104, 4616, ph)
                    binT = sb("binTc", [128, 32], F32, ph)
                    bqk = sb("bqk", [64, 20], F32, ph)
                    swab = sb("swab", [128, 16, 256], F32, ph)
                    fbias = sb("fbias", [128, 128], F32, ph)
                    snk = sb("snk", [128, 16], F32, ph)
                    S.dma("sp", binT[:], b_inT[:, :], writes=[binT.b])
                    S.dma("sp", bqk[:], b_qk64[:, :], writes=[bqk.b])
                    S.dma("sp", swab[:], c_swab[:, :, :], writes=[swab.b])
                    S.dma("sp", fbias[:], first_bias[:, :], writes=[fbias.b])
                    S.dma("sp", snk[:], sinks_rep[:, :], writes=[snk.b])
                    qaT = sb("qaT", [64, 16, 512], BF16, ph)
                    qst = {"t0": -1, "n": 0}
                    kdT = sb("kdT", [64, 4, 128 + NTP], BF16, ph)
                    kvA = sb("kvA", [128, NBP + 1, 512], BF16, ph)
                    kvf = sb("kvf", [128, 512], F32, ph)
                    Sb = [sb("Sb%d" % i, [128, 2, 256], F32, ph) for i in range(3)]
                    Pb = [sb("Pb%d" % i, [128, 256], BF16, ph) for i in range(6)]
                    astl = [sb("ast%d" % i, [128, 12], F32, ph) for i in range(3)]
                    PT2 = [sb("PT2_%d" % i, [128, 4, 128], BF16, ph) for i in range(2)]
                    ast = sb("ast", [128, 96], F32, ph)
                    hab = sb("hab", [128, D], BF16, ph)
                    S.op("dve", lambda e: e.tensor_copy(kdT[:, :, 0:128], kd_carry[:]), [kd_carry.b], [kdT.b])
                    S.op("dve", lambda e: e.tensor_copy(kvA[:, 0, :], kv_carry[:]), [kv_carry.b], [kvA.b])
                    nprompt = sum(b[2] for b in blocks if b[0] != "samp")
                    t0 = 0
                    while t0 < nprompt:
                        n = min(512, nprompt - t0)
                        for jj in range(4):
                            pst = PF()
                            for k in range(KC):
                                S.op("pe", lambda e, k=k, jj=jj, t0=t0, n=n, pst=pst: e.matmul(
                                    pst[0:64, 0:n], Wkd[:, k, jj * 64:(jj + 1) * 64], hT[:, k, t0:t0 + n],
                                    start=(k == 0), stop=(k == KC - 1)), [Wkd.b, hT.b], [pst.b], inc=(k == KC - 1))
                            S.op("act", lambda e, jj=jj, t0=t0, n=n, pst=pst: e.activation(
                                kdT[:, jj, 128 + t0:128 + t0 + n], pst[0:64, 0:n], AF.Identity, bias=bqk[:, 16 + jj:17 + jj], scale=1.0),
                                [pst.b, bqk.b], [kdT.b])
                        t0 += n
                    S.checkpoint("C1")
                    last_prompt_bi = max([bi for bi, b in enumerate(blocks) if b[0] != "samp"] + [-1])
                    for bi, (kind, ba, nt) in enumerate(blocks):
                        if kind == "samp":
                            continue
                        pst = PF()
                        mm_tok(pst, 0, 512, hT, offs[bi], 128, Wkv2, 0, bkv2, 0)
                        S.op("act", lambda e, bi=bi, pst=pst: e.activation(kvA[:, bi + 1, :], pst[:, :], AF.Identity), [pst.b], [kvA.b])
                        if blocks_last_prompt_pass and bi == last_prompt_bi:
                            S.op("dve", lambda e, pst=pst: e.tensor_copy(kvf[:], pst[:, :]), [pst.b], [kvf.b])
                            S.dma("sp", pkv_out[:, :], kvf[:], reads=[kvf.b], writes=[pkv_out.b])
                    S.checkpoint("C2a")
                    if nprompt > 0:
                        S.op("dve", lambda e: e.tensor_copy(kd_carry[:], kdT[:, :, nprompt:nprompt + 128]), [kdT.b], [kd_carry.b])
                        S.op("dve", lambda e: e.tensor_copy(kv_carry[:], kvA[:, last_prompt_bi + 1, :]), [kvA.b], [kv_carry.b])
                    S.checkpoint("C2")
                    for bi, (kind, ba, nt) in enumerate(blocks):
                        c0 = offs[bi]
                        if kind == "h2":
                            continue
                        if kind == "samp":
                            samp_swa(hT, c0, Wqa, Wkd, Wkv2, bkv2, bqk, snk, haT, ph)
                            continue
                        first_own = (kind == "own" and ba == NSCAN + 2)
                        if qst["t0"] < 0 or c0 >= qst["t0"] + qst["n"]:
                            qst["t0"] = c0
                            qst["n"] = min(512, nprompt - c0)
                            qn = qst["n"]
                            for h in range(16):
                                pst = PF()
                                for k in range(KC):
                                    S.op("pe", lambda e, k=k, h=h, pst=pst, qn=qn: e.matmul(
                                        pst[0:64, 0:qn], Wqa[:, k, h * 64:(h + 1) * 64], hT[:, k, c0:c0 + qn],
                                        start=(k == 0), stop=(k == KC - 1)), [Wqa.b, hT.b], [pst.b], inc=(k == KC - 1))
                                S.op("dve", lambda e, h=h, pst=pst, qn=qn: e.tensor_scalar(
                                    qaT[:, h, 0:qn], pst[0:64, 0:qn], bqk[:, h:h + 1], None, ALU.add),
                                    [pst.b, bqk.b], [qaT.b])
                        qo = c0 - qst["t0"]
                        po = [PL(0), PL(1)]
                        def stage1a(hp):
                            sbt = Sb[hp % 3]
                            a_ = astl[hp % 3]
                            for hh in range(2):
                                h = 2 * hp + hh
                                jj = h // 4
                                pS = PF()
                                S.op("pe", lambda e, h=h, jj=jj, pS=pS: e.matmul(
                                    pS[:, 0:256], qaT[:, h, qo:qo + 128], kdT[:, jj, c0:c0 + 256],
                                    start=True, stop=True), [qaT.b, kdT.b], [pS.b])
                                S.op("dve", lambda e, h=h, hh=hh, sbt=sbt, pS=pS: e.scalar_tensor_tensor(
                                    sbt[:, hh, :], pS[:, 0:256], 0.125, swab[:, h, :], ALU.mult, ALU.add), [pS.b, swab.b], [sbt.b])
                            if first_own:
                                S.op("dve", lambda e, sbt=sbt: e.tensor_tensor(
                                    sbt[:, :, 0:128], sbt[:, :, 0:128], fbias[:, :].unsqueeze(1).broadcast_to([128, 2, 128]), ALU.add),
                                    [sbt.b, fbias.b], [sbt.b])
                            S.op("dve", lambda e, sbt=sbt, a_=a_: e.reduce_max(a_[:, 0:2], sbt[:], AX.X), [sbt.b], [a_.b])
                            S.op("dve", lambda e, hp=hp, a_=a_: e.tensor_tensor(a_[:, 2:4], a_[:, 0:2], snk[:, 2 * hp:2 * hp + 2], ALU.max), [a_.b, snk.b], [a_.b])
                            S.op("dve", lambda e, a_=a_: e.tensor_scalar(a_[:, 4:6], a_[:, 2:4], -1.0, None, ALU.mult), [a_.b], [a_.b])
                            S.op("dve", lambda e, hp=hp, a_=a_: e.tensor_tensor(a_[:, 6:8], snk[:, 2 * hp:2 * hp + 2], a_[:, 2:4], ALU.subtract), [a_.b, snk.b], [a_.b])

                        def stage1b(hp):
                            sbt = Sb[hp % 3]
                            a_ = astl[hp % 3]
                            S.op("act", lambda e, hp=hp, a_=a_: e.activation(ast[:, 64 + 2 * hp:66 + 2 * hp], a_[:, 6:8], AF.Exp), [a_.b], [ast.b])
                            for hh in range(2):
                                h = 2 * hp + hh
                                pb_ = Pb[(hp % 3) * 2 + hh]
                                S.op("act", lambda e, h=h, hh=hh, sbt=sbt, pb_=pb_, a_=a_: e.activation(
                                    pb_[:], sbt[:, hh, :], AF.Exp, bias=a_[:, 4 + hh:5 + hh], scale=1.0, accum_out=ast[:, 48 + h:49 + h]),
                                    [sbt.b, a_.b], [pb_.b, ast.b])

                        def stage2(hp):
                            pt_ = PT2[hp % 2]
                            ptb = PB()
                            for hh in range(2):
                                pb_ = Pb[(hp % 3) * 2 + hh]
                                for half in range(2):
                                    q_ = hh * 2 + half
                                    S.op("pe", lambda e, half=half, pb_=pb_, ptb=ptb, q_=q_: e.transpose(
                                        ptb[:, q_ * 128:(q_ + 1) * 128], pb_[:, half * 128:(half + 1) * 128], identb[:, :]),
                                        [pb_.b, identb.b], [ptb.b], inc=(q_ == 3))
                            S.op("act", lambda e, pt_=pt_, ptb=ptb: e.activation(pt_[:].rearrange("p a b -> p (a b)"), ptb[:, 0:512], AF.Identity),
                                 [ptb.b], [pt_.b])
                            for hh in range(2):
                                h = 2 * hp + hh
                                jj = h // 4
                                pod = po[h // 8]
                                oc0 = (h % 8) * 64
                                for half in range(2):
                                    S.op("pe", lambda e, half=half, hh=hh, pt_=pt_, pod=pod, oc0=oc0, jj=jj: e.matmul(
                                        pod[:, oc0:oc0 + 64], pt_[:, hh * 2 + half, :], kvA[:, bi + half, 256 + jj * 64:256 + (jj + 1) * 64],
                                        start=(half == 0), stop=(half == 1)), [pt_.b, kvA.b], [pod.b], inc=(half == 1))

                        stage1a(0)
                        stage1b(0)
                        stage1a(1)
                        for hp in range(8):
                            if hp + 2 < 8:
                                stage1a(hp + 2)
                            stage2(hp)
                            if hp + 1 < 8:
                                stage1b(hp + 1)
                        S.op("dve", lambda e: e.tensor_tensor(ast[:, 80:96], ast[:, 48:64], ast[:, 64:80], ALU.add), [ast.b], [ast.b])
                        S.op("dve", lambda e: e.reciprocal(ast[:, 80:96], ast[:, 80:96]), [ast.b], [ast.b])
                        for g in range(2):
                            S.op("dve", lambda e, g=g: e.tensor_tensor(
                                hab[:, g * 512:(g + 1) * 512].rearrange("p (h d) -> p h d", h=8), po[g][:, :].rearrange("p (h d) -> p h d", h=8),
                                ast[:, 80 + 8 * g:88 + 8 * g].unsqueeze(2).broadcast_to([128, 8, 64]), ALU.mult), [po[g].b, ast.b], [hab.b])
                        ptb = PB()
                        for k in range(KC):
                            S.op("pe", lambda e, k=k, ptb=ptb: e.transpose(ptb[:, k * 128:(k + 1) * 128], hab[:, k * 128:(k + 1) * 128], identb[:, :]),
                                 [hab.b, identb.b], [ptb.b], inc=(k == KC - 1))
                        S.op("act", lambda e, ptb=ptb: e.activation(haT[:, :, c0:c0 + 128], ptb[:, :].rearrange("p (k t) -> p k t", k=KC), AF.Identity),
                             [ptb.b], [haT.b])
                    if debug and pi == 0 and stop_after == "C":
                        for nm, tt, shp, dt_ in [("qaT", qaT, [64, 16, 512], BF16), ("kdT", kdT, [64, 4, 128 + NTP], BF16),
                                                 ("kvA", kvA, [128, NBP + 1, 512], BF16), ("ast", ast, [128, 96], F32),
                                                 ("Sb0", Sb[0], [128, 2, 256], F32), ("Sb1", Sb[1], [128, 2, 256], F32), ("hab", hab, [128, D], BF16)]:
                            dbg[nm] = dout("dbg_" + nm, shp, dt_)
                            S.dma("sp", dbg[nm][:], tt[:], reads=[tt.b], writes=[dbg[nm].b])
                S.phase_end()
                if debug and pi == 0:
                    dbg["haT"] = dout("dbg_haT", [128, KC, NTP], BF16)
                    S.dma("sp", dbg["haT"][:, :, :], haT[:], reads=[haT.b], writes=[dbg["haT"].b])
                if stop_after == "C":
                    pm_.close()
                    break

                first_tok = offs[1] if blocks[0][0] == "h2" else 0
                x1rows = {}
                with contextlib.ExitStack() as ph:
                    Wgm = sb("Wgm", [128, KC, 1024], BF16, ph)
                    Wga = sb("Wga", [128, KC, 1024], BF16, ph)
                    Wbm = sb("Wbm", [128, KC, 1024], BF16, ph)
                    Wba = sb("Wba", [128, KC, 1024], BF16, ph)
                    Wout = sb("Wout", [128, KC, 1024], BF16, ph)
                    load_w(Wgm, w_in, 4616, 5640)
                    load_w(Wga, w_in, 5640, 6664)
                    load_w(Wbm, w_bm, 0, 1024)
                    load_w(Wba, w_ba, 0, 1024)
                    load_w(Wout, w_out, 0, 1024)
                    binT = sb("binTd", [128, 32], F32, ph)
                    S.dma("sp", binT[:], b_inT[:, :], writes=[binT.b])
                    l1g = sb("l1g", [128, D], F32, ph)
                    l1b = sb("l1b", [128, D], F32, ph)
                    S.dma("sp", l1g[:], ln1_g[0:1, :].broadcast_to([128, D]), writes=[l1g.b])
                    S.dma("sp", l1b[:], ln1_b[0:1, :].broadcast_to([128, D]), writes=[l1b.b])
                    mgT = sb("mgT", [128, KC, 512], BF16, ph)
                    sgm = sb("sgm", [128, 512], F32, ph)
                    sga = sb("sga", [128, 512], F32, ph)
                    t1 = sgm
                    t2 = sga
                    pre = sb("preD", [128, D], F32, ph)
                    x1tl = [sb("x1t", [128, D], F32, ph), sb("x1t2", [128, D], F32, ph)]
                    lntD = sb("lntD", [128, 20], F32, ph)
                    t0 = first_tok
                    while t0 < NTP:
                        n = min(512, NTP - t0)
                        for oc in range(KC):
                            for (Wg, Wb, src, sg, tt, bofs) in [(Wgm, Wbm, hmT, sgm, t1, 16), (Wga, Wba, haT, sga, t2, 24)]:
                                pg_ = PF()
                                for k in range(KC):
                                    S.op("pe", lambda e, k=k, Wg=Wg, pg_=pg_: e.matmul(pg_[:, 0:n], Wg[:, k, oc * 128:(oc + 1) * 128], hT[:, k, t0:t0 + n],
                                                                                       start=(k == 0), stop=(k == KC - 1)), [Wg.b, hT.b], [pg_.b], inc=(k == KC - 1))
                                S.op("act", lambda e, sg=sg, pg_=pg_, bofs=bofs: e.activation(sg[:, 0:n], pg_[:, 0:n], AF.Sigmoid,
                                                                                               bias=binT[:, bofs + oc:bofs + oc + 1], scale=1.0),
                                     [pg_.b, binT.b], [sg.b])
                                pb2 = PF()
                                for k in range(KC):
                                    S.op("pe", lambda e, k=k, Wb=Wb, src=src, pb2=pb2: e.matmul(pb2[:, 0:n], Wb[:, k, oc * 128:(oc + 1) * 128], src[:, k, t0:t0 + n],
                                                                                                start=(k == 0), stop=(k == KC - 1)), [Wb.b, src.b], [pb2.b], inc=(k == KC - 1))
                                S.op("dve", lambda e, sg=sg, tt=tt, pb2=pb2: e.tensor_tensor(tt[:, 0:n], pb2[:, 0:n], sg[:, 0:n], ALU.mult), [pb2.b, sg.b], [tt.b])
                            S.op("pool", lambda e, oc=oc: e.tensor_tensor(mgT[:, oc, 0:n], t1[:, 0:n], t2[:, 0:n], ALU.add), [t1.b, t2.b], [mgT.b])
                        def epi1(bi, kind, ba, nt, c0, x1t):
                            samp = kind == "samp"
                            xt = XT()
                            if samp:
                                S.dma("sp", xt[:nt, :], xsamp[:, :], writes=[xt.b])
                            else:
                                S.dma("sp", xt[:], xs[ba * 128:(ba + 1) * 128, :], writes=[xt.b])
                            g1 = SAMP["gS"] if samp else gP
                            for half in range(2):
                                py = PF()
                                mm_tok(py, 0, 512, mgT, c0 - t0, nt, Wout, half * 512, None, 0)
                                S.op("dve", lambda e, py=py, half=half, g1=g1: e.tensor_tensor(
                                    pre[:nt, half * 512:(half + 1) * 512], py[:nt, :], g1[:nt, 0, half * 512:(half + 1) * 512], ALU.mult),
                                    [py.b, g1.b], [pre.b])
                            S.op("dve", lambda e, xt=xt: e.scalar_tensor_tensor(pre[:nt, :], xt[:nt, :], ALPHA, pre[:nt, :], ALU.mult, ALU.add),
                                 [xt.b, pre.b], [pre.b])
                            rstd, nmr = ln_stats(pre, nt, lntD)
                            S.op("act", lambda e, rstd=rstd, nmr=nmr: e.activation(x1t[:nt, :], pre[:nt, :], AF.Identity, bias=nmr, scale=rstd),
                                 [pre.b, lntD.b], [x1t.b])
                            S.op("dve", lambda e: e.tensor_tensor(x1t[:nt, :], x1t[:nt, :], l1g[:nt, :], ALU.mult), [x1t.b, l1g.b], [x1t.b])
                            S.op("pool", lambda e: e.tensor_tensor(x1t[:nt, :], x1t[:nt, :], l1b[:nt, :], ALU.add), [x1t.b, l1b.b], [x1t.b])
                            r0 = x1row["n"]
                            x1row["n"] += nt
                            x1rows[bi] = r0
                            S.dma("sp", x1_scr[r0:r0 + nt, :], x1t[:nt, :], reads=[x1t.b], writes=[x1_scr.b])

                        tb = [(bi, kind, ba, nt, offs[bi]) for bi, (kind, ba, nt) in enumerate(blocks) if t0 <= offs[bi] < t0 + n]
                        for qi, (bi, kind, ba, nt, c0) in enumerate(tb):
                            if qi == 0:
                                epi1(bi, kind, ba, nt, c0, x1tl[0])
                            if qi + 1 < len(tb):
                                nb_ = tb[qi + 1]
                                epi1(nb_[0], nb_[1], nb_[2], nb_[3], nb_[4], x1tl[(qi + 1) % 2])
                            ln_to_hT(x1tl[qi % 2], nt, hT, c0, 2, kind == "samp", xn, lnt)
                        t0 += n
                S.phase_end()
                if debug and pi == 0:
                    dbg["h2T"] = dout("dbg_h2T", [128, KC, NTP], BF16)
                    S.dma("sp", dbg["h2T"][:, :, :], hT[:], reads=[hT.b], writes=[dbg["h2T"].b])
                pm_.close()
                S.phase_end()
                if stop_after == "D":
                    break

                nprompt = sum(b[2] for b in blocks if b[0] != "samp")
                tiles = []
                t0 = first_tok
                while t0 < nprompt:
                    n = min(512, nprompt - t0)
                    tiles.append((t0, n, False))
                    t0 += n
                if any(b[0] == "samp" for b in blocks):
                    tiles.append((nprompt, NTS, True))
                with contextlib.ExitStack() as pe_:
                    actT = sb("actT", [128, FC, NTP], BF16, pe_)
                    Wdn = sb("Wdn", [128, FC, D], BF16, pe_)
                    load_w(Wdn, w_down, 0, D, rows=DFF)
                    with contextlib.ExitStack() as ph:
                        Wupc = [sb("Wupc%d" % i, [128, KC, 256], BF16, ph) for i in range(3)]
                        bupT = sb("bupT", [128, 2 * FC], F32, ph)
                        cwT = sb("cwT", [128, 2 * FC, 3], F32, ph)
                        cbT = sb("cbT", [128, 2 * FC], F32, ph)
                        S.dma("sp", bupT[:], b_upT[:, :], writes=[bupT.b])
                        S.dma("sp", cwT[:], conv_wT[:, :, :], writes=[cwT.b])
                        S.dma("sp", cbT[:], conv_bT[:, :], writes=[cbT.b])
                        ue = [sb("ue%d" % i, [128, 520], F32, ph) for i in range(4)]
                        yc = [sb("yc%d" % i, [128, 512], F32, ph) for i in range(4)]
                        eit = {"n": 0}
                        if cfg.sample:
                            cst = sb("cst", [128, 2 * FC, 32], F32, ph)
                            S.dma("sp", cst[:], convst[:, :, :], writes=[cst.b])
                            cso = sb("cso", [128, 2 * FC, 32], F32, ph)
                        S.dma("pool", Wupc[0][:].rearrange("p k c -> p (k c)"), w_upr[0, :, :], writes=[Wupc[0].b])
                        for c in range(FC):
                            Wc = Wupc[c % 3]
                            if c + 1 < FC:
                                Wn_ = Wupc[(c + 1) % 3]
                                S.dma("pool", Wn_[:].rearrange("p k c -> p (k c)"), w_upr[c + 1, :, :], writes=[Wn_.b])
                            for (t0, n, samp) in tiles:
                                has_h1 = (not samp) and pi == 0 and t0 == first_tok
                                eit["n"] += 1
                                eb = (eit["n"] % 2) * 2
                                for part in range(2):
                                    ci = part * FC + c
                                    u_ = ue[eb + part]
                                    y_ = yc[eb + part]
                                    pu = PF()
                                    for k in range(KC):
                                        S.op("pe", lambda e, k=k, part=part, pu=pu, Wc=Wc, t0=t0, n=n: e.matmul(
                                            pu[:, 0:n], Wc[:, k, part * 128:(part + 1) * 128], hT[:, k, t0:t0 + n],
                                            start=(k == 0), stop=(k == KC - 1)), [Wc.b, hT.b], [pu.b], inc=(k == KC - 1))
                                    if not samp:
                                        S.op("pool", lambda e, u_=u_, ci=ci: e.tensor_copy(u_[:, 0:2], cv_carry[:, ci, :]), [cv_carry.b], [u_.b])
                                        S.op("act", lambda e, u_=u_, ci=ci, pu=pu, n=n: e.activation(u_[:, 2:2 + n], pu[:, 0:n], AF.Identity, bias=bupT[:, ci:ci + 1], scale=1.0),
                                             [pu.b, bupT.b], [u_.b])
                                        if has_h1:
                                            S.op("dve", lambda e, u_=u_: e.tensor_scalar(u_[:, 2:130], u_[:, 2:130], cvv[:, 0:1], None, ALU.mult), [u_.b, cvv.b], [u_.b])
                                        S.op("pool", lambda e, u_=u_, ci=ci, n=n: e.tensor_copy(cv_carry[:, ci, :], u_[:, n:n + 2]), [u_.b], [cv_carry.b])
                                        S.op("dve", lambda e, u_=u_, y_=y_, ci=ci, n=n: e.tensor_scalar(y_[:, 0:n], u_[:, 0:n], cwT[:, ci, 0:1], cbT[:, ci:ci + 1], ALU.mult, ALU.add),
                                             [u_.b, cwT.b, cbT.b], [y_.b])
                                        S.op("dve", lambda e, u_=u_, y_=y_, ci=ci, n=n: e.scalar_tensor_tensor(y_[:, 0:n], u_[:, 1:n + 1], cwT[:, ci, 1:2], y_[:, 0:n], ALU.mult, ALU.add),
                                             [u_.b, cwT.b], [y_.b])
                                        S.op("dve", lambda e, u_=u_, y_=y_, ci=ci, n=n: e.scalar_tensor_tensor(y_[:, 0:n], u_[:, 2:n + 2], cwT[:, ci, 2:3], y_[:, 0:n], ALU.mult, ALU.add),
                                             [u_.b, cwT.b], [y_.b])
                                    else:
                                        u3 = u_[:, 0:96].rearrange("p (b t) -> p b t", t=6)
                                        y3 = y_[:, 0:64].rearrange("p (b t) -> p b t", t=4)
                                        S.op("pool", lambda e, u3=u3, ci=ci: e.tensor_copy(u3[:, :, 0:2], cst[:, ci, :].rearrange("p (b t) -> p b t", t=2)), [cst.b], [u_.b])
                                        S.op("act", lambda e, u3=u3, ci=ci, pu=pu: e.activation(u3[:, :, 2:6], pu[:, 0:64].rearrange("p (b t) -> p b t", t=4), AF.Identity,
                                                                                                bias=bupT[:, ci:ci + 1], scale=1.0), [pu.b, bupT.b], [u_.b])
                                        S.op("pool", lambda e, u3=u3, ci=ci: e.tensor_copy(cso[:, ci, :].rearrange("p (b t) -> p b t", t=2), u3[:, :, 4:6]), [u_.b], [cso.b])
                                        S.op("dve", lambda e, u3=u3, y3=y3, ci=ci: e.tensor_scalar(y3, u3[:, :, 0:4], cwT[:, ci, 0:1], cbT[:, ci:ci + 1], ALU.mult, ALU.add),
                                             [u_.b, cwT.b, cbT.b], [y_.b])
                                        S.op("dve", lambda e, u3=u3, y3=y3, ci=ci: e.scalar_tensor_tensor(y3, u3[:, :, 1:5], cwT[:, ci, 1:2], y3, ALU.mult, ALU.add),
                                             [u_.b, cwT.b], [y_.b])
                                        S.op("dve", lambda e, u3=u3, y3=y3, ci=ci: e.scalar_tensor_tensor(y3, u3[:, :, 2:6], cwT[:, ci, 2:3], y3, ALU.mult, ALU.add),
                                             [u_.b, cwT.b], [y_.b])
                                ya_, yg_ = yc[eb], yc[eb + 1]
                                S.op("act", lambda e, n=n, ya_=ya_: e.activation(ya_[:, 0:n], ya_[:, 0:n], AF.Gelu_apprx_tanh), [ya_.b], [ya_.b])
                                S.op("dve", lambda e, c=c, t0=t0, n=n, ya_=ya_, yg_=yg_: e.tensor_tensor(actT[:, c, t0:t0 + n], ya_[:, 0:n], yg_[:, 0:n], ALU.mult),
                                     [ya_.b, yg_.b], [actT.b])
                        if pi == len(passes) - 1:
                            S.dma("sp", pconv_out[:, :, :], cv_carry[:], reads=[cv_carry.b], writes=[pconv_out.b])
                            if cfg.sample:
                                S.dma("sp", sconv_out[:, :, :], cso[:], reads=[cso.b], writes=[sconv_out.b])
                    S.phase_end()
                    with contextlib.ExitStack() as ph:
                        bdn = bias_hilo("bdn", b_down, 0, D, ph)
                        l2g = sb("l2g", [128, D], F32, ph)
                        l2b = sb("l2b", [128, D], F32, ph)
                        S.dma("sp", l2g[:], ln2_g[0:1, :].broadcast_to([128, D]), writes=[l2g.b])
                        S.dma("sp", l2b[:], ln2_b[0:1, :].broadcast_to([128, D]), writes=[l2b.b])
                        pre = sb("pre2", [128, D], F32, ph)
                        yt = sb("yt", [128, D], F32, ph)
                        for bi, (kind, ba, nt) in enumerate(blocks):
                            c0 = offs[bi]
                            if kind in ("h1", "h2"):
                                continue
                            samp = kind == "samp"
                            g2 = SAMP["gS"] if samp else gP
                            xt = XT()
                            r0 = x1rows[bi]
                            S.dma("sp", xt[:nt, :], x1_scr[r0:r0 + nt, :], reads=[x1_scr.b], writes=[xt.b])
                            for half in range(2):
                                pf_ = PF()
                                for k in range(FC):
                                    S.op("pe", lambda e, k=k, pf_=pf_, half=half: e.matmul(pf_[:nt, :], actT[:, k, c0:c0 + nt], Wdn[:, k, half * 512:(half + 1) * 512],
                                                                                           start=(k == 0), stop=False), [actT.b, Wdn.b], [pf_.b], inc=False)
                                S.op("pe", lambda e, pf_=pf_, half=half: e.matmul(pf_[:nt, :], onesb[0:64, 0:nt], bdn[:, half * 512:(half + 1) * 512], start=False, stop=True),
                                     [bdn.b, onesb.b], [pf_.b])
                                S.op("dve", lambda e, pf_=pf_, half=half, g2=g2: e.tensor_tensor(
                                    pre[:nt, half * 512:(half + 1) * 512], pf_[:nt, :], g2[:nt, 1, half * 512:(half + 1) * 512], ALU.mult), [pf_.b, g2.b], [pre.b])
                            S.op("dve", lambda e, xt=xt: e.scalar_tensor_tensor(pre[:nt, :], xt[:nt, :], ALPHA, pre[:nt, :], ALU.mult, ALU.add), [xt.b, pre.b], [pre.b])
                            rstd, nmr = ln_stats(pre, nt, lnt)
                            S.op("act", lambda e, rstd=rstd, nmr=nmr: e.activation(yt[:nt, :], pre[:nt, :], AF.Identity, bias=nmr, scale=rstd), [pre.b, lnt.b], [yt.b])
                            S.op("dve", lambda e: e.tensor_tensor(yt[:nt, :], yt[:nt, :], l2g[:nt, :], ALU.mult), [yt.b, l2g.b], [yt.b])
                            S.op("pool", lambda e: e.tensor_tensor(yt[:nt, :], yt[:nt, :], l2b[:nt, :], ALU.add), [yt.b, l2b.b], [yt.b])
                            if samp:
                                S.dma("sp", ys_out[:, :], yt[:nt, :], reads=[yt.b], writes=[ys_out.b])
                            else:
                                ob = ba - (NSCAN + 2)
                                S.dma("sp", y_out[ob * 128:(ob + 1) * 128, :], yt[:, :], reads=[yt.b], writes=[y_out.b])
                    S.phase_end()
            S.phase_end()

        pmd = sb("pmd", [4, 4])
        pen = sb("pen", [128, 4])
        pCs = sb("pCs", [128, 4, 257])
        S.op("dve", lambda e: e.tensor_scalar(pmd[:], ident[0:4, 0:4], mst[:, 0:1], None, ALU.mult), [ident.b, mst.b], [pmd.b])
        pst = PF()
        S.op("pe", lambda e: e.matmul(pst[:, 0:4], ones[0:4, 0:128], pmd[:, :], start=True, stop=True), [ones.b, pmd.b], [pst.b])
        S.op("act", lambda e: e.activation(pen[:], pst[:, 0:4], AF.Exp, scale=-1.0), [pst.b], [pen.b])
        for h in range(4):
            S.op("dve", lambda e, h=h: e.tensor_scalar(pCs[:, h, :], CnT[:, h, :], pen[:, h:h + 1], None, ALU.mult), [CnT.b, pen.b], [pCs.b])
        S.dma("sp", pC_out[:, :, :], pCs[:], reads=[pCs.b], writes=[pC_out.b])
        S.dma("sp", pm_out[:, :], mst[:], reads=[mst.b], writes=[pm_out.b])
        outs = [y_out, pC_out, pm_out, pkv_out, pconv_out]
        if cfg.sample:
            outs += [ys_out, sconv_out, sC_out, sm_out, sk_out, sv_out]

    except _StopBuild:
        outs = []

    return nc, S, st, locals()


def _fm(vec, nch):
    return np.ascontiguousarray(np.asarray(vec, np.float32).reshape(nch, 128).T)


def alibi_slopes():
    return np.exp2(-8.0 * np.arange(1, 17, dtype=np.float32) / 16).astype(np.float32)


def prep_shared(cfg, inp):
    sh = {}
    f = lambda a: np.ascontiguousarray(np.asarray(a, np.float32))
    b_in = f(inp["b_in"][0])
    w_in = f(inp["w_in"][0])
    sh["w_ada"] = f(inp["w_ada"][0]); sh["b_ada"] = f(inp["b_ada"]); sh["b_adaT"] = _fm(inp["b_ada"][0], 48)
    sh["w_in"] = w_in; sh["b_in"] = f(inp["b_in"])
    sh["b_inT"] = np.concatenate([_fm(b_in[2056:3080], 8), _fm(b_in[3080:4104], 8),
                                  _fm(b_in[4616:5640], 8), _fm(b_in[5640:6664], 8)], axis=1)
    sh["b_qk64"] = np.ascontiguousarray(np.concatenate([b_in[3080:4104].reshape(16, 64).T, b_in[4104:4360].reshape(4, 64).T], axis=1))
    sh["norm_wT"] = _fm(inp["mlstm_norm_w"][0], 8)
    sh["sinks_rep"] = np.ascontiguousarray(np.broadcast_to(f(inp["attn_sinks"][0])[None, :], (128, 16)))
    sh["w_bm"] = f(inp["w_branch_m"][0]); sh["w_ba"] = f(inp["w_branch_a"][0]); sh["w_out"] = f(inp["w_out"][0])
    sh["ln1_g"] = f(inp["ln1_g"]); sh["ln1_b"] = f(inp["ln1_b"]); sh["ln2_g"] = f(inp["ln2_g"]); sh["ln2_b"] = f(inp["ln2_b"])
    wu = f(inp["w_up"][0]).reshape(KC, 128, 2, FC, 128)
    sh["w_upr"] = np.ascontiguousarray(wu.transpose(3, 1, 0, 2, 4).reshape(FC, 128, KC * 256))
    sh["b_upT"] = _fm(inp["b_up"][0], 44)
    cw = f(inp["conv_w"][0])
    sh["conv_wT"] = np.ascontiguousarray(cw.reshape(3, 44, 128).transpose(2, 1, 0))
    sh["conv_bT"] = _fm(inp["conv_b"][0], 44)
    sh["w_down"] = f(inp["w_down"][0]); sh["b_down"] = f(inp["b_down"])
    sh["c_ident"] = np.eye(128, dtype=np.float32)
    sh["c_tri"] = np.triu(np.ones((128, 128), np.float32))
    t = np.arange(128)[:, None]; s_ = np.arange(256)[None, :]
    delta = (t + 128 - s_).astype(np.float32)
    ok = (delta >= 0) & (delta < 128)
    sl = alibi_slopes()
    sw = np.where(ok[:, None, :], -sl[None, :, None] * delta[:, None, :], NEG).astype(np.float32)
    sh["c_swab"] = np.ascontiguousarray(sw)
    if cfg.sample:
        tok = np.arange(NTS)
        sq = tok // 4
        ii = tok % 4
        same = (sq[:, None] == sq[None, :])
        sh["c_tri_s"] = (same & (tok[:, None] <= tok[None, :])).astype(np.float32)
        sh["c_ones_s"] = same.astype(np.float32)
        mT = (sq[:, None] == np.arange(NSEQ_S)[None, :]).astype(np.float32)
        sh["c_seqmaskT"] = np.ascontiguousarray(mT)
        sh["c_seqmask"] = np.ascontiguousarray(np.broadcast_to(mT.T[None, :, :], (128, NSEQ_S, NTS)))
        pk = np.zeros((NTS, 128), np.float32); pk[ii == 0, :] = 1.0
        sh["c_pick"] = pk
        sb_ = np.full((NTS, 16, 192), NEG, np.float32)
        s_c = np.arange(128)[None, :]
        dl = (128 + ii[:, None] - s_c).astype(np.float32)
        okc = (dl >= 0) & (dl < 128)
        sb_[:, :, 0:128] = np.where(okc[:, None, :], -sl[None, :, None] * dl[:, None, :], NEG)
        dn = (ii[:, None] - ii[None, :]).astype(np.float32)
        okn = same & (dn >= 0)
        sb_[:, :, 128:192] = np.where(okn[:, None, :], -sl[None, :, None] * dn[:, None, :], NEG)
        sh["c_sbias"] = np.ascontiguousarray(sb_)
    return sh


def prep_core(cfg, inp, c):
    seq, j = c // 4, c % 4
    SEG, NSCAN, NOWN, NB_ALL = cfg.SEG, cfg.NSCAN, cfg.NOWN, cfg.NB_ALL
    P = j * SEG
    x = np.asarray(inp["x_prompt"], np.float32)[seq]
    npre = (NSCAN + 2) * 128
    xs = np.zeros((NB_ALL * 128, D), np.float32)
    valid = np.zeros((NB_ALL * 128,), np.float32)
    if P > 0:
        xs[npre - P:npre] = x[0:P]; valid[npre - P:npre] = 1.0
    xs[npre:] = x[P:P + SEG]; valid[npre:] = 1.0
    m = {"xs": xs}
    tokc = np.zeros((128, NB_ALL, 2), np.float32)
    tokc[:, :, 0] = valid.reshape(NB_ALL, 128).T
    tokc[:, :, 1] = (tokc[:, :, 0] - 1.0) * 30000.0
    m["tokc"] = tokc
    cp = np.asarray(inp["c_prompt"], np.float32)[seq]
    cs = np.asarray(inp["c_sample"], np.float32)[c * NSEQ_S:(c + 1) * NSEQ_S]
    cs_tok = np.repeat(cs, 4, axis=0)
    ctok = np.concatenate([cp[None, :], cs_tok], axis=0)
    m["cTtok"] = np.ascontiguousarray(ctok.reshape(65, KC, 128).transpose(2, 1, 0))
    crep = np.concatenate([np.repeat(cp[None, :], 128, axis=0), cs_tok], axis=0)
    m["cTrep"] = np.ascontiguousarray(crep.reshape(192, KC, 128).transpose(2, 1, 0))
    m["xsamp"] = np.ascontiguousarray(np.asarray(inp["x_sample"], np.float32)[c * NSEQ_S:(c + 1) * NSEQ_S].reshape(NTS, D))
    cs_ = np.asarray(inp["state_ffn_conv"], np.float32)[0, c * NSEQ_S:(c + 1) * NSEQ_S]
    m["convst"] = np.ascontiguousarray(cs_.reshape(NSEQ_S, 2, 2 * FC, 128).transpose(3, 2, 0, 1).reshape(128, 2 * FC, 32))
    if cfg.sample:
        b0, b1 = c * NSEQ_S, (c + 1) * NSEQ_S
        C0 = np.asarray(inp["state_mlstm_C"], np.float32)[0, b0:b1]
        n0 = np.asarray(inp["state_mlstm_n"], np.float32)[0, b0:b1]
        m0 = np.asarray(inp["state_mlstm_m"], np.float32)[0, b0:b1]
        cn = np.concatenate([C0.transpose(0, 3, 1, 2), n0.transpose(0, 2, 1)[:, :, :, None]], axis=3)
        m["sC0"] = np.ascontiguousarray(cn.reshape(NSEQ_S, 128, 4 * 257))
        m["sm0rep"] = np.ascontiguousarray(np.broadcast_to(m0.reshape(1, 64), (128, 64)))
        m["sm0T"] = np.ascontiguousarray(m0.T)
        ck = np.asarray(inp["cache_k_win"], np.float32)[0, b0:b1]
        cv = np.asarray(inp["cache_v_win"], np.float32)[0, b0:b1]
        ckt = ck.transpose(3, 0, 2, 1)
        m["ckT"] = np.ascontiguousarray(ckt)
        m["cvn"] = np.ascontiguousarray(cv.reshape(NSEQ_S, 128, 256).transpose(1, 0, 2))
        m["ck_nat"] = np.ascontiguousarray(ck.reshape(NSEQ_S, 128, 256))
        m["cv_nat"] = np.ascontiguousarray(cv.reshape(NSEQ_S, 128, 256))
    m["first_bias"] = np.full((128, 128), NEG if j == 0 else 0.0, np.float32)
    m["convvalid"] = np.full((128, 1), 0.0 if j == 0 else 1.0, np.float32)
    return m


_CACHE = {}


def _get_program(seq):
    if seq not in _CACHE:
        cfg = Cfg(seq, sample=True)
        nc, S, st, L = build(cfg, debug=False)
        S.finish([v.b for v in L["outs"]])
        st.close()
        _CACHE[seq] = (cfg, nc)
    return _CACHE[seq]


def kernel(**inputs):
    inp = {k: np.asarray(v) for k, v in inputs.items()}
    seq = inp["x_prompt"].shape[1]
    cfg, nc = _get_program(seq)
    sh = prep_shared(cfg, inp)
    maps = []
    for c in range(8):
        m = dict(sh)
        m.update(prep_core(cfg, inp, c))
        maps.append(m)
    res = run_bass_kernel_spmd(nc, maps, core_ids=list(range(8))).results
    SEG = cfg.SEG
    f32 = np.float32
    yp = np.zeros((2, seq, D), f32)
    ys = np.zeros((128, 4, D), f32)
    pC = np.zeros((1, 2, 4, 256, 128), f32); pn = np.zeros((1, 2, 4, 128), f32); pm = np.zeros((1, 2, 4), f32)
    pk = np.zeros((1, 2, 128, 4, 64), f32); pv = np.zeros((1, 2, 128, 4, 64), f32); pcv = np.zeros((1, 2, 2, 2 * DFF), f32)
    sC = np.zeros((1, 128, 4, 256, 128), f32); sn = np.zeros((1, 128, 4, 128), f32); sm = np.zeros((1, 128, 4), f32)
    sk = np.zeros((1, 128, 128, 4, 64), f32); sv = np.zeros((1, 128, 128, 4, 64), f32); scv = np.zeros((1, 128, 2, 2 * DFF), f32)
    for c in range(8):
        r = res[c]
        s_, j = c // 4, c % 4
        yp[s_, j * SEG:(j + 1) * SEG] = np.asarray(r["y_out"], f32)
        b0, b1 = c * NSEQ_S, (c + 1) * NSEQ_S
        ys[b0:b1] = np.asarray(r["ys_out"], f32).reshape(NSEQ_S, 4, D)
        if j == 3:
            pc = np.asarray(r["pC_out"], f32)
            pC[0, s_] = pc[:, :, :256].transpose(1, 2, 0)
            pn[0, s_] = pc[:, :, 256].T
            pm[0, s_] = np.asarray(r["pm_out"], f32)[:, 0]
            kv = np.asarray(r["pkv_out"], f32)
            pk[0, s_] = kv[:, :256].reshape(128, 4, 64)
            pv[0, s_] = kv[:, 256:].reshape(128, 4, 64)
            pcv[0, s_] = np.asarray(r["pconv_out"], f32).transpose(2, 1, 0).reshape(2, 2 * DFF)
        sc = np.asarray(r["sC_out"], f32).reshape(NSEQ_S, 128, 4, 257)
        sC[0, b0:b1] = sc[:, :, :, :256].transpose(0, 2, 3, 1)
        sn[0, b0:b1] = sc[:, :, :, 256].transpose(0, 2, 1)
        sm[0, b0:b1] = np.asarray(r["sm_out"], f32).T
        sk[0, b0:b1] = np.asarray(r["sk_out"], f32).reshape(NSEQ_S, 128, 4, 64)
        sv[0, b0:b1] = np.asarray(r["sv_out"], f32).reshape(NSEQ_S, 128, 4, 64)
        scv[0, b0:b1] = np.asarray(r["sconv_out"], f32).reshape(128, 2 * FC, NSEQ_S, 2).transpose(2, 3, 1, 0).reshape(NSEQ_S, 2, 2 * DFF)
    return (yp, ys, pC, pn, pm, pk, pv, pcv, sC, sn, sm, sk, sv, scv)
```
